# Optimizing a Trainium2 kernel written in Bass

```python
import math
import jax, jax.numpy as jnp
from jax import lax
import numpy as np

D_MODEL = 4096
BATCH = 4
SEQ = 2048
DEPTH = 2

HEAD_DIM = 128
N_ATTN_HEADS = 16
ATTN_WIDTH = N_ATTN_HEADS * HEAD_DIM
CONV_WIDTH = D_MODEL - ATTN_WIDTH
CONV_GROUPS = CONV_WIDTH // HEAD_DIM
CONV_KERNEL = 31
IN_PROJ_WIDTH = 3 * ATTN_WIDTH + 2 * CONV_WIDTH
D_FF = ((8 * D_MODEL // 3 + 255) // 256) * 256
DILATED_PATTERNS = ((128, 1), (512, 4), (2048, 16))
BAND_BLOCK = 128
REL_BUCKETS = 32
REL_MAX_DIST = 2048
RMS_EPS = 1e-6
LN_EPS = 1e-5
FFN_RES_SCALE = 0.5

kernel_name = "hymba_conformer_longnet_macaron"


def rmsnorm(x, g):
    xf = x.astype(jnp.float32)
    y = xf * lax.rsqrt(jnp.mean(jnp.square(xf), axis=-1, keepdims=True) + RMS_EPS)
    return (y * g.astype(jnp.float32)).astype(x.dtype)


def layernorm(x, g, b):
    xf = x.astype(jnp.float32)
    mu = jnp.mean(xf, axis=-1, keepdims=True)
    var = jnp.mean(jnp.square(xf - mu), axis=-1, keepdims=True)
    y = (xf - mu) * lax.rsqrt(var + LN_EPS)
    return (y * g.astype(jnp.float32) + b.astype(jnp.float32)).astype(x.dtype)


def swiglu(x, w1, w3, w2):
    return (jax.nn.silu(x @ w1) * (x @ w3)) @ w2


def rel_bucket(dist):
    max_exact = REL_BUCKETS // 2
    df = np.maximum(dist, 1).astype(np.float32)
    large = max_exact + (np.log(df / max_exact) / np.log(REL_MAX_DIST / max_exact)
                         * (REL_BUCKETS - max_exact)).astype(np.int32)
    large = np.minimum(large, REL_BUCKETS - 1)
    return np.where(dist < max_exact, dist, large).astype(np.int32)


def dilated_branch(q, k, v, rel_table, window, dilation):
    B, H, S, hd = q.shape
    d = dilation
    span = window // d
    assert span <= BAND_BLOCK
    L = S // d
    nb = -(-L // BAND_BLOCK)
    Lp = nb * BAND_BLOCK

    def to_sub(t):
        t = t.reshape(B, H, L, d, hd).transpose(0, 1, 3, 2, 4)
        t = jnp.pad(t, ((0, 0), (0, 0), (0, 0), (0, Lp - L), (0, 0)))
        return t.reshape(B, H, d, nb, BAND_BLOCK, hd)

    def with_prev(t):
        prev = jnp.pad(t[:, :, :, :-1], ((0, 0), (0, 0), (0, 0), (1, 0), (0, 0), (0, 0)))
        return jnp.concatenate([prev, t], axis=4)

    qb, kb, vb = to_sub(q), to_sub(k), to_sub(v)
    kc, vc = with_prev(kb), with_prev(vb)
    s = jnp.einsum('bhrnqc,bhrnkc->bhrnqk', qb, kc).astype(jnp.float32)

    qi = np.arange(BAND_BLOCK)[:, None]
    kj = np.arange(2 * BAND_BLOCK)[None, :]
    step = qi + BAND_BLOCK - kj
    band = (step >= 0) & (step <= span)
    valid = np.concatenate([(band & (kj >= BAND_BLOCK))[None],
                            np.broadcast_to(band, (nb - 1, BAND_BLOCK, 2 * BAND_BLOCK))], axis=0)
    bucket = rel_bucket(np.clip(step, 0, None) * d)
    bias = jnp.transpose(rel_table[bucket], (2, 0, 1)).astype(jnp.float32)
    s = s + bias[None, :, None, None]
    s = jnp.where(valid[None, None, None], s, -1e30)

    m = jnp.max(s, axis=-1)
    p = jnp.exp(s - m[..., None])
    l = jnp.sum(p, axis=-1)
    o = jnp.einsum('bhrnqk,bhrnkc->bhrnqc', p.astype(vc.dtype), vc).astype(jnp.float32)

    def from_sub(t):
        tail = t.shape[5:]
        t = t.reshape(B, H, d, Lp, *tail)[:, :, :, :L]
        t = jnp.moveaxis(t, 2, 3)
        return t.reshape(B, H, S, *tail)

    return from_sub(o), from_sub(m), from_sub(l)


def dilated_attention(q, k, v, rel_table):
    outs = [dilated_branch(q, k, v, rel_table, w, d) for (w, d) in DILATED_PATTERNS]
    m_all = jnp.stack([o[1] for o in outs], axis=0)
    m_max = jnp.max(m_all, axis=0)
    num = 0.0
    den = 0.0
    for (o, m, l) in outs:
        scale = jnp.exp(m - m_max)
        num = num + scale[..., None] * o
        den = den + scale * l
    return num / den[..., None]


def causal_depthwise_conv(u, w, b):
    C = u.shape[-1]
    y = lax.conv_general_dilated(u, w[:, None, :].astype(u.dtype), window_strides=(1,),
                                 padding=((CONV_KERNEL - 1, 0),),
                                 dimension_numbers=('NWC', 'WIO', 'NWC'),
                                 feature_group_count=C)
    return y + b.astype(u.dtype)


def hybrid_mixer(h, w_in, dw_kernel, dw_bias, conv_ln_g, conv_ln_b,
                 attn_out_g, conv_out_g, w_out, rel_table):
    B, S, _ = h.shape
    proj = h @ w_in
    q, k, v, cv, cg = jnp.split(proj, [ATTN_WIDTH, 2 * ATTN_WIDTH, 3 * ATTN_WIDTH,
                                       3 * ATTN_WIDTH + CONV_WIDTH], axis=-1)

    def heads(t):
        return t.reshape(B, S, N_ATTN_HEADS, HEAD_DIM).transpose(0, 2, 1, 3)

    q = heads(q) * (HEAD_DIM ** -0.5)
    attn = dilated_attention(q, heads(k), heads(v), rel_table)
    attn = attn.transpose(0, 2, 1, 3).reshape(B, S, ATTN_WIDTH).astype(h.dtype)

    u = cv * jax.nn.sigmoid(cg)
    u = causal_depthwise_conv(u, dw_kernel, dw_bias)
    u = jax.nn.silu(layernorm(u, conv_ln_g, conv_ln_b))

    y = jnp.concatenate([rmsnorm(attn, attn_out_g), rmsnorm(u, conv_out_g)], axis=-1)
    return y @ w_out


def setup_inputs(seed: int = 0) -> dict:
    key = jax.random.key(seed)
    ks = jax.random.split(key, 24)
    f32 = jnp.float32
    L, D, F, C, A = DEPTH, D_MODEL, D_FF, CONV_WIDTH, ATTN_WIDTH

    def nrm(k, shape, scale):
        return jax.random.normal(k, shape, f32) * scale

    def gain(k, shape):
        return 1.0 + 0.05 * jax.random.normal(k, shape, f32)

    return {
        "x": jax.random.normal(ks[0], (BATCH, SEQ, D), f32),
        "ffn1_norm_g": gain(ks[1], (L, D)),
        "ffn1_w1": nrm(ks[2], (L, D, F), D ** -0.5),
        "ffn1_w3": nrm(ks[3], (L, D, F), D ** -0.5),
        "ffn1_w2": nrm(ks[4], (L, F, D), F ** -0.5),
        "mix_norm_g": gain(ks[5], (L, D)),
        "w_in": nrm(ks[6], (L, D, IN_PROJ_WIDTH), D ** -0.5),
        "dw_kernel": nrm(ks[7], (L, CONV_KERNEL, C), CONV_KERNEL ** -0.5),
        "dw_bias": nrm(ks[8], (L, C), 0.02),
        "conv_ln_g": gain(ks[9], (L, C)),
        "conv_ln_b": nrm(ks[10], (L, C), 0.02),
        "attn_out_g": gain(ks[11], (L, A)),
        "conv_out_g": gain(ks[12], (L, C)),
        "w_out": nrm(ks[13], (L, D, D), D ** -0.5),
        "ffn2_norm_g": gain(ks[14], (L, D)),
        "ffn2_w1": nrm(ks[15], (L, D, F), D ** -0.5),
        "ffn2_w3": nrm(ks[16], (L, D, F), D ** -0.5),
        "ffn2_w2": nrm(ks[17], (L, F, D), F ** -0.5),
        "rel_bias_table": nrm(ks[18], (REL_BUCKETS, N_ATTN_HEADS), 0.5),
        "final_norm_g": gain(ks[19], (D,)),
    }


def reference(x, ffn1_norm_g, ffn1_w1, ffn1_w3, ffn1_w2, mix_norm_g, w_in, dw_kernel, dw_bias,
              conv_ln_g, conv_ln_b, attn_out_g, conv_out_g, w_out, ffn2_norm_g, ffn2_w1,
              ffn2_w3, ffn2_w2, rel_bias_table, final_norm_g):
    for l in range(DEPTH):
        x = x + FFN_RES_SCALE * swiglu(rmsnorm(x, ffn1_norm_g[l]), ffn1_w1[l], ffn1_w3[l], ffn1_w2[l])
        x = x + hybrid_mixer(rmsnorm(x, mix_norm_g[l]), w_in[l], dw_kernel[l], dw_bias[l],
                             conv_ln_g[l], conv_ln_b[l], attn_out_g[l], conv_out_g[l], w_out[l],
                             rel_bias_table)
        x = x + FFN_RES_SCALE * swiglu(rmsnorm(x, ffn2_norm_g[l]), ffn2_w1[l], ffn2_w3[l], ffn2_w2[l])
    return rmsnorm(x, final_norm_g)
```

```python
from contextlib import ExitStack
from concourse.bass_utils import run_bass_kernel_spmd

import numpy as np
import concourse.bass as bass
import concourse.mybir as mybir

F32 = mybir.dt.float32
BF16 = mybir.dt.bfloat16
AF = mybir.ActivationFunctionType
ALU = mybir.AluOpType
AX = mybir.AxisListType


class Buf:
    __slots__ = ("name", "writer", "readers", "dreaders")

    def __init__(self, name):
        self.name = name
        self.writer = None
        self.readers = {}
        self.dreaders = []


class Node:
    __slots__ = ("eng", "idx", "sem", "val", "is_dma")

    def __init__(self, eng, idx):
        self.eng = eng
        self.idx = idx
        self.sem = None
        self.val = None
        self.is_dma = False


class Eng:
    def __init__(self, P, name, handle):
        self.P = P
        self.name = name
        self.h = handle
        self.sem = P.newsem("p_" + name)
        self.count = 0
        self.n = 0
        self.waited = {}
        self.nodes = []
        self.pending = []
        self.dring = None
        self.dcount = 0

    def wait(self, sem, val):
        key = id(sem)
        if self.waited.get(key, -1) >= val:
            return
        self.waited[key] = val
        self.h.wait_ge(sem, val)
        self.P.nwaits += 1


class Prog:
    def __init__(self, nc, stack):
        self.nc = nc
        self.stack = stack
        self.nsem = 0
        self.nalloc = 0
        self.stacks = []
        self.ccsems = []
        self.nwaits = 0
        self.ninstr = 0
        self.pe = Eng(self, "pe", nc.tensor)
        self.act = Eng(self, "act", nc.scalar)
        self.dve = Eng(self, "dve", nc.vector)
        self.pool = Eng(self, "pool", nc.gpsimd)
        self.sp = Eng(self, "sp", nc.sync)
        self.engs = [self.pe, self.act, self.dve, self.pool, self.sp]
        for e in (self.sp, self.pool, self.act):
            e.dring = [[self.newsem("d_%s%d" % (e.name, i)), 0] for i in range(12)]

    def newsem(self, name):
        self.nsem += 1
        return self.stack.enter_context(self.nc.semaphore(name))

    def sbuf(self, name, shape, dtype):
        self.nalloc += 1
        st = self.stacks[-1] if self.stacks else self.stack
        return st.enter_context(self.nc.sbuf_tensor("%s_%d" % (name, self.nalloc), list(shape), dtype))

    def push(self):
        from contextlib import ExitStack
        self.stacks.append(ExitStack())

    def pop(self):
        self.barrier()
        self.stacks.pop().close()

    def barrier(self):
        for e in self.engs:
            if e.pending:
                raise RuntimeError("barrier with pending non-signalled nodes on %s" % e.name)
        for e in self.engs:
            for e2 in self.engs:
                if e2 is not e and e2.count > 0:
                    e.wait(e2.sem, e2.count)
                if e2.dring:
                    for slot in e2.dring:
                        if slot[1] > 0:
                            e.wait(slot[0], slot[1])
            for sem in self.ccsems:
                e.wait(sem, 1)

    def psum(self, name, shape, dtype):
        return self.stack.enter_context(self.nc.psum_tensor(name, list(shape), dtype))

    def _dep(self, eng, node, raw):
        if node is None:
            return
        if node.is_dma:
            eng.wait(node.sem, node.val)
            return
        if node.eng is eng:
            if eng is self.pe:
                return
            if not raw:
                return
            if node.idx < eng.n - 2:
                return
        if node.val is None:
            raise RuntimeError("dependency on non-signalled node of %s" % node.eng.name)
        eng.wait(node.eng.sem, node.val)

    def op(self, eng, fn, reads=(), writes=(), signal=True, dma=False, cc=False):
        for b in reads:
            self._dep(eng, b.writer, True)
        for b in writes:
            self._dep(eng, b.writer, False)
            for r in b.readers.values():
                self._dep(eng, r, False)
            for r in b.dreaders:
                self._dep(eng, r, False)
        node = Node(eng, eng.n)
        if cc:
            node.is_dma = True
            sem = self.newsem("cc%d" % len(self.ccsems))
            self.ccsems.append(sem)
            ins = fn()
            ins.then_inc(sem, 1)
            node.sem, node.val = sem, 1
            dma = True
        elif dma:
            node.is_dma = True
            slot = eng.dring[eng.dcount % len(eng.dring)]
            eng.dcount += 1
            if slot[1] > 0:
                eng.wait(slot[0], slot[1])
            ins = fn()
            slot[1] += 16
            ins.then_inc(slot[0], 16)
            node.sem, node.val = slot[0], slot[1]
        else:
            ins = fn()
            if signal:
                eng.count += 1
                ins.then_inc(eng.sem, 1)
                node.val = eng.count
                for pn in eng.pending:
                    pn.val = eng.count
                eng.pending = []
            else:
                eng.pending.append(node)
        eng.n += 1
        self.ninstr += 1
        for b in reads:
            if dma:
                b.dreaders.append(node)
            else:
                b.readers[eng.name] = node
        for b in writes:
            b.writer = node
            b.readers = {}
            b.dreaders = []
        return node

    def finish(self, bufs):
        for b in bufs:
            self._dep(self.sp, b.writer, True)


D = 4096
DC = D // 128
TT = 512
RMS_EPS = 1e-6


class Ring:
    def __init__(self, P, name, nslots, shape):
        self.P = P
        self.nslots = nslots
        self.tiles = [P.sbuf("%s%d" % (name, i), shape, BF16) for i in range(nslots)]
        self.bufs = [Buf("%s%d" % (name, i)) for i in range(nslots)]
        self.units = []
        self.issued = 0
        self.consumed = 0

    def plan(self, parts):
        self.units.append(parts)

    def _issue(self):
        if self.issued >= len(self.units):
            return
        k = self.issued
        s = k % self.nslots
        tile = self.tiles[s]
        for (dst_fn, src) in self.units[k]:
            self.P.op(self.P.pool, lambda d=dst_fn(tile), s_=src: self.P.nc.gpsimd.dma_start(out=d, in_=s_),
                      writes=[self.bufs[s]], dma=True)
        self.issued += 1

    def start(self):
        while self.issued < min(self.nslots, len(self.units)):
            self._issue()

    def acquire(self):
        k = self.consumed
        assert k < self.issued, "ring unit not issued"
        s = k % self.nslots
        return self.tiles[s], self.bufs[s]

    def release(self):
        self.consumed += 1
        self._issue()


class Ctx:
    def __init__(self, P):
        nc = P.nc
        self.P = P
        self.ringA = Ring(P, "ringA", 2, [128, DC, 512])
        self.ringB = Ring(P, "ringB", 2, [128, 2, D])
        self.ps = [P.psum("ps%d" % i, [128, 512], F32) for i in range(8)]
        self.psb = [Buf("ps%d" % i) for i in range(8)]
        self.ones = P.sbuf("ones", [128, 128], F32)
        self.onesb = Buf("ones")
        P.op(P.dve, lambda: nc.vector.memset(self.ones[:], 1.0), writes=[self.onesb])

    def alloc_dense(self):
        P = self.P
        self.xacc = P.sbuf("xacc", [128, DC, TT], F32)
        self.xaccb = [Buf("xacc%d" % i) for i in range(DC)]
        self.hT = P.sbuf("hT", [128, DC, TT], BF16)
        self.hTb = [Buf("hT%d" % i) for i in range(DC)]
        self.gbuf = [P.sbuf("g%d" % i, [128, 2, TT], BF16) for i in range(2)]
        self.gb = [[Buf("g%d_%d" % (i, j)) for j in range(2)] for i in range(2)]
        self.sig = [P.sbuf("sig%d" % i, [128, TT], F32) for i in range(2)]
        self.sigb = [Buf("sig%d" % i) for i in range(2)]
        self.sq = self.sig
        self.sqb = self.sigb
        self.rstd = P.sbuf("rstd", [128, TT], F32)
        self.rstdb = Buf("rstd")


def plan_ffn(ctx, w1, w3, w2, F, ntiles=2):
    NG = F // 256
    w1v = w1.rearrange("(c p) f -> p c f", p=128)
    w3v = w3.rearrange("(c p) f -> p c f", p=128)
    w2v = w2.rearrange("(c p) d -> p c d", p=128)
    for t in range(ntiles):
        for g in range(NG):
            ctx.ringA.plan([
                (lambda tl: tl[:, :, 0:256], w1v[:, :, g * 256:(g + 1) * 256]),
                (lambda tl: tl[:, :, 256:512], w3v[:, :, g * 256:(g + 1) * 256]),
            ])
            ctx.ringB.plan([
                (lambda tl: tl[:, :, :], w2v[:, 2 * g:2 * g + 2, :]),
            ])


def rmsnorm_tile(ctx, gcol, gcolb):
    P = ctx.P
    nc = P.nc
    pst, pstb = ctx.ps[4], ctx.psb[4]
    for dc in range(DC):
        s = dc % 2
        P.op(P.act, lambda dc=dc, s=s: nc.scalar.activation(out=ctx.sq[s][:], in_=ctx.xacc[:, dc, :], func=AF.Square),
             reads=[ctx.xaccb[dc]], writes=[ctx.sqb[s]])
        P.op(P.pe, lambda dc=dc, s=s: nc.tensor.matmul(pst[:], lhsT=ctx.ones[:], rhs=ctx.sq[s][:], start=(dc == 0), stop=(dc == DC - 1)),
             reads=[ctx.sqb[s], ctx.onesb], writes=[pstb], signal=True)
    P.op(P.act, lambda: nc.scalar.activation(out=ctx.rstd[:], in_=pst[:], func=AF.Sqrt, scale=1.0 / D, bias=RMS_EPS),
         reads=[pstb], writes=[ctx.rstdb])
    P.op(P.dve, lambda: nc.vector.reciprocal(out=ctx.rstd[:], in_=ctx.rstd[:]), reads=[ctx.rstdb], writes=[ctx.rstdb])
    for dc in range(DC):
        P.op(P.dve, lambda dc=dc: nc.vector.scalar_tensor_tensor(
            out=ctx.hT[:, dc, :], in0=ctx.xacc[:, dc, :], scalar=gcol[:, dc:dc + 1], in1=ctx.rstd[:],
            op0=ALU.mult, op1=ALU.mult),
            reads=[ctx.xaccb[dc], ctx.rstdb, gcolb], writes=[ctx.hTb[dc]])


def load_x(ctx, xsrc_v, xsrcb, tt):
    P = ctx.P
    nc = P.nc
    for q in range(4):
        P.op(P.sp, lambda q=q: nc.sync.dma_start(out=ctx.xacc[:, q * 8:(q + 1) * 8, :], in_=xsrc_v[:, q * 8:(q + 1) * 8, tt * TT:(tt + 1) * TT]),
             reads=[xsrcb[tt]], writes=ctx.xaccb[q * 8:(q + 1) * 8], dma=True)


def store_x(ctx, xdst_v, xdstb, tt):
    P = ctx.P
    nc = P.nc
    for q in range(4):
        P.op(P.sp, lambda q=q: nc.sync.dma_start(out=xdst_v[:, q * 8:(q + 1) * 8, tt * TT:(tt + 1) * TT], in_=ctx.xacc[:, q * 8:(q + 1) * 8, :]),
             reads=ctx.xaccb[q * 8:(q + 1) * 8], writes=[xdstb[tt]], dma=True)


def ffn_tile(ctx, F, gcol, gcolb):
    P = ctx.P
    nc = P.nc
    NG = F // 256
    rmsnorm_tile(ctx, gcol, gcolb)
    A = [None, None]
    for g in range(NG + 1):
        if g < NG:
            wa, wab = ctx.ringA.acquire()
        if g >= 1:
            wb, wbb = ctx.ringB.acquire()
        for i in range(32):
            if g < NG:
                for j in range(4):
                    m = i * 4 + j
                    fc, r = divmod(m, 64)
                    w, dc = divmod(r, 32)
                    bank = fc * 2 + w
                    col = w * 256 + fc * 128
                    P.op(P.pe, lambda bank=bank, col=col, dc=dc: nc.tensor.matmul(
                        ctx.ps[bank][:], lhsT=wa[:, dc, col:col + 128], rhs=ctx.hT[:, dc, :],
                        start=(dc == 0), stop=(dc == DC - 1)),
                        reads=[wab, ctx.hTb[dc]], writes=[ctx.psb[bank]], signal=(dc == DC - 1))
                    if r == 63:
                        s = fc
                        P.op(P.act, lambda fc=fc, s=s: nc.scalar.activation(out=ctx.sig[s][:], in_=ctx.ps[fc * 2][:], func=AF.Silu),
                             reads=[ctx.psb[fc * 2]], writes=[ctx.sigb[s]])
                        P.op(P.dve, lambda fc=fc, s=s, g=g: nc.vector.tensor_tensor(
                            out=ctx.gbuf[g % 2][:, fc, :], in0=ctx.ps[fc * 2 + 1][:], in1=ctx.sig[s][:], op=ALU.mult),
                            reads=[ctx.psb[fc * 2 + 1], ctx.sigb[s]], writes=[ctx.gb[g % 2][fc]])
            if g >= 1:
                gp = (g - 1) % 2
                bank = 4 + (i % 4)
                for fc in range(2):
                    P.op(P.pe, lambda bank=bank, fc=fc, i=i, gp=gp: nc.tensor.matmul(
                        ctx.ps[bank][:], lhsT=wb[:, fc, i * 128:(i + 1) * 128], rhs=ctx.gbuf[gp][:, fc, :],
                        start=(fc == 0), stop=(fc == 1)),
                        reads=[wbb, ctx.gb[gp][fc]], writes=[ctx.psb[bank]], signal=(fc == 1))
                P.op(P.dve, lambda bank=bank, i=i: nc.vector.scalar_tensor_tensor(
                    out=ctx.xacc[:, i, :], in0=ctx.ps[bank][:], scalar=0.5, in1=ctx.xacc[:, i, :],
                    op0=ALU.mult, op1=ALU.add),
                    reads=[ctx.psb[bank], ctx.xaccb[i]], writes=[ctx.xaccb[i]])
        if g < NG:
            ctx.ringA.release()
        if g >= 1:
            ctx.ringB.release()


HD = 128
NH = 16
AW = 2048
CW = 2048
CK = 31
TOK = 1024
QSCALE = HD ** -0.5
LN_EPS = 1e-5
NEG = -30000.0
BIASW = 768 + 512 + 128


def rel_bucket_np(dist):
    max_exact = 16
    df = np.maximum(dist, 1).astype(np.float32)
    large = max_exact + (np.log(df / max_exact) / np.log(2048 / max_exact) * (32 - max_exact)).astype(np.int32)
    large = np.minimum(large, 31)
    return np.where(dist < max_exact, dist, large).astype(np.int32)


def build_bias(rel_table, half):
    out = np.full((NH, 128, BIASW), NEG, np.float32)
    j = np.arange(128)[:, None]
    i = np.arange(128)[None, :]
    for pi, d in enumerate((1, 4)):
        step_c = i - j
        val_c = step_c >= 0
        b_c = rel_bucket_np(np.clip(step_c, 0, None) * d)
        step_p = i + 128 - j
        val_p = step_p <= 128
        b_p = rel_bucket_np(np.clip(step_p, 0, None) * d)
        base = 0 if d == 1 else 768
        for h in range(NH):
            cur = np.where(val_c, rel_table[b_c, h], NEG).astype(np.float32)
            prv = np.where(val_p, rel_table[b_p, h], NEG).astype(np.float32)
            if half == 1:
                out[h, :, base:base + 128] = prv
            out[h, :, base + 128:base + 256] = cur
            out[h, :, base + 256:base + 384] = prv
            out[h, :, base + 384:base + 512] = cur
            if d == 1:
                out[h, :, base + 512:base + 640] = prv
                out[h, :, base + 640:base + 768] = cur
    i2 = np.arange(64)[None, :]
    step = 64 + i2 - j
    val = step >= 0
    if half == 0:
        val = val & (j >= 64)
    b16 = rel_bucket_np(np.clip(step, 0, None) * 16)
    for h in range(NH):
        t16 = np.where(val, rel_table[b16, h], NEG).astype(np.float32)
        out[h, :, 1280:1344] = t16
        out[h, :, 1344:1408] = t16
    return out


def plan_inproj(ctx, w_in):
    wv = w_in.rearrange("(c p) f -> p c f", p=128)
    for u in range(12):
        ctx.ringA.plan([(lambda tl: tl[:, :, :], wv[:, :, u * 512:(u + 1) * 512])])
    for c2 in range(8):
        ctx.ringA.plan([
            (lambda tl: tl[:, :, 0:256], wv[:, :, 3 * AW + c2 * 256:3 * AW + (c2 + 1) * 256]),
            (lambda tl: tl[:, :, 256:512], wv[:, :, 3 * AW + CW + c2 * 256:3 * AW + CW + (c2 + 1) * 256]),
        ])


def plan_outproj(ctx, w_out):
    wv = w_out.rearrange("(c p) f -> p c f", p=128)
    for u in range(8):
        ctx.ringA.plan([(lambda tl: tl[:, :, :], wv[:, :, u * 512:(u + 1) * 512])])


class MixDram:
    def __init__(self, nc):
        self.qT = nc.dram_tensor("mx_qT", [AW, TOK], BF16).ap()
        self.xin = nc.dram_tensor("mx_xin", [2 * AW, TOK], BF16).ap()
        self.xout = nc.dram_tensor("mx_xout", [4, 2 * 1024, TOK], BF16).ap()
        self.u = nc.dram_tensor("mx_u", [CW, TOK], F32).ap()
        self.tin = nc.dram_tensor("mx_tin", [CW, 32], F32).ap()
        self.tout = nc.dram_tensor("mx_tout", [2 * CW, 32], F32).ap()
        self.attn = nc.dram_tensor("mx_attn", [AW, TOK], F32).ap()
        self.y = nc.dram_tensor("mx_y", [D, TOK], BF16).ap()
        mk = lambda k, n: [Buf("mxb_%s%d" % (k, i)) for i in range(n)]
        self.b = {"qT": mk("qT", 16), "kown": mk("kown", 16), "vown": mk("vown", 32), "xout": mk("xout", 1),
                  "u": mk("u", 16), "tin": mk("tin", 16), "tout": mk("tout", 1), "attn": mk("attn", 16), "y": mk("y", 32)}
        self.kown = self.xin[0:AW, :]
        self.vown = self.xin[AW:2 * AW, :].rearrange("(t two) c -> t (two c)", two=2)
        self.vprev_h = [self.xout[2 + i, 0:1024, :].rearrange("(t two) c -> t (two c)", two=2) for i in range(2)]
        self.tprev = self.tout[0:CW, :]

    def kprev(self, h):
        return self.xout[h // 8, (h % 8) * 128:(h % 8 + 1) * 128, :]


def inproj_tile(ctx, md, tt, gcol, gcolb):
    P = ctx.P
    nc = P.nc
    rmsnorm_tile(ctx, gcol, gcolb)
    t0 = tt * TT
    bank_i = 0
    for u in range(20):
        wa, wab = ctx.ringA.acquire()
        if u < 8:
            for cc in range(4):
                bank = bank_i % 4
                bank_i += 1
                for dc in range(DC):
                    P.op(P.pe, lambda bank=bank, cc=cc, dc=dc: nc.tensor.matmul(
                        ctx.ps[bank][:], lhsT=wa[:, dc, cc * 128:(cc + 1) * 128], rhs=ctx.hT[:, dc, :],
                        start=(dc == 0), stop=(dc == DC - 1)),
                        reads=[wab, ctx.hTb[dc]], writes=[ctx.psb[bank]], signal=(dc == DC - 1))
                st, stb = ctx.gbuf[bank // 2][:, bank % 2, :], ctx.gb[bank // 2][bank % 2]
                if cc % 2 == 0:
                    P.op(P.act, lambda bank=bank, st=st: nc.scalar.copy(out=st, in_=ctx.ps[bank][:]),
                         reads=[ctx.psb[bank]], writes=[stb])
                else:
                    P.op(P.dve, lambda bank=bank, st=st: nc.vector.tensor_copy(out=st, in_=ctx.ps[bank][:]),
                         reads=[ctx.psb[bank]], writes=[stb])
                row = (u % 4) * 512 + cc * 128
                if u < 4:
                    dst, dstb = md.qT[row:row + 128, t0:t0 + TT], md.b["qT"][row // 128]
                else:
                    dst, dstb = md.kown[row:row + 128, t0:t0 + TT], md.b["kown"][row // 128]
                P.op(P.sp, lambda dst=dst, st=st: nc.sync.dma_start(out=dst, in_=st), reads=[stb], writes=[dstb], dma=True)
        elif u < 12:
            for ts in range(4):
                bank = bank_i % 4
                bank_i += 1
                for dc in range(DC):
                    P.op(P.pe, lambda bank=bank, ts=ts, dc=dc: nc.tensor.matmul(
                        ctx.ps[bank][:], lhsT=ctx.hT[:, dc, ts * 128:(ts + 1) * 128], rhs=wa[:, dc, :],
                        start=(dc == 0), stop=(dc == DC - 1)),
                        reads=[wab, ctx.hTb[dc]], writes=[ctx.psb[bank]], signal=(dc == DC - 1))
                st, stb = ctx.gbuf[bank // 2][:, bank % 2, :], ctx.gb[bank // 2][bank % 2]
                if ts % 2 == 0:
                    P.op(P.act, lambda bank=bank, st=st: nc.scalar.copy(out=st, in_=ctx.ps[bank][:]),
                         reads=[ctx.psb[bank]], writes=[stb])
                else:
                    P.op(P.dve, lambda bank=bank, st=st: nc.vector.tensor_copy(out=st, in_=ctx.ps[bank][:]),
                         reads=[ctx.psb[bank]], writes=[stb])
                r0 = t0 + ts * 128
                c0 = (u - 8) * 512
                dst = md.vown[r0:r0 + 128, c0:c0 + 512]
                P.op(P.sp, lambda dst=dst, st=st: nc.sync.dma_start(out=dst, in_=st), reads=[stb], writes=[md.b["vown"][(r0 // 128) * 4 + (u - 8)]], dma=True)
        else:
            c2 = u - 12
            for cc in range(2):
                ba = (bank_i % 2) * 2
                bank_i += 1
                bb = ba + 1
                for (bank, col) in ((ba, cc * 128), (bb, 256 + cc * 128)):
                    for dc in range(DC):
                        P.op(P.pe, lambda bank=bank, col=col, dc=dc: nc.tensor.matmul(
                            ctx.ps[bank][:], lhsT=wa[:, dc, col:col + 128], rhs=ctx.hT[:, dc, :],
                            start=(dc == 0), stop=(dc == DC - 1)),
                            reads=[wab, ctx.hTb[dc]], writes=[ctx.psb[bank]], signal=(dc == DC - 1))
                s = ba // 2
                P.op(P.act, lambda bb=bb, s=s: nc.scalar.activation(out=ctx.sig[s][:], in_=ctx.ps[bb][:], func=AF.Sigmoid),
                     reads=[ctx.psb[bb]], writes=[ctx.sigb[s]])
                P.op(P.dve, lambda ba=ba, s=s: nc.vector.tensor_tensor(out=ctx.sig[s][:], in0=ctx.ps[ba][:], in1=ctx.sig[s][:], op=ALU.mult),
                     reads=[ctx.psb[ba], ctx.sigb[s]], writes=[ctx.sigb[s]])
                row = (c2 * 2 + cc) * 128
                P.op(P.sp, lambda row=row, s=s: nc.sync.dma_start(out=md.u[row:row + 128, t0:t0 + TT], in_=ctx.sig[s][:]),
                     reads=[ctx.sigb[s]], writes=[md.b["u"][row // 128]], dma=True)
                if tt == 1:
                    P.op(P.sp, lambda row=row, s=s: nc.sync.dma_start(out=md.tin[row:row + 128, 2:32], in_=ctx.sig[s][:, TT - 30:TT]),
                         reads=[ctx.sigb[s]], writes=[md.b["tin"][row // 128]], dma=True)
        ctx.ringA.release()


def outproj_tile(ctx, md, tt):
    P = ctx.P
    nc = P.nc
    t0 = tt * TT
    yv = md.y.rearrange("(c p) t -> p c t", p=128)
    for q in range(4):
        P.op(P.sp, lambda q=q: nc.sync.dma_start(out=ctx.hT[:, q * 8:(q + 1) * 8, :], in_=yv[:, q * 8:(q + 1) * 8, t0:t0 + TT]),
             reads=md.b["y"][q * 8:(q + 1) * 8], writes=ctx.hTb[q * 8:(q + 1) * 8], dma=True)
    bank_i = 0
    for u in range(8):
        wa, wab = ctx.ringA.acquire()
        for cc in range(4):
            bank = bank_i % 4
            bank_i += 1
            oc = u * 4 + cc
            for dc in range(DC):
                P.op(P.pe, lambda bank=bank, cc=cc, dc=dc: nc.tensor.matmul(
                    ctx.ps[bank][:], lhsT=wa[:, dc, cc * 128:(cc + 1) * 128], rhs=ctx.hT[:, dc, :],
                    start=(dc == 0), stop=(dc == DC - 1)),
                    reads=[wab, ctx.hTb[dc]], writes=[ctx.psb[bank]], signal=(dc == DC - 1))
            P.op(P.dve, lambda bank=bank, oc=oc: nc.vector.tensor_tensor(
                out=ctx.xacc[:, oc, :], in0=ctx.ps[bank][:], in1=ctx.xacc[:, oc, :], op=ALU.add),
                reads=[ctx.psb[bank], ctx.xaccb[oc]], writes=[ctx.xaccb[oc]])
        ctx.ringA.release()


def exchange(ctx, md, groups):
    P = ctx.P
    nc = P.nc
    P.op(P.pool, lambda: nc.gpsimd.collective_compute(
        "AllGather", ALU.bypass, replica_groups=groups, ins=[md.tin], outs=[md.tout]),
        reads=md.b["tin"], writes=md.b["tout"], cc=True)
    for k in range(4):
        P.op(P.pool, lambda k=k: nc.gpsimd.collective_compute(
            "AllGather", ALU.bypass, replica_groups=groups, ins=[md.xin[k * 1024:(k + 1) * 1024, :]], outs=[md.xout[k]]),
            reads=md.b["kown"] + md.b["vown"], writes=md.b["xout"], cc=True)


class ConvConsts:
    NCOL = 16 * 31 + 16 * 5

    def __init__(self, P, name):
        self.t = P.sbuf(name, [128, self.NCOL], F32)
        self.b = Buf(name)

    def dwk(self, c, j):
        o = c * 31 + j
        return self.t[:, o:o + 1]

    def vec(self, which, c):
        o = 16 * 31 + which * 16 + c
        return self.t[:, o:o + 1]


def host_conv_consts(dw_kernel, dw_bias, ln_g, ln_b, conv_out_g, attn_out_g):
    out = np.zeros((128, ConvConsts.NCOL), np.float32)
    out[:, :16 * 31] = dw_kernel.reshape(31, 16, 128).transpose(2, 1, 0).reshape(128, 16 * 31)
    for w, v in enumerate((dw_bias, ln_g, ln_b, conv_out_g, attn_out_g)):
        out[:, 16 * 31 + w * 16:16 * 31 + (w + 1) * 16] = v.reshape(16, 128).T
    return out


CONV_PE_CH = (2, 5, 8, 11, 14)


def conv_phase(ctx, md, cc_, flag, flagb, ident_d):
    P = ctx.P
    nc = P.nc
    NCH = 16
    PE_CH = CONV_PE_CH
    ucur = [P.sbuf("cv_u%d" % i, [128, 32 + TOK], F32) for i in range(2)]
    ucb = [Buf("cv_u%d" % i) for i in range(2)]
    ucurp = P.sbuf("cv_up", [128, 32 + TOK], F32)
    ucpb = Buf("cv_up")
    dg = P.sbuf("cv_dg", [128, CK, 128], F32)
    dgb = Buf("cv_dg")
    idt = P.sbuf("cv_id", [128, 128], F32)
    idtb = Buf("cv_id")
    P.op(P.sp, lambda: nc.sync.dma_start(out=idt[:], in_=ident_d), writes=[idtb], dma=True)
    cy = P.sbuf("cv_y", [128, NCH, TOK], F32)
    cyb = [Buf("cv_y%d" % i) for i in range(NCH)]
    sq0 = P.sbuf("cv_sq", [128, TOK], F32)
    sq = [sq0, sq0]
    sqb0 = Buf("cv_sq")
    sqb = [sqb0, sqb0]
    mu = P.sbuf("cv_mu", [128, TOK], F32)
    mub = Buf("cv_mu")
    rs = P.sbuf("cv_rs", [128, TOK], F32)
    rsb = Buf("cv_rs")
    yb0 = P.sbuf("cv_yb", [128, TOK], BF16)
    yb16 = [yb0, yb0]
    yb0b = Buf("cv_yb")
    yb16b = [yb0b, yb0b]
    ps, psb = ctx.ps, ctx.psb
    order = []
    dl = [c for c in range(NCH) if c not in PE_CH]
    pl = [c for c in range(NCH) if c in PE_CH]
    while dl or pl:
        for _ in range(2):
            if dl:
                order.append(dl.pop(0))
        if pl:
            order.append(pl.pop(0))
    nd_i = 0
    np_i = 0
    for c in order:
        if c in PE_CH:
            P.op(P.sp, lambda: nc.sync.dma_start(out=ucurp[:, 32:32 + TOK], in_=md.u[c * 128:(c + 1) * 128, :]),
                 reads=[md.b["u"][c]], writes=[ucpb], dma=True)
            P.op(P.sp, lambda: nc.sync.dma_start(out=ucurp[:, 0:32], in_=md.tprev[c * 128:(c + 1) * 128, :]),
                 reads=md.b["tout"], writes=[ucpb], dma=True)
            P.op(P.act, lambda: nc.scalar.mul(out=ucurp[:, 2:32], in_=ucurp[:, 2:32], mul=flag[:, 0:1]),
                 reads=[ucpb, flagb], writes=[ucpb])
            for j in range(CK):
                P.op(P.act, lambda: nc.scalar.mul(out=dg[:, j, :], in_=idt[:], mul=cc_.dwk(c, j)),
                     reads=[idtb, cc_.b], writes=[dgb])
            for hh in range(2):
                bank = 4 + (np_i % 2) * 2 + hh
                for j in range(CK):
                    P.op(P.pe, lambda: nc.tensor.matmul(ps[bank][:], lhsT=dg[:, j, :], rhs=ucurp[:, 2 + j + hh * 512:2 + j + (hh + 1) * 512],
                                                        start=(j == 0), stop=(j == CK - 1)),
                         reads=[dgb, ucpb], writes=[psb[bank]], signal=(j == CK - 1))
                P.op(P.act, lambda: nc.scalar.activation(out=cy[:, c, hh * 512:(hh + 1) * 512], in_=ps[bank][:], func=AF.Identity,
                                                         bias=cc_.vec(0, c), scale=1.0),
                     reads=[psb[bank], cc_.b], writes=[cyb[c]])
            np_i += 1
        else:
            s = nd_i % 2
            nd_i += 1
            P.op(P.sp, lambda: nc.sync.dma_start(out=ucur[s][:, 32:32 + TOK], in_=md.u[c * 128:(c + 1) * 128, :]),
                 reads=[md.b["u"][c]], writes=[ucb[s]], dma=True)
            P.op(P.sp, lambda: nc.sync.dma_start(out=ucur[s][:, 0:32], in_=md.tprev[c * 128:(c + 1) * 128, :]),
                 reads=md.b["tout"], writes=[ucb[s]], dma=True)
            P.op(P.dve, lambda: nc.vector.tensor_scalar(out=ucur[s][:, 2:32], in0=ucur[s][:, 2:32], scalar1=flag[:, 0:1], scalar2=None, op0=ALU.mult),
                 reads=[ucb[s], flagb], writes=[ucb[s]])
            P.op(P.dve, lambda: nc.vector.tensor_scalar(
                out=cy[:, c, :], in0=ucur[s][:, 2:2 + TOK], scalar1=cc_.dwk(c, 0), scalar2=cc_.vec(0, c), op0=ALU.mult, op1=ALU.add),
                reads=[ucb[s], cc_.b], writes=[cyb[c]])
            for j in range(1, CK):
                P.op(P.dve, lambda: nc.vector.scalar_tensor_tensor(
                    out=cy[:, c, :], in0=ucur[s][:, 2 + j:2 + j + TOK], scalar=cc_.dwk(c, j), in1=cy[:, c, :], op0=ALU.mult, op1=ALU.add),
                    reads=[ucb[s], cyb[c], cc_.b], writes=[cyb[c]])
    for oi, c in enumerate(order):
        s = oi % 2
        P.op(P.act, lambda: nc.scalar.activation(out=sq[s][:], in_=cy[:, c, :], func=AF.Square),
             reads=[cyb[c]], writes=[sqb[s]])
        for hh in range(2):
            P.op(P.pe, lambda: nc.tensor.matmul(ps[hh][:], lhsT=ctx.ones[:], rhs=cy[:, c, hh * 512:(hh + 1) * 512],
                                                start=(oi == 0), stop=(oi == NCH - 1)),
                 reads=[cyb[c], ctx.onesb], writes=[psb[hh]], signal=True)
            P.op(P.pe, lambda: nc.tensor.matmul(ps[2 + hh][:], lhsT=ctx.ones[:], rhs=sq[s][:, hh * 512:(hh + 1) * 512],
                                                start=(oi == 0), stop=(oi == NCH - 1)),
                 reads=[sqb[s], ctx.onesb], writes=[psb[2 + hh]], signal=True)
    for hh in range(2):
        sl = slice(hh * 512, (hh + 1) * 512)
        P.op(P.act, lambda hh=hh, sl=sl: nc.scalar.mul(out=mu[:, sl], in_=ps[hh][:], mul=1.0 / CW),
             reads=[psb[hh]], writes=[mub])
        P.op(P.dve, lambda hh=hh, sl=sl: nc.vector.tensor_tensor(out=rs[:, sl], in0=mu[:, sl], in1=mu[:, sl], op=ALU.mult),
             reads=[mub], writes=[rsb])
        P.op(P.dve, lambda hh=hh, sl=sl: nc.vector.scalar_tensor_tensor(
            out=rs[:, sl], in0=ps[2 + hh][:], scalar=1.0 / CW, in1=rs[:, sl], op0=ALU.mult, op1=ALU.subtract),
            reads=[psb[2 + hh], rsb], writes=[rsb])
    P.op(P.act, lambda: nc.scalar.activation(out=rs[:], in_=rs[:], func=AF.Sqrt, bias=LN_EPS, scale=1.0),
         reads=[rsb], writes=[rsb])
    P.op(P.dve, lambda: nc.vector.reciprocal(out=rs[:], in_=rs[:]), reads=[rsb], writes=[rsb])
    for c in range(NCH):
        s = c % 2
        P.op(P.dve, lambda c=c: nc.vector.tensor_tensor(out=cy[:, c, :], in0=cy[:, c, :], in1=mu[:], op=ALU.subtract),
             reads=[cyb[c], mub], writes=[cyb[c]])
        P.op(P.dve, lambda c=c: nc.vector.tensor_tensor(out=cy[:, c, :], in0=cy[:, c, :], in1=rs[:], op=ALU.mult),
             reads=[cyb[c], rsb], writes=[cyb[c]])
        P.op(P.act, lambda c=c: nc.scalar.activation(out=cy[:, c, :], in_=cy[:, c, :], func=AF.Silu,
                                                     scale=cc_.vec(1, c), bias=cc_.vec(2, c)),
             reads=[cyb[c], cc_.b], writes=[cyb[c]])
        P.op(P.act, lambda c=c, s=s: nc.scalar.activation(out=sq[s][:], in_=cy[:, c, :], func=AF.Square),
             reads=[cyb[c]], writes=[sqb[s]])
        for hh in range(2):
            P.op(P.pe, lambda c=c, hh=hh, s=s: nc.tensor.matmul(ps[4 + hh][:], lhsT=ctx.ones[:], rhs=sq[s][:, hh * 512:(hh + 1) * 512],
                                                               start=(c == 0), stop=(c == NCH - 1)),
                 reads=[sqb[s], ctx.onesb], writes=[psb[4 + hh]], signal=True)
    for hh in range(2):
        sl = slice(hh * 512, (hh + 1) * 512)
        P.op(P.act, lambda hh=hh, sl=sl: nc.scalar.activation(out=rs[:, sl], in_=ps[4 + hh][:], func=AF.Sqrt, scale=1.0 / CW, bias=RMS_EPS),
             reads=[psb[4 + hh]], writes=[rsb])
    P.op(P.dve, lambda: nc.vector.reciprocal(out=rs[:], in_=rs[:]), reads=[rsb], writes=[rsb])
    for c in range(NCH):
        s = c % 2
        P.op(P.dve, lambda c=c, s=s: nc.vector.scalar_tensor_tensor(
            out=yb16[s][:], in0=cy[:, c, :], scalar=cc_.vec(3, c), in1=rs[:], op0=ALU.mult, op1=ALU.mult),
            reads=[cyb[c], rsb, cc_.b], writes=[yb16b[s]])
        P.op(P.sp, lambda c=c, s=s: nc.sync.dma_start(out=md.y[AW + c * 128:AW + (c + 1) * 128, :], in_=yb16[s][:]),
             reads=[yb16b[s]], writes=[md.b["y"][16 + c]], dma=True)


def attn_phase(ctx, md, cc_, bias_d, st):
    P = ctx.P
    nc = P.nc
    ps, psb = ctx.ps, ctx.psb
    kT = [P.sbuf("at_k%d" % i, [128, 2 * TOK], BF16) for i in range(2)]
    qT = [P.sbuf("at_q%d" % i, [128, TOK], BF16) for i in range(2)]
    v1 = [P.sbuf("at_v1%d" % i, [128, 9, HD], BF16) for i in range(2)]
    v4 = [P.sbuf("at_v4%d" % i, [128, 4, 3, HD], BF16) for i in range(2)]
    v16 = [P.sbuf("at_v16%d" % i, [128, 16, HD], BF16) for i in range(2)]
    bia = [P.sbuf("at_b%d" % i, [128, BIASW], F32) for i in range(2)]
    hb = [Buf("at_h%d" % i) for i in range(2)]
    nd = P.sbuf("at_nd", [128, 2, TOK], F32)
    ndb = Buf("at_nd")
    at = [P.sbuf("at_at%d" % i, [128, TOK], F32) for i in range(2)]
    atb = [Buf("at_at%d" % i) for i in range(2)]
    sqh = P.sbuf("at_sq", [128, TOK], F32)
    sqhb = Buf("at_sq")
    rsa = P.sbuf("at_rs", [128, TOK], F32)
    rsab = Buf("at_rs")
    yb16 = [P.sbuf("at_yb%d" % i, [128, TOK], BF16) for i in range(2)]
    yb16b = [Buf("at_yb%d" % i) for i in range(2)]
    onesb16 = P.sbuf("at_ones", [128, 128], BF16)
    onesb16b = Buf("at_ones")
    P.op(P.dve, lambda: nc.vector.memset(onesb16[:], 1.0), writes=[onesb16b])

    vown = md.vown

    hbl = [[Buf("at_h%d_%d" % (i, j)) for j in range(16)] for i in range(2)]

    def load_head(h):
        s = h % 2
        hc = slice(h * HD, (h + 1) * HD)
        cnt = [0]

        def dm(out, in_, rd):
            w = [hbl[s][cnt[0]]]
            cnt[0] += 1
            P.op(P.sp, lambda: nc.sync.dma_start(out=out, in_=in_), reads=rd, writes=w, dma=True)
        XO = md.b["xout"]
        VO = [md.b["vown"][tb * 4 + h // 4] for tb in range(8)]
        dm(kT[s][:, 0:TOK], md.kprev(h), XO)
        dm(kT[s][:, TOK:2 * TOK], md.kown[hc, :], [md.b["kown"][h]])
        dm(qT[s][:], md.qT[hc, :], [md.b["qT"][h]])
        dm(bia[s][:], bias_d[h], [])
        dm(v1[s][:, 0:1, :], md.vprev_h[1].rearrange("(b j) c -> j b c", j=128)[:, 3:4, hc], XO)
        dm(v1[s][:, 1:9, :], vown.rearrange("(b j) c -> j b c", j=128)[:, :, hc], VO)
        vp4 = md.vprev_h[1].rearrange("(j r) c -> j r c", r=4)
        vo4 = vown.rearrange("(b j r) c -> j r b c", j=128, r=4)
        dm(v4[s][:, :, 0, :], vp4[:, :, hc], XO)
        for r in range(4):
            dm(v4[s][:, r, 1:3, :], vo4[:, r, :, hc], VO)
        for i in range(2):
            dm(v16[s][32 * i:32 * (i + 1), :, :], md.vprev_h[i].rearrange("(j r) c -> j r c", r=16)[:, :, hc], XO)
        dm(v16[s][64:128, :, :], vown.rearrange("(j r) c -> j r c", r=16)[:, :, hc], VO)
        assert cnt[0] <= 16

    last_of = {}
    NR = 3
    SB = [P.sbuf("at_sbr%d" % i, [128, 512], F32) for i in range(NR)]
    SBb = [Buf("at_sbr%d" % i) for i in range(NR)]
    PT = [P.sbuf("at_pr%d" % i, [128, 512], BF16) for i in range(NR)]
    PTb = [Buf("at_pr%d" % i) for i in range(NR)]
    nd2 = [nd, P.sbuf("at_nd2", [128, 2, TOK], F32)]
    nd2b = [ndb, Buf("at_nd2")]
    SBANK = [0, 1, 2]
    OBANK = [3, 6, 7]

    def front(k, u):
        (s, subs, bias_ap, n, dst_fn, first, h) = u
        i = k % NR
        pss, pssb = ps[SBANK[i]], psb[SBANK[i]]
        nk = len(subs[0][0])
        nmm = len(subs) * nk
        c = 0
        for (kaps, vaps, qap) in subs:
            for kap in kaps:
                P.op(P.pe, lambda: nc.tensor.matmul(pss[:, c * n:(c + 1) * n], lhsT=kap, rhs=qap, start=True, stop=True),
                     reads=hbl[s], writes=[pssb], signal=(c == nmm - 1))
                c += 1
        w = nmm * n
        P.op(P.dve, lambda: nc.vector.scalar_tensor_tensor(out=SB[i][:, 0:w], in0=pss[:, 0:w], scalar=QSCALE, in1=bias_ap,
                                                           op0=ALU.mult, op1=ALU.add),
             reads=[pssb] + hbl[s], writes=[SBb[i]])
        P.op(P.act, lambda: nc.scalar.activation(out=PT[i][:, 0:w], in_=SB[i][:, 0:w], func=AF.Exp),
             reads=[SBb[i]], writes=[PTb[i]])

    def back(k, u):
        (s, subs, bias_ap, n, dst_fn, first, h) = u
        i = k % NR
        pso, psob = ps[OBANK[i]], psb[OBANK[i]]
        nk = len(subs[0][0])
        ns = len(subs)
        for si, (kaps, vaps, qap) in enumerate(subs):
            for a, vap in enumerate(vaps):
                P.op(P.pe, lambda: nc.tensor.matmul(pso[:, si * 2 * n:si * 2 * n + n], lhsT=vap,
                                                    rhs=PT[i][:, (si * nk + a) * n:(si * nk + a + 1) * n],
                                                    start=(a == 0), stop=(a == nk - 1)),
                     reads=hbl[s] + [PTb[i]], writes=[psob], signal=False)
            for a in range(nk):
                P.op(P.pe, lambda: nc.tensor.matmul(pso[:, si * 2 * n + n:(si + 1) * 2 * n], lhsT=onesb16[:],
                                                    rhs=PT[i][:, (si * nk + a) * n:(si * nk + a + 1) * n],
                                                    start=(a == 0), stop=(a == nk - 1)),
                     reads=[onesb16b, PTb[i]], writes=[psob], signal=(a == nk - 1 and si == ns - 1))
        src = pso[:, 0:ns * 2 * n].rearrange("p (s two q) -> p two s q", s=ns, two=2)
        ndt, ndtb = nd2[h % 2], nd2b[h % 2]
        dst = dst_fn(ndt)
        if first:
            P.op(P.act, lambda: nc.scalar.copy(out=dst, in_=src), reads=[psob], writes=[ndtb])
        else:
            P.op(P.dve, lambda: nc.vector.tensor_tensor(out=dst, in0=src, in1=dst, op=ALU.add), reads=[psob, ndtb], writes=[ndtb])

    def finalize(h):
        a = h % 2
        ndt, ndtb = nd2[h % 2], nd2b[h % 2]
        P.op(P.dve, lambda: nc.vector.reciprocal(out=ndt[:, 1, :], in_=ndt[:, 1, :]), reads=[ndtb], writes=[ndtb])
        P.op(P.dve, lambda: nc.vector.tensor_tensor(out=at[a][:], in0=ndt[:, 0, :], in1=ndt[:, 1, :], op=ALU.mult),
             reads=[ndtb], writes=[atb[a]])
        P.op(P.act, lambda: nc.scalar.activation(out=sqh[:], in_=at[a][:], func=AF.Square), reads=[atb[a]], writes=[sqhb])
        for hh in range(2):
            P.op(P.pe, lambda: nc.tensor.matmul(ps[4 + hh][:], lhsT=ctx.ones[:], rhs=sqh[:, hh * 512:(hh + 1) * 512],
                                                start=(h == 0), stop=(h == NH - 1)),
                 reads=[sqhb, ctx.onesb], writes=[psb[4 + hh]], signal=True)
        P.op(P.sp, lambda: nc.sync.dma_start(out=md.attn[h * HD:(h + 1) * HD, :], in_=at[a][:]),
             reads=[atb[a]], writes=[md.b["attn"][h]], dma=True)

    def head_units(h):
        s = h % 2
        us = []
        for p in range(4):
            subs = []
            for qb in (2 * p, 2 * p + 1):
                subs.append(([kT[s][:, (7 + qb) * 128:(8 + qb) * 128], kT[s][:, (8 + qb) * 128:(9 + qb) * 128]],
                             [v1[s][:, qb, :], v1[s][:, qb + 1, :]], qT[s][:, qb * 128:(qb + 1) * 128]))
            boff = 0 if p == 0 else 256
            dst_fn = lambda t, p=p: t[:, :, p * 256:(p + 1) * 256].rearrange("p two (s q) -> p two s q", s=2)
            us.append((s, subs, bia[s][:, boff:boff + 512], 128, dst_fn, True, h))
        for r in range(4):
            subs = []
            for b in (2, 3):
                kp = kT[s][:, 512 * (b - 1) + r:512 * b:4]
                kc = kT[s][:, 512 * b + r:512 * (b + 1):4]
                subs.append(([kp, kc], [v4[s][:, r, b - 2, :], v4[s][:, r, b - 1, :]], qT[s][:, 512 * (b - 2) + r:512 * (b - 1):4]))
            dst_fn = lambda t, r=r: t[:, :, r:TOK:4].rearrange("p two (s q) -> p two s q", s=2)
            us.append((s, subs, bia[s][:, 768:768 + 512], 128, dst_fn, False, h))
        for rp in range(8):
            subs = []
            for r in (2 * rp, 2 * rp + 1):
                subs.append(([kT[s][:, r:2 * TOK:16]], [v16[s][:, r, :]], qT[s][:, r:TOK:16]))
            dst_fn = lambda t, rp=rp: t[:, :, :].rearrange("p two (q r) -> p two r q", r=16)[:, :, 2 * rp:2 * rp + 2, :]
            us.append((s, subs, bia[s][:, 1280:1280 + 128], 64, dst_fn, False, h))
        return us

    LAG = 2
    load_head(0)
    pend = []
    k = 0
    for h in range(NH):
        for ui, u in enumerate(head_units(h)):
            front(k, u)
            pend.append((k, u))
            k += 1
            if len(pend) > LAG:
                kk, uu = pend.pop(0)
                back(kk, uu)
                if uu is last_of.get(uu[6]):
                    finalize(uu[6])
            if ui == LAG + 1 and h + 1 < NH:
                load_head(h + 1)
        last_of[h] = u
    while pend:
        kk, uu = pend.pop(0)
        back(kk, uu)
        if uu is last_of.get(uu[6]):
            finalize(uu[6])
    for hh in range(2):
        sl = slice(hh * 512, (hh + 1) * 512)
        P.op(P.act, lambda hh=hh, sl=sl: nc.scalar.activation(out=rsa[:, sl], in_=ps[4 + hh][:], func=AF.Sqrt, scale=1.0 / AW, bias=RMS_EPS),
             reads=[psb[4 + hh]], writes=[rsab])
    P.op(P.dve, lambda: nc.vector.reciprocal(out=rsa[:], in_=rsa[:]), reads=[rsab], writes=[rsab])
    for h in range(NH):
        a = h % 2
        P.op(P.sp, lambda a=a, h=h: nc.sync.dma_start(out=at[a][:], in_=md.attn[h * HD:(h + 1) * HD, :]),
             reads=[md.b["attn"][h]], writes=[atb[a]], dma=True)
        P.op(P.dve, lambda a=a, h=h: nc.vector.scalar_tensor_tensor(
            out=yb16[a][:], in0=at[a][:], scalar=cc_.vec(4, h), in1=rsa[:], op0=ALU.mult, op1=ALU.mult),
            reads=[atb[a], rsab, cc_.b], writes=[yb16b[a]])
        P.op(P.sp, lambda a=a, h=h: nc.sync.dma_start(out=md.y[h * HD:(h + 1) * HD, :], in_=yb16[a][:]),
             reads=[yb16b[a]], writes=[md.b["y"][h]], dma=True)

DEPTH = 2
D_FF = 11008
N_CORES = 8


def final_norm_tile(ctx, gcol, gcolb):
    P = ctx.P
    nc = P.nc
    pst, pstb = ctx.ps[4], ctx.psb[4]
    for dc in range(DC):
        s = dc % 2
        P.op(P.act, lambda: nc.scalar.activation(out=ctx.sq[s][:], in_=ctx.xacc[:, dc, :], func=AF.Square),
             reads=[ctx.xaccb[dc]], writes=[ctx.sqb[s]])
        P.op(P.pe, lambda: nc.tensor.matmul(pst[:], lhsT=ctx.ones[:], rhs=ctx.sq[s][:], start=(dc == 0), stop=(dc == DC - 1)),
             reads=[ctx.sqb[s], ctx.onesb], writes=[pstb], signal=True)
    P.op(P.act, lambda: nc.scalar.activation(out=ctx.rstd[:], in_=pst[:], func=AF.Sqrt, scale=1.0 / D, bias=RMS_EPS),
         reads=[pstb], writes=[ctx.rstdb])
    P.op(P.dve, lambda: nc.vector.reciprocal(out=ctx.rstd[:], in_=ctx.rstd[:]), reads=[ctx.rstdb], writes=[ctx.rstdb])
    for dc in range(DC):
        P.op(P.dve, lambda: nc.vector.scalar_tensor_tensor(
            out=ctx.xacc[:, dc, :], in0=ctx.xacc[:, dc, :], scalar=gcol[:, dc:dc + 1], in1=ctx.rstd[:],
            op0=ALU.mult, op1=ALU.mult),
            reads=[ctx.xaccb[dc], ctx.rstdb, gcolb], writes=[ctx.xaccb[dc]])


def build_program(F, n_cores):
    nc = bass.Bass("TRN2", target_bir_lowering=False)
    L = DEPTH
    ein = lambda name, shape: nc.dram_tensor(name, list(shape), F32, kind="ExternalInput").ap()
    xT = ein("xT", [D, TOK])
    w = {}
    for nm in ("ffn1_w1", "ffn1_w3", "ffn2_w1", "ffn2_w3"):
        w[nm] = ein(nm, [L, D, F])
    for nm in ("ffn1_w2", "ffn2_w2"):
        w[nm] = ein(nm, [L, F, D])
    w["w_in"] = ein("w_in", [L, D, 3 * AW + 2 * CW])
    w["w_out"] = ein("w_out", [L, D, D])
    gcd = ein("gcols", [128, 7 * DC])
    ccd = ein("convc", [L, 128, ConvConsts.NCOL])
    flagd = ein("flag", [128, 1])
    biasd = ein("biasd", [NH, 128, BIASW])
    identd = ein("ident", [128, 128])
    out = nc.dram_tensor("outT", [D, TOK], F32, kind="ExternalOutput").ap()
    xres = nc.dram_tensor("xres", [D, TOK], F32).ap()
    md = MixDram(nc)
    groups = [[2 * i, 2 * i + 1] for i in range(n_cores // 2)]
    with ExitStack() as st:
        P = Prog(nc, st)
        ctx = Ctx(P)
        gcol = P.sbuf("gcol", [128, 7 * DC], F32)
        gcolb = Buf("gcol")
        P.op(P.sp, lambda: nc.sync.dma_start(out=gcol[:], in_=gcd), writes=[gcolb], dma=True)
        flag = P.sbuf("flag", [128, 1], F32)
        flagb = Buf("flag")
        P.op(P.sp, lambda: nc.sync.dma_start(out=flag[:], in_=flagd), writes=[flagb], dma=True)
        g_of = lambda l, k: gcol[:, (l * 3 + k) * DC:(l * 3 + k + 1) * DC]
        g_fin = gcol[:, 6 * DC:7 * DC]
        for tt in range(2):
            plan_ffn(ctx, w["ffn1_w1"][0], w["ffn1_w3"][0], w["ffn1_w2"][0], F, ntiles=1)
            plan_inproj(ctx, w["w_in"][0])
        for tt in range(2):
            plan_outproj(ctx, w["w_out"][0])
            plan_ffn(ctx, w["ffn2_w1"][0], w["ffn2_w3"][0], w["ffn2_w2"][0], F, ntiles=1)
            plan_ffn(ctx, w["ffn1_w1"][1], w["ffn1_w3"][1], w["ffn1_w2"][1], F, ntiles=1)
            plan_inproj(ctx, w["w_in"][1])
        for tt in range(2):
            plan_outproj(ctx, w["w_out"][1])
            plan_ffn(ctx, w["ffn2_w1"][1], w["ffn2_w3"][1], w["ffn2_w2"][1], F, ntiles=1)
        ctx.ringA.start()
        ctx.ringB.start()
        xin_v = xT.rearrange("(c p) t -> p c t", p=128)
        xres_v = xres.rearrange("(c p) t -> p c t", p=128)
        out_v = out.rearrange("(c p) t -> p c t", p=128)
        xinb = [Buf("xin0"), Buf("xin1")]
        xresb = [Buf("xres0"), Buf("xres1")]
        outb = [Buf("out0"), Buf("out1")]

        def mixer_core(l):
            exchange(ctx, md, groups)
            P.push()
            cc_ = ConvConsts(P, "ccA%d" % l)
            P.op(P.sp, lambda: nc.sync.dma_start(out=cc_.t[:], in_=ccd[l]), writes=[cc_.b], dma=True)
            conv_phase(ctx, md, cc_, flag, flagb, identd)
            P.pop()
            P.push()
            cc2 = ConvConsts(P, "ccB%d" % l)
            P.op(P.sp, lambda: nc.sync.dma_start(out=cc2.t[:], in_=ccd[l]), writes=[cc2.b], dma=True)
            attn_phase(ctx, md, cc2, biasd, None)
            P.pop()

        P.push()
        ctx.alloc_dense()
        for tt in range(2):
            load_x(ctx, xin_v, xinb, tt)
            ffn_tile(ctx, F, g_of(0, 0), gcolb)
            inproj_tile(ctx, md, tt, g_of(0, 1), gcolb)
            store_x(ctx, xres_v, xresb, tt)
        P.pop()
        mixer_core(0)
        P.push()
        ctx.alloc_dense()
        for tt in range(2):
            load_x(ctx, xres_v, xresb, tt)
            outproj_tile(ctx, md, tt)
            ffn_tile(ctx, F, g_of(0, 2), gcolb)
            ffn_tile(ctx, F, g_of(1, 0), gcolb)
            inproj_tile(ctx, md, tt, g_of(1, 1), gcolb)
            store_x(ctx, xres_v, xresb, tt)
        P.pop()
        mixer_core(1)
        P.push()
        ctx.alloc_dense()
        for tt in range(2):
            load_x(ctx, xres_v, xresb, tt)
            outproj_tile(ctx, md, tt)
            ffn_tile(ctx, F, g_of(1, 2), gcolb)
            final_norm_tile(ctx, g_fin, gcolb)
            store_x(ctx, out_v, outb, tt)
        P.finish(outb)
        P.pop()
        assert ctx.ringA.consumed == len(ctx.ringA.units) and ctx.ringB.consumed == len(ctx.ringB.units)
        build_program.stats = (P.ninstr, P.nwaits, P.nsem)
    return nc


def make_in_maps(inputs, n_cores):
    f = lambda k: np.ascontiguousarray(np.asarray(inputs[k], dtype=np.float32))
    x = f("x")
    col = lambda v: np.ascontiguousarray(np.asarray(v, np.float32).reshape(DC, 128).T)
    gl = []
    for l in range(DEPTH):
        gl += [col(inputs["ffn1_norm_g"][l]), col(inputs["mix_norm_g"][l]), col(inputs["ffn2_norm_g"][l])]
    gl.append(col(inputs["final_norm_g"]))
    gcols = np.ascontiguousarray(np.concatenate(gl, axis=1))
    convc = np.stack([host_conv_consts(np.asarray(inputs["dw_kernel"][l]), np.asarray(inputs["dw_bias"][l]),
                                       np.asarray(inputs["conv_ln_g"][l]), np.asarray(inputs["conv_ln_b"][l]),
                                       np.asarray(inputs["conv_out_g"][l]), np.asarray(inputs["attn_out_g"][l]))
                      for l in range(DEPTH)])
    rel = np.asarray(inputs["rel_bias_table"], np.float32)
    bias_h = [build_bias(rel, 0), build_bias(rel, 1)]
    shared = {k: f(k) for k in ("ffn1_w1", "ffn1_w3", "ffn1_w2", "ffn2_w1", "ffn2_w3", "ffn2_w2", "w_in", "w_out")}
    maps = []
    for c in range(n_cores):
        b, half = c // 2, c % 2
        m = dict(shared)
        m["xT"] = np.ascontiguousarray(x[b, half * TOK:(half + 1) * TOK, :].T)
        m["gcols"] = gcols
        m["convc"] = convc
        m["flag"] = np.full((128, 1), float(half), np.float32)
        m["biasd"] = bias_h[half]
        m["ident"] = np.eye(128, dtype=np.float32)
        maps.append(m)
    return maps


def kernel(**inputs):
    x = np.asarray(inputs["x"])
    B, S, _ = x.shape
    n_cores = B * 2
    F = np.asarray(inputs["ffn1_w1"]).shape[-1]
    nc = build_program(F, n_cores)
    maps = make_in_maps(inputs, n_cores)
    res = run_bass_kernel_spmd(nc, maps, core_ids=list(range(n_cores)))
    out = np.empty((B, S, D), np.float32)
    for c in range(n_cores):
        b, half = c // 2, c % 2
        out[b, half * TOK:(half + 1) * TOK, :] = res.results[c]["outT"].T
    return out
```

```python
from contextlib import ExitStack
from concourse.bass_utils import run_bass_kernel_spmd

import numpy as np
import concourse.bass as bass
import concourse.mybir as mybir

F32 = mybir.dt.float32
BF16 = mybir.dt.bfloat16
AF = mybir.ActivationFunctionType
ALU = mybir.AluOpType
AX = mybir.AxisListType


class Buf:
    __slots__ = ("name", "writer", "readers", "dreaders")

    def __init__(self, name):
        self.name = name
        self.writer = None
        self.readers = {}
        self.dreaders = []


class Node:
    __slots__ = ("eng", "idx", "sem", "val", "is_dma")

    def __init__(self, eng, idx):
        self.eng = eng
        self.idx = idx
        self.sem = None
        self.val = None
        self.is_dma = False


class Eng:
    def __init__(self, P, name, handle):
        self.P = P
        self.name = name
        self.h = handle
        self.sem = P.newsem("p_" + name)
        self.count = 0
        self.n = 0
        self.waited = {}
        self.nodes = []
        self.pending = []
        self.dring = None
        self.dcount = 0

    def wait(self, sem, val):
        key = id(sem)
        if self.waited.get(key, -1) >= val:
            return
        self.waited[key] = val
        self.h.wait_ge(sem, val)
        self.P.nwaits += 1


class Prog:
    def __init__(self, nc, stack):
        self.nc = nc
        self.stack = stack
        self.nsem = 0
        self.nalloc = 0
        self.stacks = []
        self.ccsems = []
        self.nwaits = 0
        self.ninstr = 0
        self.pe = Eng(self, "pe", nc.tensor)
        self.act = Eng(self, "act", nc.scalar)
        self.dve = Eng(self, "dve", nc.vector)
        self.pool = Eng(self, "pool", nc.gpsimd)
        self.sp = Eng(self, "sp", nc.sync)
        self.engs = [self.pe, self.act, self.dve, self.pool, self.sp]
        for e in (self.sp, self.pool, self.act):
            e.dring = [[self.newsem("d_%s%d" % (e.name, i)), 0] for i in range(12)]

    def newsem(self, name):
        self.nsem += 1
        return self.stack.enter_context(self.nc.semaphore(name))

    def sbuf(self, name, shape, dtype):
        self.nalloc += 1
        st = self.stacks[-1] if self.stacks else self.stack
        return st.enter_context(self.nc.sbuf_tensor("%s_%d" % (name, self.nalloc), list(shape), dtype))

    def push(self):
        from contextlib import ExitStack
        self.stacks.append(ExitStack())

    def pop(self):
        self.barrier()
        self.stacks.pop().close()

    def barrier(self):
        for e in self.engs:
            if e.pending:
                raise RuntimeError("barrier with pending non-signalled nodes on %s" % e.name)
        for e in self.engs:
            for e2 in self.engs:
                if e2 is not e and e2.count > 0:
                    e.wait(e2.sem, e2.count)
                if e2.dring:
                    for slot in e2.dring:
                        if slot[1] > 0:
                            e.wait(slot[0], slot[1])
            for sem in self.ccsems:
                e.wait(sem, 1)

    def psum(self, name, shape, dtype):
        return self.stack.enter_context(self.nc.psum_tensor(name, list(shape), dtype))

    def _dep(self, eng, node, raw):
        if node is None:
            return
        if node.is_dma:
            eng.wait(node.sem, node.val)
            return
        if node.eng is eng:
            if eng is self.pe:
                return
            if not raw:
                return
            if node.idx < eng.n - 2:
                return
        if node.val is None:
            raise RuntimeError("dependency on non-signalled node of %s" % node.eng.name)
        eng.wait(node.eng.sem, node.val)

    def op(self, eng, fn, reads=(), writes=(), signal=True, dma=False, cc=False):
        for b in reads:
            self._dep(eng, b.writer, True)
        for b in writes:
            self._dep(eng, b.writer, False)
            for r in b.readers.values():
                self._dep(eng, r, False)
            for r in b.dreaders:
                self._dep(eng, r, False)
        node = Node(eng, eng.n)
        if cc:
            node.is_dma = True
            sem = self.newsem("cc%d" % len(self.ccsems))
            self.ccsems.append(sem)
            ins = fn()
            ins.then_inc(sem, 1)
            node.sem, node.val = sem, 1
            dma = True
        elif dma:
            node.is_dma = True
            slot = eng.dring[eng.dcount % len(eng.dring)]
            eng.dcount += 1
            if slot[1] > 0:
                eng.wait(slot[0], slot[1])
            ins = fn()
            slot[1] += 16
            ins.then_inc(slot[0], 16)
            node.sem, node.val = slot[0], slot[1]
        else:
            ins = fn()
            if signal:
                eng.count += 1
                ins.then_inc(eng.sem, 1)
                node.val = eng.count
                for pn in eng.pending:
                    pn.val = eng.count
                eng.pending = []
            else:
                eng.pending.append(node)
        eng.n += 1
        self.ninstr += 1
        for b in reads:
            if dma:
                b.dreaders.append(node)
            else:
                b.readers[eng.name] = node
        for b in writes:
            b.writer = node
            b.readers = {}
            b.dreaders = []
        return node

    def finish(self, bufs):
        for b in bufs:
            self._dep(self.sp, b.writer, True)


D = 4096
DC = D // 128
TT = 512
RMS_EPS = 1e-6


class Ring:
    def __init__(self, P, name, nslots, shape):
        self.P = P
        self.nslots = nslots
        self.tiles = [P.sbuf("%s%d" % (name, i), shape, BF16) for i in range(nslots)]
        self.bufs = [Buf("%s%d" % (name, i)) for i in range(nslots)]
        self.units = []
        self.issued = 0
        self.consumed = 0

    def plan(self, parts):
        self.units.append(parts)

    def _issue(self):
        if self.issued >= len(self.units):
            return
        k = self.issued
        s = k % self.nslots
        tile = self.tiles[s]
        for (dst_fn, src) in self.units[k]:
            self.P.op(self.P.pool, lambda d=dst_fn(tile), s_=src: self.P.nc.gpsimd.dma_start(out=d, in_=s_),
                      writes=[self.bufs[s]], dma=True)
        self.issued += 1

    def start(self):
        while self.issued < min(self.nslots, len(self.units)):
            self._issue()

    def acquire(self):
        k = self.consumed
        assert k < self.issued, "ring unit not issued"
        s = k % self.nslots
        return self.tiles[s], self.bufs[s]

    def release(self):
        self.consumed += 1
        self._issue()


class Ctx:
    def __init__(self, P):
        nc = P.nc
        self.P = P
        self.ringA = Ring(P, "ringA", 2, [128, DC, 512])
        self.ringB = Ring(P, "ringB", 2, [128, 2, D])
        self.ps = [P.psum("ps%d" % i, [128, 512], F32) for i in range(8)]
        self.psb = [Buf("ps%d" % i) for i in range(8)]
        self.ones = P.sbuf("ones", [128, 128], F32)
        self.onesb = Buf("ones")
        P.op(P.dve, lambda: nc.vector.memset(self.ones[:], 1.0), writes=[self.onesb])

    def alloc_dense(self):
        P = self.P
        self.xacc = P.sbuf("xacc", [128, DC, TT], F32)
        self.xaccb = [Buf("xacc%d" % i) for i in range(DC)]
        self.hT = P.sbuf("hT", [128, DC, TT], BF16)
        self.hTb = [Buf("hT%d" % i) for i in range(DC)]
        self.gbuf = [P.sbuf("g%d" % i, [128, 2, TT], BF16) for i in range(2)]
        self.gb = [[Buf("g%d_%d" % (i, j)) for j in range(2)] for i in range(2)]
        self.sig = [P.sbuf("sig%d" % i, [128, TT], F32) for i in range(2)]
        self.sigb = [Buf("sig%d" % i) for i in range(2)]
        self.sq = self.sig
        self.sqb = self.sigb
        self.rstd = P.sbuf("rstd", [128, TT], F32)
        self.rstdb = Buf("rstd")


def plan_ffn(ctx, w1, w3, w2, F, ntiles=2):
    NG = F // 256
    w1v = w1.rearrange("(c p) f -> p c f", p=128)
    w3v = w3.rearrange("(c p) f -> p c f", p=128)
    w2v = w2.rearrange("(c p) d -> p c d", p=128)
    for t in range(ntiles):
        for g in range(NG):
            ctx.ringA.plan([
                (lambda tl: tl[:, :, 0:256], w1v[:, :, g * 256:(g + 1) * 256]),
                (lambda tl: tl[:, :, 256:512], w3v[:, :, g * 256:(g + 1) * 256]),
            ])
            ctx.ringB.plan([
                (lambda tl: tl[:, :, :], w2v[:, 2 * g:2 * g + 2, :]),
            ])


def rmsnorm_tile(ctx, gcol, gcolb):
    P = ctx.P
    nc = P.nc
    pst, pstb = ctx.ps[4], ctx.psb[4]
    for dc in range(DC):
        s = dc % 2
        P.op(P.act, lambda dc=dc, s=s: nc.scalar.activation(out=ctx.sq[s][:], in_=ctx.xacc[:, dc, :], func=AF.Square),
             reads=[ctx.xaccb[dc]], writes=[ctx.sqb[s]])
        P.op(P.pe, lambda dc=dc, s=s: nc.tensor.matmul(pst[:], lhsT=ctx.ones[:], rhs=ctx.sq[s][:], start=(dc == 0), stop=(dc == DC - 1)),
             reads=[ctx.sqb[s], ctx.onesb], writes=[pstb], signal=True)
    P.op(P.act, lambda: nc.scalar.activation(out=ctx.rstd[:], in_=pst[:], func=AF.Sqrt, scale=1.0 / D, bias=RMS_EPS),
         reads=[pstb], writes=[ctx.rstdb])
    P.op(P.dve, lambda: nc.vector.reciprocal(out=ctx.rstd[:], in_=ctx.rstd[:]), reads=[ctx.rstdb], writes=[ctx.rstdb])
    for dc in range(DC):
        P.op(P.dve, lambda dc=dc: nc.vector.scalar_tensor_tensor(
            out=ctx.hT[:, dc, :], in0=ctx.xacc[:, dc, :], scalar=gcol[:, dc:dc + 1], in1=ctx.rstd[:],
            op0=ALU.mult, op1=ALU.mult),
            reads=[ctx.xaccb[dc], ctx.rstdb, gcolb], writes=[ctx.hTb[dc]])


def load_x(ctx, xsrc_v, xsrcb, tt):
    P = ctx.P
    nc = P.nc
    for q in range(4):
        P.op(P.sp, lambda q=q: nc.sync.dma_start(out=ctx.xacc[:, q * 8:(q + 1) * 8, :], in_=xsrc_v[:, q * 8:(q + 1) * 8, tt * TT:(tt + 1) * TT]),
             reads=[xsrcb[tt]], writes=ctx.xaccb[q * 8:(q + 1) * 8], dma=True)


def store_x(ctx, xdst_v, xdstb, tt):
    P = ctx.P
    nc = P.nc
    for q in range(4):
        P.op(P.sp, lambda q=q: nc.sync.dma_start(out=xdst_v[:, q * 8:(q + 1) * 8, tt * TT:(tt + 1) * TT], in_=ctx.xacc[:, q * 8:(q + 1) * 8, :]),
             reads=ctx.xaccb[q * 8:(q + 1) * 8], writes=[xdstb[tt]], dma=True)


def ffn_tile(ctx, F, gcol, gcolb):
    P = ctx.P
    nc = P.nc
    NG = F // 256
    rmsnorm_tile(ctx, gcol, gcolb)
    A = [None, None]
    for g in range(NG + 1):
        if g < NG:
            wa, wab = ctx.ringA.acquire()
        if g >= 1:
            wb, wbb = ctx.ringB.acquire()
        for i in range(32):
            if g < NG:
                for j in range(4):
                    m = i * 4 + j
                    fc, r = divmod(m, 64)
                    w, dc = divmod(r, 32)
                    bank = fc * 2 + w
                    col = w * 256 + fc * 128
                    P.op(P.pe, lambda bank=bank, col=col, dc=dc: nc.tensor.matmul(
                        ctx.ps[bank][:], lhsT=wa[:, dc, col:col + 128], rhs=ctx.hT[:, dc, :],
                        start=(dc == 0), stop=(dc == DC - 1)),
                        reads=[wab, ctx.hTb[dc]], writes=[ctx.psb[bank]], signal=(dc == DC - 1))
                    if r == 63:
                        s = fc
                        P.op(P.act, lambda fc=fc, s=s: nc.scalar.activation(out=ctx.sig[s][:], in_=ctx.ps[fc * 2][:], func=AF.Silu),
                             reads=[ctx.psb[fc * 2]], writes=[ctx.sigb[s]])
                        P.op(P.dve, lambda fc=fc, s=s, g=g: nc.vector.tensor_tensor(
                            out=ctx.gbuf[g % 2][:, fc, :], in0=ctx.ps[fc * 2 + 1][:], in1=ctx.sig[s][:], op=ALU.mult),
                            reads=[ctx.psb[fc * 2 + 1], ctx.sigb[s]], writes=[ctx.gb[g % 2][fc]])
            if g >= 1:
                gp = (g - 1) % 2
                bank = 4 + (i % 4)
                for fc in range(2):
                    P.op(P.pe, lambda bank=bank, fc=fc, i=i, gp=gp: nc.tensor.matmul(
                        ctx.ps[bank][:], lhsT=wb[:, fc, i * 128:(i + 1) * 128], rhs=ctx.gbuf[gp][:, fc, :],
                        start=(fc == 0), stop=(fc == 1)),
                        reads=[wbb, ctx.gb[gp][fc]], writes=[ctx.psb[bank]], signal=(fc == 1))
                P.op(P.dve, lambda bank=bank, i=i: nc.vector.scalar_tensor_tensor(
                    out=ctx.xacc[:, i, :], in0=ctx.ps[bank][:], scalar=0.5, in1=ctx.xacc[:, i, :],
                    op0=ALU.mult, op1=ALU.add),
                    reads=[ctx.psb[bank], ctx.xaccb[i]], writes=[ctx.xaccb[i]])
        if g < NG:
            ctx.ringA.release()
        if g >= 1:
            ctx.ringB.release()


HD = 128
NH = 16
AW = 2048
CW = 2048
CK = 31
TOK = 1024
QSCALE = HD ** -0.5
LN_EPS = 1e-5
NEG = -30000.0
BIASW = 768 + 512 + 128


def rel_bucket_np(dist):
    max_exact = 16
    df = np.maximum(dist, 1).astype(np.float32)
    large = max_exact + (np.log(df / max_exact) / np.log(2048 / max_exact) * (32 - max_exact)).astype(np.int32)
    large = np.minimum(large, 31)
    return np.where(dist < max_exact, dist, large).astype(np.int32)


def build_bias(rel_table, half):
    out = np.full((NH, 128, BIASW), NEG, np.float32)
    j = np.arange(128)[:, None]
    i = np.arange(128)[None, :]
    for pi, d in enumerate((1, 4)):
        step_c = i - j
        val_c = step_c >= 0
        b_c = rel_bucket_np(np.clip(step_c, 0, None) * d)
        step_p = i + 128 - j
        val_p = step_p <= 128
        b_p = rel_bucket_np(np.clip(step_p, 0, None) * d)
        base = 0 if d == 1 else 768
        for h in range(NH):
            cur = np.where(val_c, rel_table[b_c, h], NEG).astype(np.float32)
            prv = np.where(val_p, rel_table[b_p, h], NEG).astype(np.float32)
            if half == 1:
                out[h, :, base:base + 128] = prv
            out[h, :, base + 128:base + 256] = cur
            out[h, :, base + 256:base + 384] = prv
            out[h, :, base + 384:base + 512] = cur
            if d == 1:
                out[h, :, base + 512:base + 640] = prv
                out[h, :, base + 640:base + 768] = cur
    i2 = np.arange(64)[None, :]
    step = 64 + i2 - j
    val = step >= 0
    if half == 0:
        val = val & (j >= 64)
    b16 = rel_bucket_np(np.clip(step, 0, None) * 16)
    for h in range(NH):
        t16 = np.where(val, rel_table[b16, h], NEG).astype(np.float32)
        out[h, :, 1280:1344] = t16
        out[h, :, 1344:1408] = t16
    return out


def plan_inproj(ctx, w_in):
    wv = w_in.rearrange("(c p) f -> p c f", p=128)
    for u in range(12):
        ctx.ringA.plan([(lambda tl: tl[:, :, :], wv[:, :, u * 512:(u + 1) * 512])])
    for c2 in range(8):
        ctx.ringA.plan([
            (lambda tl: tl[:, :, 0:256], wv[:, :, 3 * AW + c2 * 256:3 * AW + (c2 + 1) * 256]),
            (lambda tl: tl[:, :, 256:512], wv[:, :, 3 * AW + CW + c2 * 256:3 * AW + CW + (c2 + 1) * 256]),
        ])


def plan_outproj(ctx, w_out):
    wv = w_out.rearrange("(c p) f -> p c f", p=128)
    for u in range(8):
        ctx.ringA.plan([(lambda tl: tl[:, :, :], wv[:, :, u * 512:(u + 1) * 512])])


class MixDram:
    def __init__(self, nc):
        self.qT = nc.dram_tensor("mx_qT", [AW, TOK], BF16).ap()
        self.xin = nc.dram_tensor("mx_xin", [2 * AW, TOK], BF16).ap()
        self.xout = nc.dram_tensor("mx_xout", [4, 2 * 1024, TOK], BF16).ap()
        self.u = nc.dram_tensor("mx_u", [CW, TOK], F32).ap()
        self.tin = nc.dram_tensor("mx_tin", [CW, 32], F32).ap()
        self.tout = nc.dram_tensor("mx_tout", [2 * CW, 32], F32).ap()
        self.attn = nc.dram_tensor("mx_attn", [AW, TOK], F32).ap()
        self.y = nc.dram_tensor("mx_y", [D, TOK], BF16).ap()
        mk = lambda k, n: [Buf("mxb_%s%d" % (k, i)) for i in range(n)]
        self.b = {"qT": mk("qT", 16), "kown": mk("kown", 16), "vown": mk("vown", 32), "xout": mk("xout", 1),
                  "u": mk("u", 16), "tin": mk("tin", 16), "tout": mk("tout", 1), "attn": mk("attn", 16), "y": mk("y", 32)}
        self.kown = self.xin[0:AW, :]
        self.vown = self.xin[AW:2 * AW, :].rearrange("(t two) c -> t (two c)", two=2)
        self.vprev_h = [self.xout[2 + i, 0:1024, :].rearrange("(t two) c -> t (two c)", two=2) for i in range(2)]
        self.tprev = self.tout[0:CW, :]

    def kprev(self, h):
        return self.xout[h // 8, (h % 8) * 128:(h % 8 + 1) * 128, :]


def inproj_tile(ctx, md, tt, gcol, gcolb, after_norm=None):
    P = ctx.P
    nc = P.nc
    rmsnorm_tile(ctx, gcol, gcolb)
    if after_norm is not None:
        after_norm()
    t0 = tt * TT
    bank_i = 0
    for u in range(20):
        wa, wab = ctx.ringA.acquire()
        if u < 8:
            for cc in range(4):
                bank = bank_i % 4
                bank_i += 1
                for dc in range(DC):
                    P.op(P.pe, lambda bank=bank, cc=cc, dc=dc: nc.tensor.matmul(
                        ctx.ps[bank][:], lhsT=wa[:, dc, cc * 128:(cc + 1) * 128], rhs=ctx.hT[:, dc, :],
                        start=(dc == 0), stop=(dc == DC - 1)),
                        reads=[wab, ctx.hTb[dc]], writes=[ctx.psb[bank]], signal=(dc == DC - 1))
                st, stb = ctx.gbuf[bank // 2][:, bank % 2, :], ctx.gb[bank // 2][bank % 2]
                if cc % 2 == 0:
                    P.op(P.act, lambda bank=bank, st=st: nc.scalar.copy(out=st, in_=ctx.ps[bank][:]),
                         reads=[ctx.psb[bank]], writes=[stb])
                else:
                    P.op(P.dve, lambda bank=bank, st=st: nc.vector.tensor_copy(out=st, in_=ctx.ps[bank][:]),
                         reads=[ctx.psb[bank]], writes=[stb])
                row = (u % 4) * 512 + cc * 128
                if u < 4:
                    dst, dstb = md.qT[row:row + 128, t0:t0 + TT], md.b["qT"][row // 128]
                else:
                    dst, dstb = md.kown[row:row + 128, t0:t0 + TT], md.b["kown"][row // 128]
                P.op(P.sp, lambda dst=dst, st=st: nc.sync.dma_start(out=dst, in_=st), reads=[stb], writes=[dstb], dma=True)
        elif u < 12:
            for ts in range(4):
                bank = bank_i % 4
                bank_i += 1
                for dc in range(DC):
                    P.op(P.pe, lambda bank=bank, ts=ts, dc=dc: nc.tensor.matmul(
                        ctx.ps[bank][:], lhsT=ctx.hT[:, dc, ts * 128:(ts + 1) * 128], rhs=wa[:, dc, :],
                        start=(dc == 0), stop=(dc == DC - 1)),
                        reads=[wab, ctx.hTb[dc]], writes=[ctx.psb[bank]], signal=(dc == DC - 1))
                st, stb = ctx.gbuf[bank // 2][:, bank % 2, :], ctx.gb[bank // 2][bank % 2]
                if ts % 2 == 0:
                    P.op(P.act, lambda bank=bank, st=st: nc.scalar.copy(out=st, in_=ctx.ps[bank][:]),
                         reads=[ctx.psb[bank]], writes=[stb])
                else:
                    P.op(P.dve, lambda bank=bank, st=st: nc.vector.tensor_copy(out=st, in_=ctx.ps[bank][:]),
                         reads=[ctx.psb[bank]], writes=[stb])
                r0 = t0 + ts * 128
                c0 = (u - 8) * 512
                dst = md.vown[r0:r0 + 128, c0:c0 + 512]
                P.op(P.sp, lambda dst=dst, st=st: nc.sync.dma_start(out=dst, in_=st), reads=[stb], writes=[md.b["vown"][(r0 // 128) * 4 + (u - 8)]], dma=True)
        else:
            c2 = u - 12
            for cc in range(2):
                ba = (bank_i % 2) * 2
                bank_i += 1
                bb = ba + 1
                for (bank, col) in ((ba, cc * 128), (bb, 256 + cc * 128)):
                    for dc in range(DC):
                        P.op(P.pe, lambda bank=bank, col=col, dc=dc: nc.tensor.matmul(
                            ctx.ps[bank][:], lhsT=wa[:, dc, col:col + 128], rhs=ctx.hT[:, dc, :],
                            start=(dc == 0), stop=(dc == DC - 1)),
                            reads=[wab, ctx.hTb[dc]], writes=[ctx.psb[bank]], signal=(dc == DC - 1))
                s = ba // 2
                P.op(P.act, lambda bb=bb, s=s: nc.scalar.activation(out=ctx.sig[s][:], in_=ctx.ps[bb][:], func=AF.Sigmoid),
                     reads=[ctx.psb[bb]], writes=[ctx.sigb[s]])
                P.op(P.dve, lambda ba=ba, s=s: nc.vector.tensor_tensor(out=ctx.sig[s][:], in0=ctx.ps[ba][:], in1=ctx.sig[s][:], op=ALU.mult),
                     reads=[ctx.psb[ba], ctx.sigb[s]], writes=[ctx.sigb[s]])
                row = (c2 * 2 + cc) * 128
                P.op(P.sp, lambda row=row, s=s: nc.sync.dma_start(out=md.u[row:row + 128, t0:t0 + TT], in_=ctx.sig[s][:]),
                     reads=[ctx.sigb[s]], writes=[md.b["u"][row // 128]], dma=True)
                if tt == 1:
                    P.op(P.sp, lambda row=row, s=s: nc.sync.dma_start(out=md.tin[row:row + 128, 2:32], in_=ctx.sig[s][:, TT - 30:TT]),
                         reads=[ctx.sigb[s]], writes=[md.b["tin"][row // 128]], dma=True)
        ctx.ringA.release()


def outproj_tile(ctx, md, tt):
    P = ctx.P
    nc = P.nc
    t0 = tt * TT
    yv = md.y.rearrange("(c p) t -> p c t", p=128)
    for q in range(4):
        P.op(P.sp, lambda q=q: nc.sync.dma_start(out=ctx.hT[:, q * 8:(q + 1) * 8, :], in_=yv[:, q * 8:(q + 1) * 8, t0:t0 + TT]),
             reads=md.b["y"][q * 8:(q + 1) * 8], writes=ctx.hTb[q * 8:(q + 1) * 8], dma=True)
    bank_i = 0
    for u in range(8):
        wa, wab = ctx.ringA.acquire()
        for cc in range(4):
            bank = bank_i % 4
            bank_i += 1
            oc = u * 4 + cc
            for dc in range(DC):
                P.op(P.pe, lambda bank=bank, cc=cc, dc=dc: nc.tensor.matmul(
                    ctx.ps[bank][:], lhsT=wa[:, dc, cc * 128:(cc + 1) * 128], rhs=ctx.hT[:, dc, :],
                    start=(dc == 0), stop=(dc == DC - 1)),
                    reads=[wab, ctx.hTb[dc]], writes=[ctx.psb[bank]], signal=(dc == DC - 1))
            P.op(P.dve, lambda bank=bank, oc=oc: nc.vector.tensor_tensor(
                out=ctx.xacc[:, oc, :], in0=ctx.ps[bank][:], in1=ctx.xacc[:, oc, :], op=ALU.add),
                reads=[ctx.psb[bank], ctx.xaccb[oc]], writes=[ctx.xaccb[oc]])
        ctx.ringA.release()


def exchange(ctx, md, groups):
    P = ctx.P
    nc = P.nc
    P.op(P.pool, lambda: nc.gpsimd.collective_compute(
        "AllGather", ALU.bypass, replica_groups=groups, ins=[md.tin], outs=[md.tout]),
        reads=md.b["tin"], writes=md.b["tout"], cc=True)
    for k in range(4):
        P.op(P.pool, lambda k=k: nc.gpsimd.collective_compute(
            "AllGather", ALU.bypass, replica_groups=groups, ins=[md.xin[k * 1024:(k + 1) * 1024, :]], outs=[md.xout[k]]),
            reads=md.b["kown"] + md.b["vown"], writes=md.b["xout"], cc=True)


class ConvConsts:
    NCOL = 16 * 31 + 16 * 5

    def __init__(self, P, name):
        self.t = P.sbuf(name, [128, self.NCOL], F32)
        self.b = Buf(name)

    def dwk(self, c, j):
        o = c * 31 + j
        return self.t[:, o:o + 1]

    def vec(self, which, c):
        o = 16 * 31 + which * 16 + c
        return self.t[:, o:o + 1]


def host_conv_consts(dw_kernel, dw_bias, ln_g, ln_b, conv_out_g, attn_out_g):
    out = np.zeros((128, ConvConsts.NCOL), np.float32)
    out[:, :16 * 31] = dw_kernel.reshape(31, 16, 128).transpose(2, 1, 0).reshape(128, 16 * 31)
    for w, v in enumerate((dw_bias, ln_g, ln_b, conv_out_g, attn_out_g)):
        out[:, 16 * 31 + w * 16:16 * 31 + (w + 1) * 16] = v.reshape(16, 128).T
    return out


CONV_PE_CH = (2, 5, 8, 10, 12, 14)


def conv_phase(ctx, md, cc_, flag, flagb, ident_d):
    P = ctx.P
    nc = P.nc
    NCH = 16
    PE_CH = CONV_PE_CH
    ucur = [P.sbuf("cv_u%d" % i, [128, 32 + TOK], F32) for i in range(2)]
    ucb = [Buf("cv_u%d" % i) for i in range(2)]
    ucurp = P.sbuf("cv_up", [128, 32 + TOK], F32)
    ucpb = Buf("cv_up")
    dg = P.sbuf("cv_dg", [128, CK, 128], F32)
    dgb = Buf("cv_dg")
    idt = P.sbuf("cv_id", [128, 128], F32)
    idtb = Buf("cv_id")
    P.op(P.sp, lambda: nc.sync.dma_start(out=idt[:], in_=ident_d), writes=[idtb], dma=True)
    cy = P.sbuf("cv_y", [128, NCH, TOK], F32)
    cyb = [Buf("cv_y%d" % i) for i in range(NCH)]
    sq0 = P.sbuf("cv_sq", [128, TOK], F32)
    sq = [sq0, sq0]
    sqb0 = Buf("cv_sq")
    sqb = [sqb0, sqb0]
    mu = P.sbuf("cv_mu", [128, TOK], F32)
    mub = Buf("cv_mu")
    rs = P.sbuf("cv_rs", [128, TOK], F32)
    rsb = Buf("cv_rs")
    yb0 = P.sbuf("cv_yb", [128, TOK], BF16)
    yb16 = [yb0, yb0]
    yb0b = Buf("cv_yb")
    yb16b = [yb0b, yb0b]
    ps, psb = ctx.ps, ctx.psb
    order = []
    dl = [c for c in range(NCH) if c not in PE_CH]
    pl = [c for c in range(NCH) if c in PE_CH]
    while dl or pl:
        for _ in range(2):
            if dl:
                order.append(dl.pop(0))
        if pl:
            order.append(pl.pop(0))
    nd_i = 0
    np_i = 0
    for c in order:
        if c in PE_CH:
            P.op(P.sp, lambda: nc.sync.dma_start(out=ucurp[:, 32:32 + TOK], in_=md.u[c * 128:(c + 1) * 128, :]),
                 reads=[md.b["u"][c]], writes=[ucpb], dma=True)
            P.op(P.sp, lambda: nc.sync.dma_start(out=ucurp[:, 0:32], in_=md.tprev[c * 128:(c + 1) * 128, :]),
                 reads=md.b["tout"], writes=[ucpb], dma=True)
            P.op(P.act, lambda: nc.scalar.mul(out=ucurp[:, 2:32], in_=ucurp[:, 2:32], mul=flag[:, 0:1]),
                 reads=[ucpb, flagb], writes=[ucpb])
            for j in range(CK):
                P.op(P.act, lambda: nc.scalar.mul(out=dg[:, j, :], in_=idt[:], mul=cc_.dwk(c, j)),
                     reads=[idtb, cc_.b], writes=[dgb])
            for hh in range(2):
                bank = 4 + (np_i % 2) * 2 + hh
                for j in range(CK):
                    P.op(P.pe, lambda: nc.tensor.matmul(ps[bank][:], lhsT=dg[:, j, :], rhs=ucurp[:, 2 + j + hh * 512:2 + j + (hh + 1) * 512],
                                                        start=(j == 0), stop=(j == CK - 1)),
                         reads=[dgb, ucpb], writes=[psb[bank]], signal=(j == CK - 1))
                P.op(P.act, lambda: nc.scalar.activation(out=cy[:, c, hh * 512:(hh + 1) * 512], in_=ps[bank][:], func=AF.Identity,
                                                         bias=cc_.vec(0, c), scale=1.0),
                     reads=[psb[bank], cc_.b], writes=[cyb[c]])
            np_i += 1
        else:
            s = nd_i % 2
            nd_i += 1
            P.op(P.sp, lambda: nc.sync.dma_start(out=ucur[s][:, 32:32 + TOK], in_=md.u[c * 128:(c + 1) * 128, :]),
                 reads=[md.b["u"][c]], writes=[ucb[s]], dma=True)
            P.op(P.sp, lambda: nc.sync.dma_start(out=ucur[s][:, 0:32], in_=md.tprev[c * 128:(c + 1) * 128, :]),
                 reads=md.b["tout"], writes=[ucb[s]], dma=True)
            P.op(P.dve, lambda: nc.vector.tensor_scalar(out=ucur[s][:, 2:32], in0=ucur[s][:, 2:32], scalar1=flag[:, 0:1], scalar2=None, op0=ALU.mult),
                 reads=[ucb[s], flagb], writes=[ucb[s]])
            P.op(P.dve, lambda: nc.vector.tensor_scalar(
                out=cy[:, c, :], in0=ucur[s][:, 2:2 + TOK], scalar1=cc_.dwk(c, 0), scalar2=cc_.vec(0, c), op0=ALU.mult, op1=ALU.add),
                reads=[ucb[s], cc_.b], writes=[cyb[c]])
            for j in range(1, CK):
                P.op(P.dve, lambda: nc.vector.scalar_tensor_tensor(
                    out=cy[:, c, :], in0=ucur[s][:, 2 + j:2 + j + TOK], scalar=cc_.dwk(c, j), in1=cy[:, c, :], op0=ALU.mult, op1=ALU.add),
                    reads=[ucb[s], cyb[c], cc_.b], writes=[cyb[c]])
    for oi, c in enumerate(order):
        s = oi % 2
        P.op(P.act, lambda: nc.scalar.activation(out=sq[s][:], in_=cy[:, c, :], func=AF.Square),
             reads=[cyb[c]], writes=[sqb[s]])
        for hh in range(2):
            P.op(P.pe, lambda: nc.tensor.matmul(ps[hh][:], lhsT=ctx.ones[:], rhs=cy[:, c, hh * 512:(hh + 1) * 512],
                                                start=(oi == 0), stop=(oi == NCH - 1)),
                 reads=[cyb[c], ctx.onesb], writes=[psb[hh]], signal=True)
            P.op(P.pe, lambda: nc.tensor.matmul(ps[2 + hh][:], lhsT=ctx.ones[:], rhs=sq[s][:, hh * 512:(hh + 1) * 512],
                                                start=(oi == 0), stop=(oi == NCH - 1)),
                 reads=[sqb[s], ctx.onesb], writes=[psb[2 + hh]], signal=True)
    for hh in range(2):
        sl = slice(hh * 512, (hh + 1) * 512)
        P.op(P.act, lambda hh=hh, sl=sl: nc.scalar.mul(out=mu[:, sl], in_=ps[hh][:], mul=1.0 / CW),
             reads=[psb[hh]], writes=[mub])
        P.op(P.dve, lambda hh=hh, sl=sl: nc.vector.tensor_tensor(out=rs[:, sl], in0=mu[:, sl], in1=mu[:, sl], op=ALU.mult),
             reads=[mub], writes=[rsb])
        P.op(P.dve, lambda hh=hh, sl=sl: nc.vector.scalar_tensor_tensor(
            out=rs[:, sl], in0=ps[2 + hh][:], scalar=1.0 / CW, in1=rs[:, sl], op0=ALU.mult, op1=ALU.subtract),
            reads=[psb[2 + hh], rsb], writes=[rsb])
    P.op(P.act, lambda: nc.scalar.activation(out=rs[:], in_=rs[:], func=AF.Sqrt, bias=LN_EPS, scale=1.0),
         reads=[rsb], writes=[rsb])
    P.op(P.dve, lambda: nc.vector.reciprocal(out=rs[:], in_=rs[:]), reads=[rsb], writes=[rsb])
    for c in range(NCH):
        s = c % 2
        P.op(P.dve, lambda c=c: nc.vector.tensor_tensor(out=cy[:, c, :], in0=cy[:, c, :], in1=mu[:], op=ALU.subtract),
             reads=[cyb[c], mub], writes=[cyb[c]])
        P.op(P.dve, lambda c=c: nc.vector.tensor_tensor(out=cy[:, c, :], in0=cy[:, c, :], in1=rs[:], op=ALU.mult),
             reads=[cyb[c], rsb], writes=[cyb[c]])
        P.op(P.act, lambda c=c: nc.scalar.activation(out=cy[:, c, :], in_=cy[:, c, :], func=AF.Silu,
                                                     scale=cc_.vec(1, c), bias=cc_.vec(2, c)),
             reads=[cyb[c], cc_.b], writes=[cyb[c]])
        P.op(P.act, lambda c=c, s=s: nc.scalar.activation(out=sq[s][:], in_=cy[:, c, :], func=AF.Square),
             reads=[cyb[c]], writes=[sqb[s]])
        for hh in range(2):
            P.op(P.pe, lambda c=c, hh=hh, s=s: nc.tensor.matmul(ps[4 + hh][:], lhsT=ctx.ones[:], rhs=sq[s][:, hh * 512:(hh + 1) * 512],
                                                               start=(c == 0), stop=(c == NCH - 1)),
                 reads=[sqb[s], ctx.onesb], writes=[psb[4 + hh]], signal=True)
    for hh in range(2):
        sl = slice(hh * 512, (hh + 1) * 512)
        P.op(P.act, lambda hh=hh, sl=sl: nc.scalar.activation(out=rs[:, sl], in_=ps[4 + hh][:], func=AF.Sqrt, scale=1.0 / CW, bias=RMS_EPS),
             reads=[psb[4 + hh]], writes=[rsb])
    P.op(P.dve, lambda: nc.vector.reciprocal(out=rs[:], in_=rs[:]), reads=[rsb], writes=[rsb])
    for c in range(NCH):
        s = c % 2
        P.op(P.dve, lambda c=c, s=s: nc.vector.scalar_tensor_tensor(
            out=yb16[s][:], in0=cy[:, c, :], scalar=cc_.vec(3, c), in1=rs[:], op0=ALU.mult, op1=ALU.mult),
            reads=[cyb[c], rsb, cc_.b], writes=[yb16b[s]])
        P.op(P.sp, lambda c=c, s=s: nc.sync.dma_start(out=md.y[AW + c * 128:AW + (c + 1) * 128, :], in_=yb16[s][:]),
             reads=[yb16b[s]], writes=[md.b["y"][16 + c]], dma=True)


def attn_phase(ctx, md, cc_, bias_d, st):
    P = ctx.P
    nc = P.nc
    ps, psb = ctx.ps, ctx.psb
    kT = [P.sbuf("at_k%d" % i, [128, 2 * TOK], BF16) for i in range(2)]
    qT = [P.sbuf("at_q%d" % i, [128, TOK], BF16) for i in range(2)]
    v1 = [P.sbuf("at_v1%d" % i, [128, 9, HD], BF16) for i in range(2)]
    v4 = [P.sbuf("at_v4%d" % i, [128, 4, 3, HD], BF16) for i in range(2)]
    v16 = [P.sbuf("at_v16%d" % i, [128, 16, HD], BF16) for i in range(2)]
    bia = [P.sbuf("at_b%d" % i, [128, BIASW], F32) for i in range(2)]
    hb = [Buf("at_h%d" % i) for i in range(2)]
    nd = P.sbuf("at_nd", [128, 2, TOK], F32)
    ndb = Buf("at_nd")
    at = [P.sbuf("at_at%d" % i, [128, TOK], F32) for i in range(2)]
    atb = [Buf("at_at%d" % i) for i in range(2)]
    sqh = P.sbuf("at_sq", [128, TOK], F32)
    sqhb = Buf("at_sq")
    sqacc = P.sbuf("at_sqacc", [128, TOK], F32)
    sqaccb = Buf("at_sqacc")
    rsa = P.sbuf("at_rs", [128, TOK], F32)
    rsab = Buf("at_rs")
    yb16 = [P.sbuf("at_yb%d" % i, [128, TOK], BF16) for i in range(2)]
    yb16b = [Buf("at_yb%d" % i) for i in range(2)]
    onesb16 = P.sbuf("at_ones", [128, 128], BF16)
    onesb16b = Buf("at_ones")
    P.op(P.dve, lambda: nc.vector.memset(onesb16[:], 1.0), writes=[onesb16b])

    vown = md.vown

    hbl = [[Buf("at_h%d_%d" % (i, j)) for j in range(16)] for i in range(2)]

    def load_head(h):
        s = h % 2
        hc = slice(h * HD, (h + 1) * HD)
        cnt = [0]

        def dm(out, in_, rd):
            w = [hbl[s][cnt[0]]]
            cnt[0] += 1
            P.op(P.sp, lambda: nc.sync.dma_start(out=out, in_=in_), reads=rd, writes=w, dma=True)
        XO = md.b["xout"]
        VO = [md.b["vown"][tb * 4 + h // 4] for tb in range(8)]
        dm(kT[s][:, 0:TOK], md.kprev(h), XO)
        dm(kT[s][:, TOK:2 * TOK], md.kown[hc, :], [md.b["kown"][h]])
        dm(qT[s][:], md.qT[hc, :], [md.b["qT"][h]])
        dm(bia[s][:], bias_d[h], [])
        dm(v1[s][:, 0:1, :], md.vprev_h[1].rearrange("(b j) c -> j b c", j=128)[:, 3:4, hc], XO)
        dm(v1[s][:, 1:9, :], vown.rearrange("(b j) c -> j b c", j=128)[:, :, hc], VO)
        vp4 = md.vprev_h[1].rearrange("(j r) c -> j r c", r=4)
        vo4 = vown.rearrange("(b j r) c -> j r b c", j=128, r=4)
        dm(v4[s][:, :, 0, :], vp4[:, :, hc], XO)
        for r in range(4):
            dm(v4[s][:, r, 1:3, :], vo4[:, r, :, hc], VO)
        for i in range(2):
            dm(v16[s][32 * i:32 * (i + 1), :, :], md.vprev_h[i].rearrange("(j r) c -> j r c", r=16)[:, :, hc], XO)
        dm(v16[s][64:128, :, :], vown.rearrange("(j r) c -> j r c", r=16)[:, :, hc], VO)
        assert cnt[0] <= 16

    last_of = {}
    NR = 4
    SB = [P.sbuf("at_sbr%d" % i, [128, 512], F32) for i in range(NR)]
    SBb = [Buf("at_sbr%d" % i) for i in range(NR)]
    PT = [P.sbuf("at_pr%d" % i, [128, 512], BF16) for i in range(NR)]
    PTb = [Buf("at_pr%d" % i) for i in range(NR)]
    nd2 = [nd, P.sbuf("at_nd2", [128, 2, TOK], F32)]
    nd2b = [ndb, Buf("at_nd2")]
    SBANK = [0, 1, 2, 3]
    OBANK = [4, 5, 6, 7]

    def front(k, u):
        (s, subs, bias_ap, n, dst_fn, first, h) = u
        i = k % NR
        pss, pssb = ps[SBANK[i]], psb[SBANK[i]]
        nk = len(subs[0][0])
        nmm = len(subs) * nk
        c = 0
        for (kaps, vaps, qap) in subs:
            for kap in kaps:
                P.op(P.pe, lambda: nc.tensor.matmul(pss[:, c * n:(c + 1) * n], lhsT=kap, rhs=qap, start=True, stop=True),
                     reads=hbl[s], writes=[pssb], signal=(c == nmm - 1))
                c += 1
        w = nmm * n
        P.op(P.dve, lambda: nc.vector.scalar_tensor_tensor(out=SB[i][:, 0:w], in0=pss[:, 0:w], scalar=QSCALE, in1=bias_ap,
                                                           op0=ALU.mult, op1=ALU.add),
             reads=[pssb] + hbl[s], writes=[SBb[i]])
        P.op(P.act, lambda: nc.scalar.activation(out=PT[i][:, 0:w], in_=SB[i][:, 0:w], func=AF.Exp),
             reads=[SBb[i]], writes=[PTb[i]])

    def back(k, u):
        (s, subs, bias_ap, n, dst_fn, first, h) = u
        i = k % NR
        pso, psob = ps[OBANK[i]], psb[OBANK[i]]
        nk = len(subs[0][0])
        ns = len(subs)
        for si, (kaps, vaps, qap) in enumerate(subs):
            for a, vap in enumerate(vaps):
                P.op(P.pe, lambda: nc.tensor.matmul(pso[:, si * 2 * n:si * 2 * n + n], lhsT=vap,
                                                    rhs=PT[i][:, (si * nk + a) * n:(si * nk + a + 1) * n],
                                                    start=(a == 0), stop=(a == nk - 1)),
                     reads=hbl[s] + [PTb[i]], writes=[psob], signal=False)
            for a in range(nk):
                P.op(P.pe, lambda: nc.tensor.matmul(pso[:, si * 2 * n + n:(si + 1) * 2 * n], lhsT=onesb16[:],
                                                    rhs=PT[i][:, (si * nk + a) * n:(si * nk + a + 1) * n],
                                                    start=(a == 0), stop=(a == nk - 1)),
                     reads=[onesb16b, PTb[i]], writes=[psob], signal=(a == nk - 1 and si == ns - 1))
        src = pso[:, 0:ns * 2 * n].rearrange("p (s two q) -> p two s q", s=ns, two=2)
        ndt, ndtb = nd2[h % 2], nd2b[h % 2]
        dst = dst_fn(ndt)
        if first:
            P.op(P.act, lambda: nc.scalar.copy(out=dst, in_=src), reads=[psob], writes=[ndtb])
        else:
            P.op(P.dve, lambda: nc.vector.tensor_tensor(out=dst, in0=src, in1=dst, op=ALU.add), reads=[psob, ndtb], writes=[ndtb])

    def finalize(h):
        a = h % 2
        ndt, ndtb = nd2[h % 2], nd2b[h % 2]
        P.op(P.dve, lambda: nc.vector.reciprocal(out=ndt[:, 1, :], in_=ndt[:, 1, :]), reads=[ndtb], writes=[ndtb])
        P.op(P.dve, lambda: nc.vector.tensor_tensor(out=at[a][:], in0=ndt[:, 0, :], in1=ndt[:, 1, :], op=ALU.mult),
             reads=[ndtb], writes=[atb[a]])
        if h == 0:
            P.op(P.act, lambda: nc.scalar.activation(out=sqacc[:], in_=at[a][:], func=AF.Square), reads=[atb[a]], writes=[sqaccb])
        else:
            P.op(P.act, lambda: nc.scalar.activation(out=sqh[:], in_=at[a][:], func=AF.Square), reads=[atb[a]], writes=[sqhb])
            P.op(P.pool, lambda: nc.gpsimd.tensor_tensor(out=sqacc[:], in0=sqacc[:], in1=sqh[:], op=ALU.add),
                 reads=[sqhb, sqaccb], writes=[sqaccb])
        P.op(P.sp, lambda: nc.sync.dma_start(out=md.attn[h * HD:(h + 1) * HD, :], in_=at[a][:]),
             reads=[atb[a]], writes=[md.b["attn"][h]], dma=True)

    def head_units(h):
        s = h % 2
        us = []
        for p in range(4):
            subs = []
            for qb in (2 * p, 2 * p + 1):
                subs.append(([kT[s][:, (7 + qb) * 128:(8 + qb) * 128], kT[s][:, (8 + qb) * 128:(9 + qb) * 128]],
                             [v1[s][:, qb, :], v1[s][:, qb + 1, :]], qT[s][:, qb * 128:(qb + 1) * 128]))
            boff = 0 if p == 0 else 256
            dst_fn = lambda t, p=p: t[:, :, p * 256:(p + 1) * 256].rearrange("p two (s q) -> p two s q", s=2)
            us.append((s, subs, bia[s][:, boff:boff + 512], 128, dst_fn, True, h))
        for r in range(4):
            subs = []
            for b in (2, 3):
                kp = kT[s][:, 512 * (b - 1) + r:512 * b:4]
                kc = kT[s][:, 512 * b + r:512 * (b + 1):4]
                subs.append(([kp, kc], [v4[s][:, r, b - 2, :], v4[s][:, r, b - 1, :]], qT[s][:, 512 * (b - 2) + r:512 * (b - 1):4]))
            dst_fn = lambda t, r=r: t[:, :, r:TOK:4].rearrange("p two (s q) -> p two s q", s=2)
            us.append((s, subs, bia[s][:, 768:768 + 512], 128, dst_fn, False, h))
        for rp in range(8):
            subs = []
            for r in (2 * rp, 2 * rp + 1):
                subs.append(([kT[s][:, r:2 * TOK:16]], [v16[s][:, r, :]], qT[s][:, r:TOK:16]))
            dst_fn = lambda t, rp=rp: t[:, :, :].rearrange("p two (q r) -> p two r q", r=16)[:, :, 2 * rp:2 * rp + 2, :]
            us.append((s, subs, bia[s][:, 1280:1280 + 128], 64, dst_fn, False, h))
        return us

    LAG = 3
    load_head(0)
    pend = []
    k = 0
    for h in range(NH):
        for ui, u in enumerate(head_units(h)):
            front(k, u)
            pend.append((k, u))
            k += 1
            if len(pend) > LAG:
                kk, uu = pend.pop(0)
                back(kk, uu)
                if uu is last_of.get(uu[6]):
                    finalize(uu[6])
            if ui == LAG + 1 and h + 1 < NH:
                load_head(h + 1)
        last_of[h] = u
    while pend:
        kk, uu = pend.pop(0)
        back(kk, uu)
        if uu is last_of.get(uu[6]):
            finalize(uu[6])
    for hh in range(2):
        P.op(P.pe, lambda: nc.tensor.matmul(ps[hh][:], lhsT=ctx.ones[:], rhs=sqacc[:, hh * 512:(hh + 1) * 512], start=True, stop=True),
             reads=[sqaccb, ctx.onesb], writes=[psb[hh]], signal=True)
    for hh in range(2):
        sl = slice(hh * 512, (hh + 1) * 512)
        P.op(P.act, lambda hh=hh, sl=sl: nc.scalar.activation(out=rsa[:, sl], in_=ps[hh][:], func=AF.Sqrt, scale=1.0 / AW, bias=RMS_EPS),
             reads=[psb[hh]], writes=[rsab])
    P.op(P.dve, lambda: nc.vector.reciprocal(out=rsa[:], in_=rsa[:]), reads=[rsab], writes=[rsab])
    for h in range(NH):
        a = h % 2
        P.op(P.sp, lambda a=a, h=h: nc.sync.dma_start(out=at[a][:], in_=md.attn[h * HD:(h + 1) * HD, :]),
             reads=[md.b["attn"][h]], writes=[atb[a]], dma=True)
        P.op(P.dve, lambda a=a, h=h: nc.vector.scalar_tensor_tensor(
            out=yb16[a][:], in0=at[a][:], scalar=cc_.vec(4, h), in1=rsa[:], op0=ALU.mult, op1=ALU.mult),
            reads=[atb[a], rsab, cc_.b], writes=[yb16b[a]])
        P.op(P.sp, lambda a=a, h=h: nc.sync.dma_start(out=md.y[h * HD:(h + 1) * HD, :], in_=yb16[a][:]),
             reads=[yb16b[a]], writes=[md.b["y"][h]], dma=True)

DEPTH = 2
D_FF = 11008
N_CORES = 8


def final_norm_tile(ctx, gcol, gcolb):
    P = ctx.P
    nc = P.nc
    pst, pstb = ctx.ps[4], ctx.psb[4]
    for dc in range(DC):
        s = dc % 2
        P.op(P.act, lambda: nc.scalar.activation(out=ctx.sq[s][:], in_=ctx.xacc[:, dc, :], func=AF.Square),
             reads=[ctx.xaccb[dc]], writes=[ctx.sqb[s]])
        P.op(P.pe, lambda: nc.tensor.matmul(pst[:], lhsT=ctx.ones[:], rhs=ctx.sq[s][:], start=(dc == 0), stop=(dc == DC - 1)),
             reads=[ctx.sqb[s], ctx.onesb], writes=[pstb], signal=True)
    P.op(P.act, lambda: nc.scalar.activation(out=ctx.rstd[:], in_=pst[:], func=AF.Sqrt, scale=1.0 / D, bias=RMS_EPS),
         reads=[pstb], writes=[ctx.rstdb])
    P.op(P.dve, lambda: nc.vector.reciprocal(out=ctx.rstd[:], in_=ctx.rstd[:]), reads=[ctx.rstdb], writes=[ctx.rstdb])
    for dc in range(DC):
        P.op(P.dve, lambda: nc.vector.scalar_tensor_tensor(
            out=ctx.xacc[:, dc, :], in0=ctx.xacc[:, dc, :], scalar=gcol[:, dc:dc + 1], in1=ctx.rstd[:],
            op0=ALU.mult, op1=ALU.mult),
            reads=[ctx.xaccb[dc], ctx.rstdb, gcolb], writes=[ctx.xaccb[dc]])


def build_program(F, n_cores):
    nc = bass.Bass("TRN2", target_bir_lowering=False)
    L = DEPTH
    ein = lambda name, shape: nc.dram_tensor(name, list(shape), F32, kind="ExternalInput").ap()
    xT = ein("xT", [D, TOK])
    w = {}
    for nm in ("ffn1_w1", "ffn1_w3", "ffn2_w1", "ffn2_w3"):
        w[nm] = ein(nm, [L, D, F])
    for nm in ("ffn1_w2", "ffn2_w2"):
        w[nm] = ein(nm, [L, F, D])
    w["w_in"] = ein("w_in", [L, D, 3 * AW + 2 * CW])
    w["w_out"] = ein("w_out", [L, D, D])
    gcd = ein("gcols", [128, 7 * DC])
    ccd = ein("convc", [L, 128, ConvConsts.NCOL])
    flagd = ein("flag", [128, 1])
    biasd = ein("biasd", [NH, 128, BIASW])
    identd = ein("ident", [128, 128])
    out = nc.dram_tensor("outT", [D, TOK], F32, kind="ExternalOutput").ap()
    xres = nc.dram_tensor("xres", [D, TOK], F32).ap()
    md = MixDram(nc)
    groups = [[2 * i, 2 * i + 1] for i in range(n_cores // 2)]
    with ExitStack() as st:
        P = Prog(nc, st)
        ctx = Ctx(P)
        gcol = P.sbuf("gcol", [128, 7 * DC], F32)
        gcolb = Buf("gcol")
        P.op(P.sp, lambda: nc.sync.dma_start(out=gcol[:], in_=gcd), writes=[gcolb], dma=True)
        flag = P.sbuf("flag", [128, 1], F32)
        flagb = Buf("flag")
        P.op(P.sp, lambda: nc.sync.dma_start(out=flag[:], in_=flagd), writes=[flagb], dma=True)
        g_of = lambda l, k: gcol[:, (l * 3 + k) * DC:(l * 3 + k + 1) * DC]
        g_fin = gcol[:, 6 * DC:7 * DC]
        for tt in range(2):
            plan_ffn(ctx, w["ffn1_w1"][0], w["ffn1_w3"][0], w["ffn1_w2"][0], F, ntiles=1)
            plan_inproj(ctx, w["w_in"][0])
        for tt in range(2):
            plan_outproj(ctx, w["w_out"][0])
            plan_ffn(ctx, w["ffn2_w1"][0], w["ffn2_w3"][0], w["ffn2_w2"][0], F, ntiles=1)
            plan_ffn(ctx, w["ffn1_w1"][1], w["ffn1_w3"][1], w["ffn1_w2"][1], F, ntiles=1)
            plan_inproj(ctx, w["w_in"][1])
        for tt in range(2):
            plan_outproj(ctx, w["w_out"][1])
            plan_ffn(ctx, w["ffn2_w1"][1], w["ffn2_w3"][1], w["ffn2_w2"][1], F, ntiles=1)
        ctx.ringA.start()
        ctx.ringB.start()
        xin_v = xT.rearrange("(c p) t -> p c t", p=128)
        xres_v = xres.rearrange("(c p) t -> p c t", p=128)
        out_v = out.rearrange("(c p) t -> p c t", p=128)
        xinb = [Buf("xin0"), Buf("xin1")]
        xresb = [Buf("xres0"), Buf("xres1")]
        outb = [Buf("out0"), Buf("out1")]

        def mixer_core(l):
            exchange(ctx, md, groups)
            P.push()
            cc_ = ConvConsts(P, "ccA%d" % l)
            P.op(P.sp, lambda: nc.sync.dma_start(out=cc_.t[:], in_=ccd[l]), writes=[cc_.b], dma=True)
            conv_phase(ctx, md, cc_, flag, flagb, identd)
            P.pop()
            P.push()
            cc2 = ConvConsts(P, "ccB%d" % l)
            P.op(P.sp, lambda: nc.sync.dma_start(out=cc2.t[:], in_=ccd[l]), writes=[cc2.b], dma=True)
            attn_phase(ctx, md, cc2, biasd, None)
            P.pop()

        P.push()
        ctx.alloc_dense()
        load_x(ctx, xin_v, xinb, 0)
        for tt in range(2):
            ffn_tile(ctx, F, g_of(0, 0), gcolb)

            def _swap(tt=tt):
                store_x(ctx, xres_v, xresb, tt)
                if tt == 0:
                    load_x(ctx, xin_v, xinb, 1)
            inproj_tile(ctx, md, tt, g_of(0, 1), gcolb, _swap)
        P.pop()
        mixer_core(0)
        P.push()
        ctx.alloc_dense()
        load_x(ctx, xres_v, xresb, 0)
        for tt in range(2):
            outproj_tile(ctx, md, tt)
            ffn_tile(ctx, F, g_of(0, 2), gcolb)
            ffn_tile(ctx, F, g_of(1, 0), gcolb)

            def _swap(tt=tt):
                store_x(ctx, xres_v, xresb, tt)
                if tt == 0:
                    load_x(ctx, xres_v, xresb, 1)
            inproj_tile(ctx, md, tt, g_of(1, 1), gcolb, _swap)
        P.pop()
        mixer_core(1)
        P.push()
        ctx.alloc_dense()
        for tt in range(2):
            load_x(ctx, xres_v, xresb, tt)
            outproj_tile(ctx, md, tt)
            ffn_tile(ctx, F, g_of(1, 2), gcolb)
            final_norm_tile(ctx, g_fin, gcolb)
            store_x(ctx, out_v, outb, tt)
        P.finish(outb)
        P.pop()
        assert ctx.ringA.consumed == len(ctx.ringA.units) and ctx.ringB.consumed == len(ctx.ringB.units)
        build_program.stats = (P.ninstr, P.nwaits, P.nsem)
    return nc


def make_in_maps(inputs, n_cores):
    f = lambda k: np.ascontiguousarray(np.asarray(inputs[k], dtype=np.float32))
    x = f("x")
    col = lambda v: np.ascontiguousarray(np.asarray(v, np.float32).reshape(DC, 128).T)
    gl = []
    for l in range(DEPTH):
        gl += [col(inputs["ffn1_norm_g"][l]), col(inputs["mix_norm_g"][l]), col(inputs["ffn2_norm_g"][l])]
    gl.append(col(inputs["final_norm_g"]))
    gcols = np.ascontiguousarray(np.concatenate(gl, axis=1))
    convc = np.stack([host_conv_consts(np.asarray(inputs["dw_kernel"][l]), np.asarray(inputs["dw_bias"][l]),
                                       np.asarray(inputs["conv_ln_g"][l]), np.asarray(inputs["conv_ln_b"][l]),
                                       np.asarray(inputs["conv_out_g"][l]), np.asarray(inputs["attn_out_g"][l]))
                      for l in range(DEPTH)])
    rel = np.asarray(inputs["rel_bias_table"], np.float32)
    bias_h = [build_bias(rel, 0), build_bias(rel, 1)]
    shared = {k: f(k) for k in ("ffn1_w1", "ffn1_w3", "ffn1_w2", "ffn2_w1", "ffn2_w3", "ffn2_w2", "w_in", "w_out")}
    maps = []
    for c in range(n_cores):
        b, half = c // 2, c % 2
        m = dict(shared)
        m["xT"] = np.ascontiguousarray(x[b, half * TOK:(half + 1) * TOK, :].T)
        m["gcols"] = gcols
        m["convc"] = convc
        m["flag"] = np.full((128, 1), float(half), np.float32)
        m["biasd"] = bias_h[half]
        m["ident"] = np.eye(128, dtype=np.float32)
        maps.append(m)
    return maps


def kernel(**inputs):
    x = np.asarray(inputs["x"])
    B, S, _ = x.shape
    n_cores = B * 2
    F = np.asarray(inputs["ffn1_w1"]).shape[-1]
    nc = build_program(F, n_cores)
    maps = make_in_maps(inputs, n_cores)
    res = run_bass_kernel_spmd(nc, maps, core_ids=list(range(n_cores)))
    out = np.empty((B, S, D), np.float32)
    for c in range(n_cores):
        b, half = c // 2, c % 2
        out[b, half * TOK:(half + 1) * TOK, :] = res.results[c]["outT"].T
    return out
```

```python
from contextlib import ExitStack
from concourse.bass_utils import run_bass_kernel_spmd

import numpy as np
import concourse.bass as bass
import concourse.mybir as mybir

F32 = mybir.dt.float32
BF16 = mybir.dt.bfloat16
AF = mybir.ActivationFunctionType
ALU = mybir.AluOpType
AX = mybir.AxisListType


class Buf:
    __slots__ = ("name", "writer", "readers", "dreaders")

    def __init__(self, name):
        self.name = name
        self.writer = None
        self.readers = {}
        self.dreaders = []


class Node:
    __slots__ = ("eng", "idx", "sem", "val", "is_dma")

    def __init__(self, eng, idx):
        self.eng = eng
        self.idx = idx
        self.sem = None
        self.val = None
        self.is_dma = False


class Eng:
    def __init__(self, P, name, handle):
        self.P = P
        self.name = name
        self.h = handle
        self.sem = P.newsem("p_" + name)
        self.count = 0
        self.n = 0
        self.waited = {}
        self.nodes = []
        self.pending = []
        self.dring = None
        self.dcount = 0

    def wait(self, sem, val):
        key = id(sem)
        if self.waited.get(key, -1) >= val:
            return
        self.waited[key] = val
        self.h.wait_ge(sem, val)
        self.P.nwaits += 1


class Prog:
    def __init__(self, nc, stack):
        self.nc = nc
        self.stack = stack
        self.nsem = 0
        self.nalloc = 0
        self.stacks = []
        self.ccsems = []
        self.nwaits = 0
        self.ninstr = 0
        self.pe = Eng(self, "pe", nc.tensor)
        self.act = Eng(self, "act", nc.scalar)
        self.dve = Eng(self, "dve", nc.vector)
        self.pool = Eng(self, "pool", nc.gpsimd)
        self.sp = Eng(self, "sp", nc.sync)
        self.engs = [self.pe, self.act, self.dve, self.pool, self.sp]
        for e in (self.sp, self.pool, self.act):
            e.dring = [[self.newsem("d_%s%d" % (e.name, i)), 0] for i in range(12)]

    def newsem(self, name):
        self.nsem += 1
        return self.stack.enter_context(self.nc.semaphore(name))

    def sbuf(self, name, shape, dtype):
        self.nalloc += 1
        st = self.stacks[-1] if self.stacks else self.stack
        return st.enter_context(self.nc.sbuf_tensor("%s_%d" % (name, self.nalloc), list(shape), dtype))

    def push(self):
        from contextlib import ExitStack
        self.stacks.append(ExitStack())

    def pop(self):
        self.barrier()
        self.stacks.pop().close()

    def barrier(self):
        for e in self.engs:
            if e.pending:
                raise RuntimeError("barrier with pending non-signalled nodes on %s" % e.name)
        for e in self.engs:
            for e2 in self.engs:
                if e2 is not e and e2.count > 0:
                    e.wait(e2.sem, e2.count)
                if e2.dring:
                    for slot in e2.dring:
                        if slot[1] > 0:
                            e.wait(slot[0], slot[1])
            for sem in self.ccsems:
                e.wait(sem, 1)

    def psum(self, name, shape, dtype):
        return self.stack.enter_context(self.nc.psum_tensor(name, list(shape), dtype))

    def _dep(self, eng, node, raw):
        if node is None:
            return
        if node.is_dma:
            eng.wait(node.sem, node.val)
            return
        if node.eng is eng:
            if eng is self.pe:
                return
            if not raw:
                return
            if node.idx < eng.n - 2:
                return
        if node.val is None:
            raise RuntimeError("dependency on non-signalled node of %s" % node.eng.name)
        eng.wait(node.eng.sem, node.val)

    def op(self, eng, fn, reads=(), writes=(), signal=True, dma=False, cc=False):
        for b in reads:
            self._dep(eng, b.writer, True)
        for b in writes:
            self._dep(eng, b.writer, False)
            for r in b.readers.values():
                self._dep(eng, r, False)
            for r in b.dreaders:
                self._dep(eng, r, False)
        node = Node(eng, eng.n)
        if cc:
            node.is_dma = True
            sem = self.newsem("cc%d" % len(self.ccsems))
            self.ccsems.append(sem)
            ins = fn()
            ins.then_inc(sem, 1)
            node.sem, node.val = sem, 1
            dma = True
        elif dma:
            node.is_dma = True
            slot = eng.dring[eng.dcount % len(eng.dring)]
            eng.dcount += 1
            if slot[1] > 0:
                eng.wait(slot[0], slot[1])
            ins = fn()
            slot[1] += 16
            ins.then_inc(slot[0], 16)
            node.sem, node.val = slot[0], slot[1]
        else:
            ins = fn()
            if signal:
                eng.count += 1
                ins.then_inc(eng.sem, 1)
                node.val = eng.count
                for pn in eng.pending:
                    pn.val = eng.count
                eng.pending = []
            else:
                eng.pending.append(node)
        eng.n += 1
        self.ninstr += 1
        for b in reads:
            if dma:
                b.dreaders.append(node)
            else:
                b.readers[eng.name] = node
        for b in writes:
            b.writer = node
            b.readers = {}
            b.dreaders = []
        return node

    def finish(self, bufs):
        for b in bufs:
            self._dep(self.sp, b.writer, True)


D = 4096
DC = D // 128
TT = 512
RMS_EPS = 1e-6


class Ring:
    def __init__(self, P, name, nslots, shape):
        self.P = P
        self.nslots = nslots
        self.tiles = [P.sbuf("%s%d" % (name, i), shape, BF16) for i in range(nslots)]
        self.bufs = [Buf("%s%d" % (name, i)) for i in range(nslots)]
        self.units = []
        self.issued = 0
        self.consumed = 0

    def plan(self, parts):
        self.units.append(parts)

    def _issue(self):
        if self.issued >= len(self.units):
            return
        k = self.issued
        s = k % self.nslots
        tile = self.tiles[s]
        for (dst_fn, src) in self.units[k]:
            self.P.op(self.P.pool, lambda d=dst_fn(tile), s_=src: self.P.nc.gpsimd.dma_start(out=d, in_=s_),
                      writes=[self.bufs[s]], dma=True)
        self.issued += 1

    def start(self):
        while self.issued < min(self.nslots, len(self.units)):
            self._issue()

    def acquire(self):
        k = self.consumed
        assert k < self.issued, "ring unit not issued"
        s = k % self.nslots
        return self.tiles[s], self.bufs[s]

    def release(self):
        self.consumed += 1
        self._issue()


class Ctx:
    def __init__(self, P):
        nc = P.nc
        self.P = P
        self.ringA = Ring(P, "ringA", 2, [128, DC, 512])
        self.ringB = Ring(P, "ringB", 2, [128, 2, D])
        self.ps = [P.psum("ps%d" % i, [128, 512], F32) for i in range(8)]
        self.psb = [Buf("ps%d" % i) for i in range(8)]
        self.ones = P.sbuf("ones", [128, 128], F32)
        self.onesb = Buf("ones")
        P.op(P.dve, lambda: nc.vector.memset(self.ones[:], 1.0), writes=[self.onesb])

    def alloc_dense(self):
        P = self.P
        self.xacc = P.sbuf("xacc", [128, DC, TT], F32)
        self.xaccb = [Buf("xacc%d" % i) for i in range(DC)]
        self.hT = P.sbuf("hT", [128, DC, TT], BF16)
        self.hTb = [Buf("hT%d" % i) for i in range(DC)]
        self.gbuf = [P.sbuf("g%d" % i, [128, 2, TT], BF16) for i in range(2)]
        self.gb = [[Buf("g%d_%d" % (i, j)) for j in range(2)] for i in range(2)]
        self.sig = [P.sbuf("sig%d" % i, [128, TT], F32) for i in range(2)]
        self.sigb = [Buf("sig%d" % i) for i in range(2)]
        self.sq = self.sig
        self.sqb = self.sigb
        self.rstd = P.sbuf("rstd", [128, TT], F32)
        self.rstdb = Buf("rstd")


def plan_ffn(ctx, w1, w3, w2, F, ntiles=2):
    NG = F // 256
    w1v = w1.rearrange("(c p) f -> p c f", p=128)
    w3v = w3.rearrange("(c p) f -> p c f", p=128)
    w2v = w2.rearrange("(c p) d -> p c d", p=128)
    for t in range(ntiles):
        for g in range(NG):
            ctx.ringA.plan([
                (lambda tl: tl[:, :, 0:256], w1v[:, :, g * 256:(g + 1) * 256]),
                (lambda tl: tl[:, :, 256:512], w3v[:, :, g * 256:(g + 1) * 256]),
            ])
            ctx.ringB.plan([
                (lambda tl: tl[:, :, :], w2v[:, 2 * g:2 * g + 2, :]),
            ])


def rmsnorm_tile(ctx, gcol, gcolb):
    P = ctx.P
    nc = P.nc
    pst, pstb = ctx.ps[4], ctx.psb[4]
    for dc in range(DC):
        s = dc % 2
        P.op(P.act, lambda dc=dc, s=s: nc.scalar.activation(out=ctx.sq[s][:], in_=ctx.xacc[:, dc, :], func=AF.Square),
             reads=[ctx.xaccb[dc]], writes=[ctx.sqb[s]])
        P.op(P.pe, lambda dc=dc, s=s: nc.tensor.matmul(pst[:], lhsT=ctx.ones[:], rhs=ctx.sq[s][:], start=(dc == 0), stop=(dc == DC - 1)),
             reads=[ctx.sqb[s], ctx.onesb], writes=[pstb], signal=True)
    P.op(P.act, lambda: nc.scalar.activation(out=ctx.rstd[:], in_=pst[:], func=AF.Sqrt, scale=1.0 / D, bias=RMS_EPS),
         reads=[pstb], writes=[ctx.rstdb])
    P.op(P.dve, lambda: nc.vector.reciprocal(out=ctx.rstd[:], in_=ctx.rstd[:]), reads=[ctx.rstdb], writes=[ctx.rstdb])
    for dc in range(DC):
        P.op(P.dve, lambda dc=dc: nc.vector.scalar_tensor_tensor(
            out=ctx.hT[:, dc, :], in0=ctx.xacc[:, dc, :], scalar=gcol[:, dc:dc + 1], in1=ctx.rstd[:],
            op0=ALU.mult, op1=ALU.mult),
            reads=[ctx.xaccb[dc], ctx.rstdb, gcolb], writes=[ctx.hTb[dc]])


def load_x(ctx, xsrc_v, xsrcb, tt):
    P = ctx.P
    nc = P.nc
    for q in range(4):
        P.op(P.sp, lambda q=q: nc.sync.dma_start(out=ctx.xacc[:, q * 8:(q + 1) * 8, :], in_=xsrc_v[:, q * 8:(q + 1) * 8, tt * TT:(tt + 1) * TT]),
             reads=[xsrcb[tt]], writes=ctx.xaccb[q * 8:(q + 1) * 8], dma=True)


def store_x(ctx, xdst_v, xdstb, tt):
    P = ctx.P
    nc = P.nc
    for q in range(4):
        P.op(P.sp, lambda q=q: nc.sync.dma_start(out=xdst_v[:, q * 8:(q + 1) * 8, tt * TT:(tt + 1) * TT], in_=ctx.xacc[:, q * 8:(q + 1) * 8, :]),
             reads=ctx.xaccb[q * 8:(q + 1) * 8], writes=[xdstb[tt]], dma=True)


def ffn_tile(ctx, F, gcol, gcolb):
    P = ctx.P
    nc = P.nc
    NG = F // 256
    rmsnorm_tile(ctx, gcol, gcolb)
    A = [None, None]
    for g in range(NG + 1):
        if g < NG:
            wa, wab = ctx.ringA.acquire()
        if g >= 1:
            wb, wbb = ctx.ringB.acquire()
        for i in range(32):
            if g < NG:
                for j in range(4):
                    m = i * 4 + j
                    fc, r = divmod(m, 64)
                    w, dc = divmod(r, 32)
                    bank = fc * 2 + w
                    col = w * 256 + fc * 128
                    P.op(P.pe, lambda bank=bank, col=col, dc=dc: nc.tensor.matmul(
                        ctx.ps[bank][:], lhsT=wa[:, dc, col:col + 128], rhs=ctx.hT[:, dc, :],
                        start=(dc == 0), stop=(dc == DC - 1)),
                        reads=[wab, ctx.hTb[dc]], writes=[ctx.psb[bank]], signal=(dc == DC - 1))
                    if r == 63:
                        s = fc
                        P.op(P.act, lambda fc=fc, s=s: nc.scalar.activation(out=ctx.sig[s][:], in_=ctx.ps[fc * 2][:], func=AF.Silu),
                             reads=[ctx.psb[fc * 2]], writes=[ctx.sigb[s]])
                        P.op(P.dve, lambda fc=fc, s=s, g=g: nc.vector.tensor_tensor(
                            out=ctx.gbuf[g % 2][:, fc, :], in0=ctx.ps[fc * 2 + 1][:], in1=ctx.sig[s][:], op=ALU.mult),
                            reads=[ctx.psb[fc * 2 + 1], ctx.sigb[s]], writes=[ctx.gb[g % 2][fc]])
            if g >= 1:
                gp = (g - 1) % 2
                bank = 4 + (i % 4)
                for fc in range(2):
                    P.op(P.pe, lambda bank=bank, fc=fc, i=i, gp=gp: nc.tensor.matmul(
                        ctx.ps[bank][:], lhsT=wb[:, fc, i * 128:(i + 1) * 128], rhs=ctx.gbuf[gp][:, fc, :],
                        start=(fc == 0), stop=(fc == 1)),
                        reads=[wbb, ctx.gb[gp][fc]], writes=[ctx.psb[bank]], signal=(fc == 1))
                P.op(P.dve, lambda bank=bank, i=i: nc.vector.scalar_tensor_tensor(
                    out=ctx.xacc[:, i, :], in0=ctx.ps[bank][:], scalar=0.5, in1=ctx.xacc[:, i, :],
                    op0=ALU.mult, op1=ALU.add),
                    reads=[ctx.psb[bank], ctx.xaccb[i]], writes=[ctx.xaccb[i]])
        if g < NG:
            ctx.ringA.release()
        if g >= 1:
            ctx.ringB.release()


HD = 128
NH = 16
AW = 2048
CW = 2048
CK = 31
TOK = 1024
QSCALE = HD ** -0.5
LN_EPS = 1e-5
NEG = -30000.0
BIASW = 768 + 512 + 128


def rel_bucket_np(dist):
    max_exact = 16
    df = np.maximum(dist, 1).astype(np.float32)
    large = max_exact + (np.log(df / max_exact) / np.log(2048 / max_exact) * (32 - max_exact)).astype(np.int32)
    large = np.minimum(large, 31)
    return np.where(dist < max_exact, dist, large).astype(np.int32)


def build_bias(rel_table, half):
    out = np.full((NH, 128, BIASW), NEG, np.float32)
    j = np.arange(128)[:, None]
    i = np.arange(128)[None, :]
    for pi, d in enumerate((1, 4)):
        step_c = i - j
        val_c = step_c >= 0
        b_c = rel_bucket_np(np.clip(step_c, 0, None) * d)
        step_p = i + 128 - j
        val_p = step_p <= 128
        b_p = rel_bucket_np(np.clip(step_p, 0, None) * d)
        base = 0 if d == 1 else 768
        for h in range(NH):
            cur = np.where(val_c, rel_table[b_c, h], NEG).astype(np.float32)
            prv = np.where(val_p, rel_table[b_p, h], NEG).astype(np.float32)
            if half == 1:
                out[h, :, base:base + 128] = prv
            out[h, :, base + 128:base + 256] = cur
            out[h, :, base + 256:base + 384] = prv
            out[h, :, base + 384:base + 512] = cur
            if d == 1:
                out[h, :, base + 512:base + 640] = prv
                out[h, :, base + 640:base + 768] = cur
    i2 = np.arange(64)[None, :]
    step = 64 + i2 - j
    val = step >= 0
    if half == 0:
        val = val & (j >= 64)
    b16 = rel_bucket_np(np.clip(step, 0, None) * 16)
    for h in range(NH):
        t16 = np.where(val, rel_table[b16, h], NEG).astype(np.float32)
        out[h, :, 1280:1344] = t16
        out[h, :, 1344:1408] = t16
    return out


def plan_inproj(ctx, w_in):
    wv = w_in.rearrange("(c p) f -> p c f", p=128)
    for c2 in range(8):
        ctx.ringA.plan([
            (lambda tl: tl[:, :, 0:256], wv[:, :, 3 * AW + c2 * 256:3 * AW + (c2 + 1) * 256]),
            (lambda tl: tl[:, :, 256:512], wv[:, :, 3 * AW + CW + c2 * 256:3 * AW + CW + (c2 + 1) * 256]),
        ])
    for u in range(12):
        ctx.ringA.plan([(lambda tl: tl[:, :, :], wv[:, :, u * 512:(u + 1) * 512])])


def plan_outproj(ctx, w_out):
    wv = w_out.rearrange("(c p) f -> p c f", p=128)
    for u in range(8):
        ctx.ringA.plan([(lambda tl: tl[:, :, :], wv[:, :, u * 512:(u + 1) * 512])])


class MixDram:
    def __init__(self, nc):
        self.qT = nc.dram_tensor("mx_qT", [AW, TOK], BF16).ap()
        self.xin = nc.dram_tensor("mx_xin", [2 * AW, TOK], BF16).ap()
        self.xout = nc.dram_tensor("mx_xout", [4, 2 * 1024, TOK], BF16).ap()
        self.u = nc.dram_tensor("mx_u", [CW, TOK], F32).ap()
        self.tin = nc.dram_tensor("mx_tin", [CW, 32], F32).ap()
        self.tout = nc.dram_tensor("mx_tout", [2 * CW, 32], F32).ap()
        self.attn = nc.dram_tensor("mx_attn", [AW, TOK], F32).ap()
        self.convy = nc.dram_tensor("mx_convy", [CW, TT], F32).ap()
        self.y = nc.dram_tensor("mx_y", [D, TOK], BF16).ap()
        mk = lambda k, n: [Buf("mxb_%s%d" % (k, i)) for i in range(n)]
        self.b = {"qT": mk("qT", 16), "kown": mk("kown", 16), "vown": mk("vown", 32), "xout": mk("xout", 1),
                  "u": mk("u", 16), "tin": mk("tin", 16), "tout": mk("tout", 1), "attn": mk("attn", 16), "y": mk("y", 32), "convy": mk("convy", 16)}
        self.kown = self.xin[0:AW, :]
        self.vown = self.xin[AW:2 * AW, :].rearrange("(t two) c -> t (two c)", two=2)
        self.vprev_h = [self.xout[2 + i, 0:1024, :].rearrange("(t two) c -> t (two c)", two=2) for i in range(2)]
        self.tprev = self.tout[0:CW, :]

    def kprev(self, h):
        return self.xout[h // 8, (h % 8) * 128:(h % 8 + 1) * 128, :]


def _ec_bufs(ctx, k):
    s = k % 2
    U = ctx.xacc[:, 2 * s:2 * s + 2, :].rearrange("p a t -> p (a t)")
    return U, ctx.xaccb[2 * s:2 * s + 2], ctx.xacc[:, 4 + s, :], [ctx.xaccb[4 + s]]


def early_conv_load(ctx, md, c):
    P = ctx.P
    nc = P.nc
    U, Ub, C, Cb = _ec_bufs(ctx, c)
    P.op(P.sp, lambda: nc.sync.dma_start(out=U[:, 0:32 + TT], in_=md.u[c * 128:(c + 1) * 128, TT - 32:2 * TT]),
         reads=[md.b["u"][c]], writes=Ub, dma=True)


def early_conv_store(ctx, md, c):
    P = ctx.P
    nc = P.nc
    U, Ub, C, Cb = _ec_bufs(ctx, c)
    P.op(P.sp, lambda: nc.sync.dma_start(out=md.convy[c * 128:(c + 1) * 128, :], in_=C),
         reads=Cb, writes=[md.b["convy"][c]], dma=True)


def early_conv_chunk(ctx, md, cc_, c, k):
    P = ctx.P
    nc = P.nc
    U, Ub, C, Cb = _ec_bufs(ctx, c)
    if c == 0:
        early_conv_load(ctx, md, 0)
    if c + 1 < 16:
        early_conv_load(ctx, md, c + 1)
    P.op(P.dve, lambda: nc.vector.tensor_scalar(out=C, in0=U[:, 2:2 + TT], scalar1=cc_.dwk(c, 0), scalar2=cc_.vec(0, c),
                                                op0=ALU.mult, op1=ALU.add),
         reads=Ub + [cc_.b], writes=Cb)
    for j in range(1, CK):
        P.op(P.dve, lambda: nc.vector.scalar_tensor_tensor(out=C, in0=U[:, 2 + j:2 + j + TT], scalar=cc_.dwk(c, j), in1=C,
                                                           op0=ALU.mult, op1=ALU.add),
             reads=Ub + Cb + [cc_.b], writes=Cb)
    if c >= 1:
        early_conv_store(ctx, md, c - 1)
    if c == 15:
        early_conv_store(ctx, md, 15)


def inproj_tile(ctx, md, tt, gcol, gcolb, after_norm=None, early_cc=None):
    P = ctx.P
    nc = P.nc
    rmsnorm_tile(ctx, gcol, gcolb)
    if after_norm is not None:
        after_norm()
    t0 = tt * TT
    bank_i = 0
    early = (early_cc is not None and tt == 1)
    ec = 0
    for ui, u in enumerate(list(range(12, 20)) + list(range(12))):
        if early and ui >= 8:
            while ec < 16 and ec * 12 < (ui - 8 + 1) * 16:
                early_conv_chunk(ctx, md, early_cc, ec, ec)
                ec += 1
        wa, wab = ctx.ringA.acquire()
        if u < 8:
            for cc in range(4):
                bank = bank_i % 4
                bank_i += 1
                for dc in range(DC):
                    P.op(P.pe, lambda bank=bank, cc=cc, dc=dc: nc.tensor.matmul(
                        ctx.ps[bank][:], lhsT=wa[:, dc, cc * 128:(cc + 1) * 128], rhs=ctx.hT[:, dc, :],
                        start=(dc == 0), stop=(dc == DC - 1)),
                        reads=[wab, ctx.hTb[dc]], writes=[ctx.psb[bank]], signal=(dc == DC - 1))
                st, stb = ctx.gbuf[bank // 2][:, bank % 2, :], ctx.gb[bank // 2][bank % 2]
                if cc % 2 == 0 or early:
                    P.op(P.act, lambda bank=bank, st=st: nc.scalar.copy(out=st, in_=ctx.ps[bank][:]),
                         reads=[ctx.psb[bank]], writes=[stb])
                else:
                    P.op(P.dve, lambda bank=bank, st=st: nc.vector.tensor_copy(out=st, in_=ctx.ps[bank][:]),
                         reads=[ctx.psb[bank]], writes=[stb])
                row = (u % 4) * 512 + cc * 128
                if u < 4:
                    dst, dstb = md.qT[row:row + 128, t0:t0 + TT], md.b["qT"][row // 128]
                else:
                    dst, dstb = md.kown[row:row + 128, t0:t0 + TT], md.b["kown"][row // 128]
                P.op(P.sp, lambda dst=dst, st=st: nc.sync.dma_start(out=dst, in_=st), reads=[stb], writes=[dstb], dma=True)
        elif u < 12:
            for ts in range(4):
                bank = bank_i % 4
                bank_i += 1
                for dc in range(DC):
                    P.op(P.pe, lambda bank=bank, ts=ts, dc=dc: nc.tensor.matmul(
                        ctx.ps[bank][:], lhsT=ctx.hT[:, dc, ts * 128:(ts + 1) * 128], rhs=wa[:, dc, :],
                        start=(dc == 0), stop=(dc == DC - 1)),
                        reads=[wab, ctx.hTb[dc]], writes=[ctx.psb[bank]], signal=(dc == DC - 1))
                st, stb = ctx.gbuf[bank // 2][:, bank % 2, :], ctx.gb[bank // 2][bank % 2]
                if ts % 2 == 0 or early:
                    P.op(P.act, lambda bank=bank, st=st: nc.scalar.copy(out=st, in_=ctx.ps[bank][:]),
                         reads=[ctx.psb[bank]], writes=[stb])
                else:
                    P.op(P.dve, lambda bank=bank, st=st: nc.vector.tensor_copy(out=st, in_=ctx.ps[bank][:]),
                         reads=[ctx.psb[bank]], writes=[stb])
                r0 = t0 + ts * 128
                c0 = (u - 8) * 512
                dst = md.vown[r0:r0 + 128, c0:c0 + 512]
                P.op(P.sp, lambda dst=dst, st=st: nc.sync.dma_start(out=dst, in_=st), reads=[stb], writes=[md.b["vown"][(r0 // 128) * 4 + (u - 8)]], dma=True)
        else:
            c2 = u - 12
            for cc in range(2):
                ba = (bank_i % 2) * 2
                bank_i += 1
                bb = ba + 1
                for (bank, col) in ((ba, cc * 128), (bb, 256 + cc * 128)):
                    for dc in range(DC):
                        P.op(P.pe, lambda bank=bank, col=col, dc=dc: nc.tensor.matmul(
                            ctx.ps[bank][:], lhsT=wa[:, dc, col:col + 128], rhs=ctx.hT[:, dc, :],
                            start=(dc == 0), stop=(dc == DC - 1)),
                            reads=[wab, ctx.hTb[dc]], writes=[ctx.psb[bank]], signal=(dc == DC - 1))
                s = ba // 2
                P.op(P.act, lambda bb=bb, s=s: nc.scalar.activation(out=ctx.sig[s][:], in_=ctx.ps[bb][:], func=AF.Sigmoid),
                     reads=[ctx.psb[bb]], writes=[ctx.sigb[s]])
                P.op(P.dve, lambda ba=ba, s=s: nc.vector.tensor_tensor(out=ctx.sig[s][:], in0=ctx.ps[ba][:], in1=ctx.sig[s][:], op=ALU.mult),
                     reads=[ctx.psb[ba], ctx.sigb[s]], writes=[ctx.sigb[s]])
                row = (c2 * 2 + cc) * 128
                P.op(P.sp, lambda row=row, s=s: nc.sync.dma_start(out=md.u[row:row + 128, t0:t0 + TT], in_=ctx.sig[s][:]),
                     reads=[ctx.sigb[s]], writes=[md.b["u"][row // 128]], dma=True)
                if tt == 1:
                    P.op(P.sp, lambda row=row, s=s: nc.sync.dma_start(out=md.tin[row:row + 128, 2:32], in_=ctx.sig[s][:, TT - 30:TT]),
                         reads=[ctx.sigb[s]], writes=[md.b["tin"][row // 128]], dma=True)
        ctx.ringA.release()


def outproj_tile(ctx, md, tt):
    P = ctx.P
    nc = P.nc
    t0 = tt * TT
    yv = md.y.rearrange("(c p) t -> p c t", p=128)
    for q in range(4):
        P.op(P.sp, lambda q=q: nc.sync.dma_start(out=ctx.hT[:, q * 8:(q + 1) * 8, :], in_=yv[:, q * 8:(q + 1) * 8, t0:t0 + TT]),
             reads=md.b["y"][q * 8:(q + 1) * 8], writes=ctx.hTb[q * 8:(q + 1) * 8], dma=True)
    bank_i = 0
    for u in range(8):
        wa, wab = ctx.ringA.acquire()
        for cc in range(4):
            bank = bank_i % 4
            bank_i += 1
            oc = u * 4 + cc
            for dc in range(DC):
                P.op(P.pe, lambda bank=bank, cc=cc, dc=dc: nc.tensor.matmul(
                    ctx.ps[bank][:], lhsT=wa[:, dc, cc * 128:(cc + 1) * 128], rhs=ctx.hT[:, dc, :],
                    start=(dc == 0), stop=(dc == DC - 1)),
                    reads=[wab, ctx.hTb[dc]], writes=[ctx.psb[bank]], signal=(dc == DC - 1))
            P.op(P.dve, lambda bank=bank, oc=oc: nc.vector.tensor_tensor(
                out=ctx.xacc[:, oc, :], in0=ctx.ps[bank][:], in1=ctx.xacc[:, oc, :], op=ALU.add),
                reads=[ctx.psb[bank], ctx.xaccb[oc]], writes=[ctx.xaccb[oc]])
        ctx.ringA.release()


def exchange(ctx, md, groups):
    P = ctx.P
    nc = P.nc
    P.op(P.pool, lambda: nc.gpsimd.collective_compute(
        "AllGather", ALU.bypass, replica_groups=groups, ins=[md.tin], outs=[md.tout]),
        reads=md.b["tin"], writes=md.b["tout"], cc=True)
    for k in range(4):
        P.op(P.pool, lambda k=k: nc.gpsimd.collective_compute(
            "AllGather", ALU.bypass, replica_groups=groups, ins=[md.xin[k * 1024:(k + 1) * 1024, :]], outs=[md.xout[k]]),
            reads=md.b["kown"] + md.b["vown"], writes=md.b["xout"], cc=True)


class ConvConsts:
    NCOL = 16 * 31 + 16 * 5

    def __init__(self, P, name):
        self.t = P.sbuf(name, [128, self.NCOL], F32)
        self.b = Buf(name)

    def dwk(self, c, j):
        o = c * 31 + j
        return self.t[:, o:o + 1]

    def vec(self, which, c):
        o = 16 * 31 + which * 16 + c
        return self.t[:, o:o + 1]


def host_conv_consts(dw_kernel, dw_bias, ln_g, ln_b, conv_out_g, attn_out_g):
    out = np.zeros((128, ConvConsts.NCOL), np.float32)
    out[:, :16 * 31] = dw_kernel.reshape(31, 16, 128).transpose(2, 1, 0).reshape(128, 16 * 31)
    for w, v in enumerate((dw_bias, ln_g, ln_b, conv_out_g, attn_out_g)):
        out[:, 16 * 31 + w * 16:16 * 31 + (w + 1) * 16] = v.reshape(16, 128).T
    return out


CONV_PE_CH = (2, 5, 8, 10, 12, 14)


def conv_phase(ctx, md, cc_, flag, flagb, ident_d, early=False):
    P = ctx.P
    nc = P.nc
    NCH = 16
    PE_CH = CONV_PE_CH
    ucur = [P.sbuf("cv_u%d" % i, [128, 32 + TOK], F32) for i in range(2)]
    ucb = [Buf("cv_u%d" % i) for i in range(2)]
    ucurp = P.sbuf("cv_up", [128, 32 + TOK], F32)
    ucpb = Buf("cv_up")
    dg = P.sbuf("cv_dg", [128, CK, 128], F32)
    dgb = Buf("cv_dg")
    idt = P.sbuf("cv_id", [128, 128], F32)
    idtb = Buf("cv_id")
    P.op(P.sp, lambda: nc.sync.dma_start(out=idt[:], in_=ident_d), writes=[idtb], dma=True)
    cy = P.sbuf("cv_y", [128, NCH, TOK], F32)
    cyb = [Buf("cv_y%d" % i) for i in range(NCH)]
    sq0 = P.sbuf("cv_sq", [128, TOK], F32)
    sq = [sq0, sq0]
    sqb0 = Buf("cv_sq")
    sqb = [sqb0, sqb0]
    mu = P.sbuf("cv_mu", [128, TOK], F32)
    mub = Buf("cv_mu")
    rs = P.sbuf("cv_rs", [128, TOK], F32)
    rsb = Buf("cv_rs")
    yb0 = P.sbuf("cv_yb", [128, TOK], BF16)
    yb16 = [yb0, yb0]
    yb0b = Buf("cv_yb")
    yb16b = [yb0b, yb0b]
    ps, psb = ctx.ps, ctx.psb
    order = []
    dl = [c for c in range(NCH) if c not in PE_CH]
    pl = [c for c in range(NCH) if c in PE_CH]
    while dl or pl:
        for _ in range(2):
            if dl:
                order.append(dl.pop(0))
        if pl:
            order.append(pl.pop(0))
    nd_i = 0
    np_i = 0
    NT = TT if early else TOK
    for c in order:
        if early:
            P.op(P.sp, lambda: nc.sync.dma_start(out=cy[:, c, TT:TOK], in_=md.convy[c * 128:(c + 1) * 128, :]),
                 reads=[md.b["convy"][c]], writes=[cyb[c]], dma=True)
        if c in PE_CH:
            P.op(P.sp, lambda: nc.sync.dma_start(out=ucurp[:, 32:32 + NT], in_=md.u[c * 128:(c + 1) * 128, 0:NT]),
                 reads=[md.b["u"][c]], writes=[ucpb], dma=True)
            P.op(P.sp, lambda: nc.sync.dma_start(out=ucurp[:, 0:32], in_=md.tprev[c * 128:(c + 1) * 128, :]),
                 reads=md.b["tout"], writes=[ucpb], dma=True)
            P.op(P.act, lambda: nc.scalar.mul(out=ucurp[:, 2:32], in_=ucurp[:, 2:32], mul=flag[:, 0:1]),
                 reads=[ucpb, flagb], writes=[ucpb])
            for j in range(CK):
                P.op(P.act, lambda: nc.scalar.mul(out=dg[:, j, :], in_=idt[:], mul=cc_.dwk(c, j)),
                     reads=[idtb, cc_.b], writes=[dgb])
            for hh in range(NT // 512):
                bank = 4 + (np_i % 2) * 2 + hh
                for j in range(CK):
                    P.op(P.pe, lambda: nc.tensor.matmul(ps[bank][:], lhsT=dg[:, j, :], rhs=ucurp[:, 2 + j + hh * 512:2 + j + (hh + 1) * 512],
                                                        start=(j == 0), stop=(j == CK - 1)),
                         reads=[dgb, ucpb], writes=[psb[bank]], signal=(j == CK - 1))
                P.op(P.act, lambda: nc.scalar.activation(out=cy[:, c, hh * 512:(hh + 1) * 512], in_=ps[bank][:], func=AF.Identity,
                                                         bias=cc_.vec(0, c), scale=1.0),
                     reads=[psb[bank], cc_.b], writes=[cyb[c]])
            np_i += 1
        else:
            s = nd_i % 2
            nd_i += 1
            P.op(P.sp, lambda: nc.sync.dma_start(out=ucur[s][:, 32:32 + NT], in_=md.u[c * 128:(c + 1) * 128, 0:NT]),
                 reads=[md.b["u"][c]], writes=[ucb[s]], dma=True)
            P.op(P.sp, lambda: nc.sync.dma_start(out=ucur[s][:, 0:32], in_=md.tprev[c * 128:(c + 1) * 128, :]),
                 reads=md.b["tout"], writes=[ucb[s]], dma=True)
            P.op(P.dve, lambda: nc.vector.tensor_scalar(out=ucur[s][:, 2:32], in0=ucur[s][:, 2:32], scalar1=flag[:, 0:1], scalar2=None, op0=ALU.mult),
                 reads=[ucb[s], flagb], writes=[ucb[s]])
            P.op(P.dve, lambda: nc.vector.tensor_scalar(
                out=cy[:, c, 0:NT], in0=ucur[s][:, 2:2 + NT], scalar1=cc_.dwk(c, 0), scalar2=cc_.vec(0, c), op0=ALU.mult, op1=ALU.add),
                reads=[ucb[s], cc_.b], writes=[cyb[c]])
            for j in range(1, CK):
                P.op(P.dve, lambda: nc.vector.scalar_tensor_tensor(
                    out=cy[:, c, 0:NT], in0=ucur[s][:, 2 + j:2 + j + NT], scalar=cc_.dwk(c, j), in1=cy[:, c, 0:NT], op0=ALU.mult, op1=ALU.add),
                    reads=[ucb[s], cyb[c], cc_.b], writes=[cyb[c]])
    for oi, c in enumerate(order):
        s = oi % 2
        P.op(P.act, lambda: nc.scalar.activation(out=sq[s][:], in_=cy[:, c, :], func=AF.Square),
             reads=[cyb[c]], writes=[sqb[s]])
        for hh in range(2):
            P.op(P.pe, lambda: nc.tensor.matmul(ps[hh][:], lhsT=ctx.ones[:], rhs=cy[:, c, hh * 512:(hh + 1) * 512],
                                                start=(oi == 0), stop=(oi == NCH - 1)),
                 reads=[cyb[c], ctx.onesb], writes=[psb[hh]], signal=True)
            P.op(P.pe, lambda: nc.tensor.matmul(ps[2 + hh][:], lhsT=ctx.ones[:], rhs=sq[s][:, hh * 512:(hh + 1) * 512],
                                                start=(oi == 0), stop=(oi == NCH - 1)),
                 reads=[sqb[s], ctx.onesb], writes=[psb[2 + hh]], signal=True)
    for hh in range(2):
        sl = slice(hh * 512, (hh + 1) * 512)
        P.op(P.act, lambda hh=hh, sl=sl: nc.scalar.mul(out=mu[:, sl], in_=ps[hh][:], mul=1.0 / CW),
             reads=[psb[hh]], writes=[mub])
        P.op(P.dve, lambda hh=hh, sl=sl: nc.vector.tensor_tensor(out=rs[:, sl], in0=mu[:, sl], in1=mu[:, sl], op=ALU.mult),
             reads=[mub], writes=[rsb])
        P.op(P.dve, lambda hh=hh, sl=sl: nc.vector.scalar_tensor_tensor(
            out=rs[:, sl], in0=ps[2 + hh][:], scalar=1.0 / CW, in1=rs[:, sl], op0=ALU.mult, op1=ALU.subtract),
            reads=[psb[2 + hh], rsb], writes=[rsb])
    P.op(P.act, lambda: nc.scalar.activation(out=rs[:], in_=rs[:], func=AF.Sqrt, bias=LN_EPS, scale=1.0),
         reads=[rsb], writes=[rsb])
    P.op(P.dve, lambda: nc.vector.reciprocal(out=rs[:], in_=rs[:]), reads=[rsb], writes=[rsb])
    for c in range(NCH):
        s = c % 2
        P.op(P.dve, lambda c=c: nc.vector.tensor_tensor(out=cy[:, c, :], in0=cy[:, c, :], in1=mu[:], op=ALU.subtract),
             reads=[cyb[c], mub], writes=[cyb[c]])
        P.op(P.dve, lambda c=c: nc.vector.tensor_tensor(out=cy[:, c, :], in0=cy[:, c, :], in1=rs[:], op=ALU.mult),
             reads=[cyb[c], rsb], writes=[cyb[c]])
        P.op(P.act, lambda c=c: nc.scalar.activation(out=cy[:, c, :], in_=cy[:, c, :], func=AF.Silu,
                                                     scale=cc_.vec(1, c), bias=cc_.vec(2, c)),
             reads=[cyb[c], cc_.b], writes=[cyb[c]])
        P.op(P.act, lambda c=c, s=s: nc.scalar.activation(out=sq[s][:], in_=cy[:, c, :], func=AF.Square),
             reads=[cyb[c]], writes=[sqb[s]])
        for hh in range(2):
            P.op(P.pe, lambda c=c, hh=hh, s=s: nc.tensor.matmul(ps[4 + hh][:], lhsT=ctx.ones[:], rhs=sq[s][:, hh * 512:(hh + 1) * 512],
                                                               start=(c == 0), stop=(c == NCH - 1)),
                 reads=[sqb[s], ctx.onesb], writes=[psb[4 + hh]], signal=True)
    for hh in range(2):
        sl = slice(hh * 512, (hh + 1) * 512)
        P.op(P.act, lambda hh=hh, sl=sl: nc.scalar.activation(out=rs[:, sl], in_=ps[4 + hh][:], func=AF.Sqrt, scale=1.0 / CW, bias=RMS_EPS),
             reads=[psb[4 + hh]], writes=[rsb])
    P.op(P.dve, lambda: nc.vector.reciprocal(out=rs[:], in_=rs[:]), reads=[rsb], writes=[rsb])
    for c in range(NCH):
        s = c % 2
        P.op(P.dve, lambda c=c, s=s: nc.vector.scalar_tensor_tensor(
            out=yb16[s][:], in0=cy[:, c, :], scalar=cc_.vec(3, c), in1=rs[:], op0=ALU.mult, op1=ALU.mult),
            reads=[cyb[c], rsb, cc_.b], writes=[yb16b[s]])
        P.op(P.sp, lambda c=c, s=s: nc.sync.dma_start(out=md.y[AW + c * 128:AW + (c + 1) * 128, :], in_=yb16[s][:]),
             reads=[yb16b[s]], writes=[md.b["y"][16 + c]], dma=True)


def attn_phase(ctx, md, cc_, bias_d, st):
    P = ctx.P
    nc = P.nc
    ps, psb = ctx.ps, ctx.psb
    kT = [P.sbuf("at_k%d" % i, [128, 2 * TOK], BF16) for i in range(2)]
    qT = [P.sbuf("at_q%d" % i, [128, TOK], BF16) for i in range(2)]
    v1 = [P.sbuf("at_v1%d" % i, [128, 9, HD], BF16) for i in range(2)]
    v4 = [P.sbuf("at_v4%d" % i, [128, 4, 3, HD], BF16) for i in range(2)]
    v16 = [P.sbuf("at_v16%d" % i, [128, 16, HD], BF16) for i in range(2)]
    bia = [P.sbuf("at_b%d" % i, [128, BIASW], F32) for i in range(2)]
    hb = [Buf("at_h%d" % i) for i in range(2)]
    nd = P.sbuf("at_nd", [128, 2, TOK], F32)
    ndb = Buf("at_nd")
    at = [P.sbuf("at_at%d" % i, [128, TOK], F32) for i in range(2)]
    atb = [Buf("at_at%d" % i) for i in range(2)]
    sqh = P.sbuf("at_sq", [128, TOK], F32)
    sqhb = Buf("at_sq")
    sqacc = P.sbuf("at_sqacc", [128, TOK], F32)
    sqaccb = Buf("at_sqacc")
    rsa = P.sbuf("at_rs", [128, TOK], F32)
    rsab = Buf("at_rs")
    yb16 = [P.sbuf("at_yb%d" % i, [128, TOK], BF16) for i in range(2)]
    yb16b = [Buf("at_yb%d" % i) for i in range(2)]
    onesb16 = P.sbuf("at_ones", [128, 128], BF16)
    onesb16b = Buf("at_ones")
    P.op(P.dve, lambda: nc.vector.memset(onesb16[:], 1.0), writes=[onesb16b])

    vown = md.vown

    hbl = [[Buf("at_h%d_%d" % (i, j)) for j in range(16)] for i in range(2)]

    def load_head(h):
        s = h % 2
        hc = slice(h * HD, (h + 1) * HD)
        cnt = [0]

        def dm(out, in_, rd):
            w = [hbl[s][cnt[0]]]
            cnt[0] += 1
            P.op(P.sp, lambda: nc.sync.dma_start(out=out, in_=in_), reads=rd, writes=w, dma=True)
        XO = md.b["xout"]
        VO = [md.b["vown"][tb * 4 + h // 4] for tb in range(8)]
        dm(kT[s][:, 0:TOK], md.kprev(h), XO)
        dm(kT[s][:, TOK:2 * TOK], md.kown[hc, :], [md.b["kown"][h]])
        dm(qT[s][:], md.qT[hc, :], [md.b["qT"][h]])
        dm(bia[s][:], bias_d[h], [])
        dm(v1[s][:, 0:1, :], md.vprev_h[1].rearrange("(b j) c -> j b c", j=128)[:, 3:4, hc], XO)
        dm(v1[s][:, 1:9, :], vown.rearrange("(b j) c -> j b c", j=128)[:, :, hc], VO)
        vp4 = md.vprev_h[1].rearrange("(j r) c -> j r c", r=4)
        vo4 = vown.rearrange("(b j r) c -> j r b c", j=128, r=4)
        dm(v4[s][:, :, 0, :], vp4[:, :, hc], XO)
        for r in range(4):
            dm(v4[s][:, r, 1:3, :], vo4[:, r, :, hc], VO)
        for i in range(2):
            dm(v16[s][32 * i:32 * (i + 1), :, :], md.vprev_h[i].rearrange("(j r) c -> j r c", r=16)[:, :, hc], XO)
        dm(v16[s][64:128, :, :], vown.rearrange("(j r) c -> j r c", r=16)[:, :, hc], VO)
        assert cnt[0] <= 16

    last_of = {}
    NR = 4
    SB = [P.sbuf("at_sbr%d" % i, [128, 512], F32) for i in range(NR)]
    SBb = [Buf("at_sbr%d" % i) for i in range(NR)]
    PT = [P.sbuf("at_pr%d" % i, [128, 512], BF16) for i in range(NR)]
    PTb = [Buf("at_pr%d" % i) for i in range(NR)]
    nd2 = [nd, P.sbuf("at_nd2", [128, 2, TOK], F32)]
    nd2b = [ndb, Buf("at_nd2")]
    SBANK = [0, 1, 2, 3]
    OBANK = [4, 5, 6, 7]

    def front(k, u):
        (s, subs, bias_ap, n, dst_fn, first, h) = u
        i = k % NR
        pss, pssb = ps[SBANK[i]], psb[SBANK[i]]
        nk = len(subs[0][0])
        nmm = len(subs) * nk
        c = 0
        for (kaps, vaps, qap) in subs:
            for kap in kaps:
                P.op(P.pe, lambda: nc.tensor.matmul(pss[:, c * n:(c + 1) * n], lhsT=kap, rhs=qap, start=True, stop=True),
                     reads=hbl[s], writes=[pssb], signal=(c == nmm - 1))
                c += 1
        w = nmm * n
        P.op(P.dve, lambda: nc.vector.scalar_tensor_tensor(out=SB[i][:, 0:w], in0=pss[:, 0:w], scalar=QSCALE, in1=bias_ap,
                                                           op0=ALU.mult, op1=ALU.add),
             reads=[pssb] + hbl[s], writes=[SBb[i]])
        P.op(P.act, lambda: nc.scalar.activation(out=PT[i][:, 0:w], in_=SB[i][:, 0:w], func=AF.Exp),
             reads=[SBb[i]], writes=[PTb[i]])

    def back(k, u):
        (s, subs, bias_ap, n, dst_fn, first, h) = u
        i = k % NR
        pso, psob = ps[OBANK[i]], psb[OBANK[i]]
        nk = len(subs[0][0])
        ns = len(subs)
        for si, (kaps, vaps, qap) in enumerate(subs):
            for a, vap in enumerate(vaps):
                P.op(P.pe, lambda: nc.tensor.matmul(pso[:, si * 2 * n:si * 2 * n + n], lhsT=vap,
                                                    rhs=PT[i][:, (si * nk + a) * n:(si * nk + a + 1) * n],
                                                    start=(a == 0), stop=(a == nk - 1)),
                     reads=hbl[s] + [PTb[i]], writes=[psob], signal=False)
            for a in range(nk):
                P.op(P.pe, lambda: nc.tensor.matmul(pso[:, si * 2 * n + n:(si + 1) * 2 * n], lhsT=onesb16[:],
                                                    rhs=PT[i][:, (si * nk + a) * n:(si * nk + a + 1) * n],
                                                    start=(a == 0), stop=(a == nk - 1)),
                     reads=[onesb16b, PTb[i]], writes=[psob], signal=(a == nk - 1 and si == ns - 1))
        src = pso[:, 0:ns * 2 * n].rearrange("p (s two q) -> p two s q", s=ns, two=2)
        ndt, ndtb = nd2[h % 2], nd2b[h % 2]
        dst = dst_fn(ndt)
        if first:
            P.op(P.act, lambda: nc.scalar.copy(out=dst, in_=src), reads=[psob], writes=[ndtb])
        else:
            P.op(P.dve, lambda: nc.vector.tensor_tensor(out=dst, in0=src, in1=dst, op=ALU.add), reads=[psob, ndtb], writes=[ndtb])

    def finalize(h):
        a = h % 2
        ndt, ndtb = nd2[h % 2], nd2b[h % 2]
        P.op(P.dve, lambda: nc.vector.reciprocal(out=ndt[:, 1, :], in_=ndt[:, 1, :]), reads=[ndtb], writes=[ndtb])
        P.op(P.dve, lambda: nc.vector.tensor_tensor(out=at[a][:], in0=ndt[:, 0, :], in1=ndt[:, 1, :], op=ALU.mult),
             reads=[ndtb], writes=[atb[a]])
        if h == 0:
            P.op(P.act, lambda: nc.scalar.activation(out=sqacc[:], in_=at[a][:], func=AF.Square), reads=[atb[a]], writes=[sqaccb])
        else:
            P.op(P.act, lambda: nc.scalar.activation(out=sqh[:], in_=at[a][:], func=AF.Square), reads=[atb[a]], writes=[sqhb])
            P.op(P.pool, lambda: nc.gpsimd.tensor_tensor(out=sqacc[:], in0=sqacc[:], in1=sqh[:], op=ALU.add),
                 reads=[sqhb, sqaccb], writes=[sqaccb])
        P.op(P.sp, lambda: nc.sync.dma_start(out=md.attn[h * HD:(h + 1) * HD, :], in_=at[a][:]),
             reads=[atb[a]], writes=[md.b["attn"][h]], dma=True)

    def head_units(h):
        s = h % 2
        us = []
        for p in range(4):
            subs = []
            for qb in (2 * p, 2 * p + 1):
                subs.append(([kT[s][:, (7 + qb) * 128:(8 + qb) * 128], kT[s][:, (8 + qb) * 128:(9 + qb) * 128]],
                             [v1[s][:, qb, :], v1[s][:, qb + 1, :]], qT[s][:, qb * 128:(qb + 1) * 128]))
            boff = 0 if p == 0 else 256
            dst_fn = lambda t, p=p: t[:, :, p * 256:(p + 1) * 256].rearrange("p two (s q) -> p two s q", s=2)
            us.append((s, subs, bia[s][:, boff:boff + 512], 128, dst_fn, True, h))
        for r in range(4):
            subs = []
            for b in (2, 3):
                kp = kT[s][:, 512 * (b - 1) + r:512 * b:4]
                kc = kT[s][:, 512 * b + r:512 * (b + 1):4]
                subs.append(([kp, kc], [v4[s][:, r, b - 2, :], v4[s][:, r, b - 1, :]], qT[s][:, 512 * (b - 2) + r:512 * (b - 1):4]))
            dst_fn = lambda t, r=r: t[:, :, r:TOK:4].rearrange("p two (s q) -> p two s q", s=2)
            us.append((s, subs, bia[s][:, 768:768 + 512], 128, dst_fn, False, h))
        for rp in range(8):
            subs = []
            for r in (2 * rp, 2 * rp + 1):
                subs.append(([kT[s][:, r:2 * TOK:16]], [v16[s][:, r, :]], qT[s][:, r:TOK:16]))
            dst_fn = lambda t, rp=rp: t[:, :, :].rearrange("p two (q r) -> p two r q", r=16)[:, :, 2 * rp:2 * rp + 2, :]
            us.append((s, subs, bia[s][:, 1280:1280 + 128], 64, dst_fn, False, h))
        return us

    LAG = 3
    load_head(0)
    pend = []
    k = 0
    for h in range(NH):
        for ui, u in enumerate(head_units(h)):
            front(k, u)
            pend.append((k, u))
            k += 1
            if len(pend) > LAG:
                kk, uu = pend.pop(0)
                back(kk, uu)
                if uu is last_of.get(uu[6]):
                    finalize(uu[6])
            if ui == LAG + 1 and h + 1 < NH:
                load_head(h + 1)
        last_of[h] = u
    while pend:
        kk, uu = pend.pop(0)
        back(kk, uu)
        if uu is last_of.get(uu[6]):
            finalize(uu[6])
    for hh in range(2):
        P.op(P.pe, lambda: nc.tensor.matmul(ps[hh][:], lhsT=ctx.ones[:], rhs=sqacc[:, hh * 512:(hh + 1) * 512], start=True, stop=True),
             reads=[sqaccb, ctx.onesb], writes=[psb[hh]], signal=True)
    for hh in range(2):
        sl = slice(hh * 512, (hh + 1) * 512)
        P.op(P.act, lambda hh=hh, sl=sl: nc.scalar.activation(out=rsa[:, sl], in_=ps[hh][:], func=AF.Sqrt, scale=1.0 / AW, bias=RMS_EPS),
             reads=[psb[hh]], writes=[rsab])
    P.op(P.dve, lambda: nc.vector.reciprocal(out=rsa[:], in_=rsa[:]), reads=[rsab], writes=[rsab])
    for h in range(NH):
        a = h % 2
        P.op(P.sp, lambda a=a, h=h: nc.sync.dma_start(out=at[a][:], in_=md.attn[h * HD:(h + 1) * HD, :]),
             reads=[md.b["attn"][h]], writes=[atb[a]], dma=True)
        P.op(P.dve, lambda a=a, h=h: nc.vector.scalar_tensor_tensor(
            out=yb16[a][:], in0=at[a][:], scalar=cc_.vec(4, h), in1=rsa[:], op0=ALU.mult, op1=ALU.mult),
            reads=[atb[a], rsab, cc_.b], writes=[yb16b[a]])
        P.op(P.sp, lambda a=a, h=h: nc.sync.dma_start(out=md.y[h * HD:(h + 1) * HD, :], in_=yb16[a][:]),
             reads=[yb16b[a]], writes=[md.b["y"][h]], dma=True)

DEPTH = 2
D_FF = 11008
N_CORES = 8


def final_norm_tile(ctx, gcol, gcolb):
    P = ctx.P
    nc = P.nc
    pst, pstb = ctx.ps[4], ctx.psb[4]
    for dc in range(DC):
        s = dc % 2
        P.op(P.act, lambda: nc.scalar.activation(out=ctx.sq[s][:], in_=ctx.xacc[:, dc, :], func=AF.Square),
             reads=[ctx.xaccb[dc]], writes=[ctx.sqb[s]])
        P.op(P.pe, lambda: nc.tensor.matmul(pst[:], lhsT=ctx.ones[:], rhs=ctx.sq[s][:], start=(dc == 0), stop=(dc == DC - 1)),
             reads=[ctx.sqb[s], ctx.onesb], writes=[pstb], signal=True)
    P.op(P.act, lambda: nc.scalar.activation(out=ctx.rstd[:], in_=pst[:], func=AF.Sqrt, scale=1.0 / D, bias=RMS_EPS),
         reads=[pstb], writes=[ctx.rstdb])
    P.op(P.dve, lambda: nc.vector.reciprocal(out=ctx.rstd[:], in_=ctx.rstd[:]), reads=[ctx.rstdb], writes=[ctx.rstdb])
    for dc in range(DC):
        P.op(P.dve, lambda: nc.vector.scalar_tensor_tensor(
            out=ctx.xacc[:, dc, :], in0=ctx.xacc[:, dc, :], scalar=gcol[:, dc:dc + 1], in1=ctx.rstd[:],
            op0=ALU.mult, op1=ALU.mult),
            reads=[ctx.xaccb[dc], ctx.rstdb, gcolb], writes=[ctx.xaccb[dc]])


def build_program(F, n_cores):
    nc = bass.Bass("TRN2", target_bir_lowering=False)
    L = DEPTH
    ein = lambda name, shape: nc.dram_tensor(name, list(shape), F32, kind="ExternalInput").ap()
    xT = ein("xT", [D, TOK])
    w = {}
    for nm in ("ffn1_w1", "ffn1_w3", "ffn2_w1", "ffn2_w3"):
        w[nm] = ein(nm, [L, D, F])
    for nm in ("ffn1_w2", "ffn2_w2"):
        w[nm] = ein(nm, [L, F, D])
    w["w_in"] = ein("w_in", [L, D, 3 * AW + 2 * CW])
    w["w_out"] = ein("w_out", [L, D, D])
    gcd = ein("gcols", [128, 7 * DC])
    ccd = ein("convc", [L, 128, ConvConsts.NCOL])
    flagd = ein("flag", [128, 1])
    biasd = ein("biasd", [NH, 128, BIASW])
    identd = ein("ident", [128, 128])
    out = nc.dram_tensor("outT", [D, TOK], F32, kind="ExternalOutput").ap()
    xres = nc.dram_tensor("xres", [D, TOK], F32).ap()
    md = MixDram(nc)
    groups = [[2 * i, 2 * i + 1] for i in range(n_cores // 2)]
    with ExitStack() as st:
        P = Prog(nc, st)
        ctx = Ctx(P)
        gcol = P.sbuf("gcol", [128, 7 * DC], F32)
        gcolb = Buf("gcol")
        P.op(P.sp, lambda: nc.sync.dma_start(out=gcol[:], in_=gcd), writes=[gcolb], dma=True)
        flag = P.sbuf("flag", [128, 1], F32)
        flagb = Buf("flag")
        P.op(P.sp, lambda: nc.sync.dma_start(out=flag[:], in_=flagd), writes=[flagb], dma=True)
        g_of = lambda l, k: gcol[:, (l * 3 + k) * DC:(l * 3 + k + 1) * DC]
        g_fin = gcol[:, 6 * DC:7 * DC]
        for tt in range(2):
            plan_ffn(ctx, w["ffn1_w1"][0], w["ffn1_w3"][0], w["ffn1_w2"][0], F, ntiles=1)
            plan_inproj(ctx, w["w_in"][0])
        for tt in range(2):
            plan_outproj(ctx, w["w_out"][0])
            plan_ffn(ctx, w["ffn2_w1"][0], w["ffn2_w3"][0], w["ffn2_w2"][0], F, ntiles=1)
            plan_ffn(ctx, w["ffn1_w1"][1], w["ffn1_w3"][1], w["ffn1_w2"][1], F, ntiles=1)
            plan_inproj(ctx, w["w_in"][1])
        for tt in range(2):
            plan_outproj(ctx, w["w_out"][1])
            plan_ffn(ctx, w["ffn2_w1"][1], w["ffn2_w3"][1], w["ffn2_w2"][1], F, ntiles=1)
        ctx.ringA.start()
        ctx.ringB.start()
        xin_v = xT.rearrange("(c p) t -> p c t", p=128)
        xres_v = xres.rearrange("(c p) t -> p c t", p=128)
        out_v = out.rearrange("(c p) t -> p c t", p=128)
        xinb = [Buf("xin0"), Buf("xin1")]
        xresb = [Buf("xres0"), Buf("xres1")]
        outb = [Buf("out0"), Buf("out1")]

        def mixer_core(l):
            exchange(ctx, md, groups)
            P.push()
            cc_ = ConvConsts(P, "ccA%d" % l)
            P.op(P.sp, lambda: nc.sync.dma_start(out=cc_.t[:], in_=ccd[l]), writes=[cc_.b], dma=True)
            conv_phase(ctx, md, cc_, flag, flagb, identd, True)
            P.pop()
            P.push()
            cc2 = ConvConsts(P, "ccB%d" % l)
            P.op(P.sp, lambda: nc.sync.dma_start(out=cc2.t[:], in_=ccd[l]), writes=[cc2.b], dma=True)
            attn_phase(ctx, md, cc2, biasd, None)
            P.pop()

        def load_cce(l):
            c = ConvConsts(P, "ccE%d" % l)
            P.op(P.sp, lambda: nc.sync.dma_start(out=c.t[:], in_=ccd[l]), writes=[c.b], dma=True)
            return c

        P.push()
        ctx.alloc_dense()
        cce = load_cce(0)
        load_x(ctx, xin_v, xinb, 0)
        for tt in range(2):
            ffn_tile(ctx, F, g_of(0, 0), gcolb)

            def _swap(tt=tt):
                store_x(ctx, xres_v, xresb, tt)
                if tt == 0:
                    load_x(ctx, xin_v, xinb, 1)
            inproj_tile(ctx, md, tt, g_of(0, 1), gcolb, _swap, cce)
        P.pop()
        mixer_core(0)
        P.push()
        ctx.alloc_dense()
        cce = load_cce(1)
        load_x(ctx, xres_v, xresb, 0)
        for tt in range(2):
            outproj_tile(ctx, md, tt)
            ffn_tile(ctx, F, g_of(0, 2), gcolb)
            ffn_tile(ctx, F, g_of(1, 0), gcolb)

            def _swap(tt=tt):
                store_x(ctx, xres_v, xresb, tt)
                if tt == 0:
                    load_x(ctx, xres_v, xresb, 1)
            inproj_tile(ctx, md, tt, g_of(1, 1), gcolb, _swap, cce)
        P.pop()
        mixer_core(1)
        P.push()
        ctx.alloc_dense()
        for tt in range(2):
            load_x(ctx, xres_v, xresb, tt)
            outproj_tile(ctx, md, tt)
            ffn_tile(ctx, F, g_of(1, 2), gcolb)
            final_norm_tile(ctx, g_fin, gcolb)
            store_x(ctx, out_v, outb, tt)
        P.finish(outb)
        P.pop()
        assert ctx.ringA.consumed == len(ctx.ringA.units) and ctx.ringB.consumed == len(ctx.ringB.units)
        build_program.stats = (P.ninstr, P.nwaits, P.nsem)
    return nc


def make_in_maps(inputs, n_cores):
    f = lambda k: np.ascontiguousarray(np.asarray(inputs[k], dtype=np.float32))
    x = f("x")
    col = lambda v: np.ascontiguousarray(np.asarray(v, np.float32).reshape(DC, 128).T)
    gl = []
    for l in range(DEPTH):
        gl += [col(inputs["ffn1_norm_g"][l]), col(inputs["mix_norm_g"][l]), col(inputs["ffn2_norm_g"][l])]
    gl.append(col(inputs["final_norm_g"]))
    gcols = np.ascontiguousarray(np.concatenate(gl, axis=1))
    convc = np.stack([host_conv_consts(np.asarray(inputs["dw_kernel"][l]), np.asarray(inputs["dw_bias"][l]),
                                       np.asarray(inputs["conv_ln_g"][l]), np.asarray(inputs["conv_ln_b"][l]),
                                       np.asarray(inputs["conv_out_g"][l]), np.asarray(inputs["attn_out_g"][l]))
                      for l in range(DEPTH)])
    rel = np.asarray(inputs["rel_bias_table"], np.float32)
    bias_h = [build_bias(rel, 0), build_bias(rel, 1)]
    shared = {k: f(k) for k in ("ffn1_w1", "ffn1_w3", "ffn1_w2", "ffn2_w1", "ffn2_w3", "ffn2_w2", "w_in", "w_out")}
    maps = []
    for c in range(n_cores):
        b, half = c // 2, c % 2
        m = dict(shared)
        m["xT"] = np.ascontiguousarray(x[b, half * TOK:(half + 1) * TOK, :].T)
        m["gcols"] = gcols
        m["convc"] = convc
        m["flag"] = np.full((128, 1), float(half), np.float32)
        m["biasd"] = bias_h[half]
        m["ident"] = np.eye(128, dtype=np.float32)
        maps.append(m)
    return maps


def kernel(**inputs):
    x = np.asarray(inputs["x"])
    B, S, _ = x.shape
    n_cores = B * 2
    F = np.asarray(inputs["ffn1_w1"]).shape[-1]
    nc = build_program(F, n_cores)
    maps = make_in_maps(inputs, n_cores)
    res = run_bass_kernel_spmd(nc, maps, core_ids=list(range(n_cores)))
    out = np.empty((B, S, D), np.float32)
    for c in range(n_cores):
        b, half = c // 2, c % 2
        out[b, half * TOK:(half + 1) * TOK, :] = res.results[c]["outT"].T
    return out
```

```python
from contextlib import ExitStack
from concourse.bass_utils import run_bass_kernel_spmd

import numpy as np
import concourse.bass as bass
import concourse.mybir as mybir

F32 = mybir.dt.float32
BF16 = mybir.dt.bfloat16
AF = mybir.ActivationFunctionType
ALU = mybir.AluOpType
AX = mybir.AxisListType


class Buf:
    __slots__ = ("name", "writer", "readers", "dreaders")

    def __init__(self, name):
        self.name = name
        self.writer = None
        self.readers = {}
        self.dreaders = []


class Node:
    __slots__ = ("eng", "idx", "sem", "val", "is_dma")

    def __init__(self, eng, idx):
        self.eng = eng
        self.idx = idx
        self.sem = None
        self.val = None
        self.is_dma = False


class Eng:
    def __init__(self, P, name, handle):
        self.P = P
        self.name = name
        self.h = handle
        self.sem = P.newsem("p_" + name)
        self.count = 0
        self.n = 0
        self.waited = {}
        self.nodes = []
        self.pending = []
        self.dring = None
        self.dcount = 0

    def wait(self, sem, val):
        key = id(sem)
        if self.waited.get(key, -1) >= val:
            return
        self.waited[key] = val
        self.h.wait_ge(sem, val)
        self.P.nwaits += 1


class Prog:
    def __init__(self, nc, stack):
        self.nc = nc
        self.stack = stack
        self.nsem = 0
        self.nalloc = 0
        self.stacks = []
        self.ccsems = []
        self.nwaits = 0
        self.ninstr = 0
        self.pe = Eng(self, "pe", nc.tensor)
        self.act = Eng(self, "act", nc.scalar)
        self.dve = Eng(self, "dve", nc.vector)
        self.pool = Eng(self, "pool", nc.gpsimd)
        self.sp = Eng(self, "sp", nc.sync)
        self.engs = [self.pe, self.act, self.dve, self.pool, self.sp]
        for e in (self.sp, self.pool, self.act):
            e.dring = [[self.newsem("d_%s%d" % (e.name, i)), 0] for i in range(12)]

    def newsem(self, name):
        self.nsem += 1
        return self.stack.enter_context(self.nc.semaphore(name))

    def sbuf(self, name, shape, dtype):
        self.nalloc += 1
        st = self.stacks[-1] if self.stacks else self.stack
        return st.enter_context(self.nc.sbuf_tensor("%s_%d" % (name, self.nalloc), list(shape), dtype))

    def push(self):
        from contextlib import ExitStack
        self.stacks.append(ExitStack())

    def pop(self):
        self.barrier()
        self.stacks.pop().close()

    def barrier(self):
        for e in self.engs:
            if e.pending:
                raise RuntimeError("barrier with pending non-signalled nodes on %s" % e.name)
        for e in self.engs:
            for e2 in self.engs:
                if e2 is not e and e2.count > 0:
                    e.wait(e2.sem, e2.count)
                if e2.dring:
                    for slot in e2.dring:
                        if slot[1] > 0:
                            e.wait(slot[0], slot[1])
            for sem in self.ccsems:
                e.wait(sem, 1)

    def psum(self, name, shape, dtype):
        return self.stack.enter_context(self.nc.psum_tensor(name, list(shape), dtype))

    def _dep(self, eng, node, raw):
        if node is None:
            return
        if node.is_dma:
            eng.wait(node.sem, node.val)
            return
        if node.eng is eng:
            if eng is self.pe:
                return
            if not raw:
                return
            if node.idx < eng.n - 2:
                return
        if node.val is None:
            raise RuntimeError("dependency on non-signalled node of %s" % node.eng.name)
        eng.wait(node.eng.sem, node.val)

    def op(self, eng, fn, reads=(), writes=(), signal=True, dma=False, cc=False):
        for b in reads:
            self._dep(eng, b.writer, True)
        for b in writes:
            self._dep(eng, b.writer, False)
            for r in b.readers.values():
                self._dep(eng, r, False)
            for r in b.dreaders:
                self._dep(eng, r, False)
        node = Node(eng, eng.n)
        if cc:
            node.is_dma = True
            sem = self.newsem("cc%d" % len(self.ccsems))
            self.ccsems.append(sem)
            ins = fn()
            ins.then_inc(sem, 1)
            node.sem, node.val = sem, 1
            dma = True
        elif dma:
            node.is_dma = True
            slot = eng.dring[eng.dcount % len(eng.dring)]
            eng.dcount += 1
            if slot[1] > 0:
                eng.wait(slot[0], slot[1])
            ins = fn()
            slot[1] += 16
            ins.then_inc(slot[0], 16)
            node.sem, node.val = slot[0], slot[1]
        else:
            ins = fn()
            if signal:
                eng.count += 1
                ins.then_inc(eng.sem, 1)
                node.val = eng.count
                for pn in eng.pending:
                    pn.val = eng.count
                eng.pending = []
            else:
                eng.pending.append(node)
        eng.n += 1
        self.ninstr += 1
        for b in reads:
            if dma:
                b.dreaders.append(node)
            else:
                b.readers[eng.name] = node
        for b in writes:
            b.writer = node
            b.readers = {}
            b.dreaders = []
        return node

    def finish(self, bufs):
        for b in bufs:
            self._dep(self.sp, b.writer, True)


D = 4096
DC = D // 128
TT = 512
RMS_EPS = 1e-6


class Ring:
    def __init__(self, P, name, nslots, shape):
        self.P = P
        self.nslots = nslots
        self.tiles = [P.sbuf("%s%d" % (name, i), shape, BF16) for i in range(nslots)]
        self.bufs = [Buf("%s%d" % (name, i)) for i in range(nslots)]
        self.units = []
        self.issued = 0
        self.consumed = 0

    def plan(self, parts):
        self.units.append(parts)

    def _issue(self):
        if self.issued >= len(self.units):
            return
        k = self.issued
        s = k % self.nslots
        tile = self.tiles[s]
        for (dst_fn, src) in self.units[k]:
            self.P.op(self.P.pool, lambda d=dst_fn(tile), s_=src: self.P.nc.gpsimd.dma_start(out=d, in_=s_),
                      writes=[self.bufs[s]], dma=True)
        self.issued += 1

    def start(self):
        while self.issued < min(self.nslots, len(self.units)):
            self._issue()

    def acquire(self):
        k = self.consumed
        assert k < self.issued, "ring unit not issued"
        s = k % self.nslots
        return self.tiles[s], self.bufs[s]

    def release(self):
        self.consumed += 1
        self._issue()


class Ctx:
    def __init__(self, P):
        nc = P.nc
        self.P = P
        self.ringA = Ring(P, "ringA", 2, [128, DC, 512])
        self.ringB = Ring(P, "ringB", 2, [128, 2, D])
        self.ps = [P.psum("ps%d" % i, [128, 512], F32) for i in range(8)]
        self.psb = [Buf("ps%d" % i) for i in range(8)]
        self.ones = P.sbuf("ones", [128, 128], F32)
        self.onesb = Buf("ones")
        P.op(P.dve, lambda: nc.vector.memset(self.ones[:], 1.0), writes=[self.onesb])

    def alloc_dense(self):
        P = self.P
        self.xacc = P.sbuf("xacc", [128, DC, TT], F32)
        self.xaccb = [Buf("xacc%d" % i) for i in range(DC)]
        self.hT = P.sbuf("hT", [128, DC, TT], BF16)
        self.hTb = [Buf("hT%d" % i) for i in range(DC)]
        self.gbuf = [P.sbuf("g%d" % i, [128, 2, TT], BF16) for i in range(2)]
        self.gb = [[Buf("g%d_%d" % (i, j)) for j in range(2)] for i in range(2)]
        self.sig = [P.sbuf("sig%d" % i, [128, TT], F32) for i in range(2)]
        self.sigb = [Buf("sig%d" % i) for i in range(2)]
        self.sq = self.sig
        self.sqb = self.sigb
        self.rstd = P.sbuf("rstd", [128, TT], F32)
        self.rstdb = Buf("rstd")


def plan_ffn(ctx, w1, w3, w2, F, ntiles=2):
    NG = F // 256
    w1v = w1.rearrange("(c p) f -> p c f", p=128)
    w3v = w3.rearrange("(c p) f -> p c f", p=128)
    w2v = w2.rearrange("(c p) d -> p c d", p=128)
    for t in range(ntiles):
        for g in range(NG):
            ctx.ringA.plan([
                (lambda tl: tl[:, :, 0:256], w1v[:, :, g * 256:(g + 1) * 256]),
                (lambda tl: tl[:, :, 256:512], w3v[:, :, g * 256:(g + 1) * 256]),
            ])
            ctx.ringB.plan([
                (lambda tl: tl[:, :, :], w2v[:, 2 * g:2 * g + 2, :]),
            ])


def rmsnorm_tile(ctx, gcol, gcolb):
    P = ctx.P
    nc = P.nc
    pst, pstb = ctx.ps[4], ctx.psb[4]
    for dc in range(DC):
        s = dc % 2
        P.op(P.act, lambda dc=dc, s=s: nc.scalar.activation(out=ctx.sq[s][:], in_=ctx.xacc[:, dc, :], func=AF.Square),
             reads=[ctx.xaccb[dc]], writes=[ctx.sqb[s]])
        P.op(P.pe, lambda dc=dc, s=s: nc.tensor.matmul(pst[:], lhsT=ctx.ones[:], rhs=ctx.sq[s][:], start=(dc == 0), stop=(dc == DC - 1)),
             reads=[ctx.sqb[s], ctx.onesb], writes=[pstb], signal=True)
    P.op(P.act, lambda: nc.scalar.activation(out=ctx.rstd[:], in_=pst[:], func=AF.Sqrt, scale=1.0 / D, bias=RMS_EPS),
         reads=[pstb], writes=[ctx.rstdb])
    P.op(P.dve, lambda: nc.vector.reciprocal(out=ctx.rstd[:], in_=ctx.rstd[:]), reads=[ctx.rstdb], writes=[ctx.rstdb])
    for dc in range(DC):
        P.op(P.dve, lambda dc=dc: nc.vector.scalar_tensor_tensor(
            out=ctx.hT[:, dc, :], in0=ctx.xacc[:, dc, :], scalar=gcol[:, dc:dc + 1], in1=ctx.rstd[:],
            op0=ALU.mult, op1=ALU.mult),
            reads=[ctx.xaccb[dc], ctx.rstdb, gcolb], writes=[ctx.hTb[dc]])


def load_x(ctx, xsrc_v, xsrcb, tt, groups=(0, 1, 2, 3)):
    P = ctx.P
    nc = P.nc
    for q in groups:
        P.op(P.sp, lambda q=q: nc.sync.dma_start(out=ctx.xacc[:, q * 8:(q + 1) * 8, :], in_=xsrc_v[:, q * 8:(q + 1) * 8, tt * TT:(tt + 1) * TT]),
             reads=[xsrcb[tt]], writes=ctx.xaccb[q * 8:(q + 1) * 8], dma=True)


def store_x(ctx, xdst_v, xdstb, tt):
    P = ctx.P
    nc = P.nc
    for q in range(4):
        P.op(P.sp, lambda q=q: nc.sync.dma_start(out=xdst_v[:, q * 8:(q + 1) * 8, tt * TT:(tt + 1) * TT], in_=ctx.xacc[:, q * 8:(q + 1) * 8, :]),
             reads=ctx.xaccb[q * 8:(q + 1) * 8], writes=[xdstb[tt]], dma=True)


def ffn_tile(ctx, F, gcol, gcolb):
    P = ctx.P
    nc = P.nc
    NG = F // 256
    rmsnorm_tile(ctx, gcol, gcolb)
    A = [None, None]
    for g in range(NG + 1):
        if g < NG:
            wa, wab = ctx.ringA.acquire()
        if g >= 1:
            wb, wbb = ctx.ringB.acquire()
        for i in range(32):
            if g < NG:
                for j in range(4):
                    m = i * 4 + j
                    fc, r = divmod(m, 64)
                    w, dc = divmod(r, 32)
                    bank = fc * 2 + w
                    col = w * 256 + fc * 128
                    P.op(P.pe, lambda bank=bank, col=col, dc=dc: nc.tensor.matmul(
                        ctx.ps[bank][:], lhsT=wa[:, dc, col:col + 128], rhs=ctx.hT[:, dc, :],
                        start=(dc == 0), stop=(dc == DC - 1)),
                        reads=[wab, ctx.hTb[dc]], writes=[ctx.psb[bank]], signal=(dc == DC - 1))
                    if r == 63:
                        s = fc
                        P.op(P.act, lambda fc=fc, s=s: nc.scalar.activation(out=ctx.sig[s][:], in_=ctx.ps[fc * 2][:], func=AF.Silu),
                             reads=[ctx.psb[fc * 2]], writes=[ctx.sigb[s]])
                        P.op(P.dve, lambda fc=fc, s=s, g=g: nc.vector.tensor_tensor(
                            out=ctx.gbuf[g % 2][:, fc, :], in0=ctx.ps[fc * 2 + 1][:], in1=ctx.sig[s][:], op=ALU.mult),
                            reads=[ctx.psb[fc * 2 + 1], ctx.sigb[s]], writes=[ctx.gb[g % 2][fc]])
            if g >= 1:
                gp = (g - 1) % 2
                bank = 4 + (i % 4)
                for fc in range(2):
                    P.op(P.pe, lambda bank=bank, fc=fc, i=i, gp=gp: nc.tensor.matmul(
                        ctx.ps[bank][:], lhsT=wb[:, fc, i * 128:(i + 1) * 128], rhs=ctx.gbuf[gp][:, fc, :],
                        start=(fc == 0), stop=(fc == 1)),
                        reads=[wbb, ctx.gb[gp][fc]], writes=[ctx.psb[bank]], signal=(fc == 1))
                P.op(P.dve, lambda bank=bank, i=i: nc.vector.scalar_tensor_tensor(
                    out=ctx.xacc[:, i, :], in0=ctx.ps[bank][:], scalar=0.5, in1=ctx.xacc[:, i, :],
                    op0=ALU.mult, op1=ALU.add),
                    reads=[ctx.psb[bank], ctx.xaccb[i]], writes=[ctx.xaccb[i]])
        if g < NG:
            ctx.ringA.release()
        if g >= 1:
            ctx.ringB.release()


HD = 128
NH = 16
AW = 2048
CW = 2048
CK = 31
TOK = 1024
QSCALE = HD ** -0.5
LN_EPS = 1e-5
NEG = -30000.0
BIASW = 768 + 512 + 128


def rel_bucket_np(dist):
    max_exact = 16
    df = np.maximum(dist, 1).astype(np.float32)
    large = max_exact + (np.log(df / max_exact) / np.log(2048 / max_exact) * (32 - max_exact)).astype(np.int32)
    large = np.minimum(large, 31)
    return np.where(dist < max_exact, dist, large).astype(np.int32)


def build_bias(rel_table, half):
    out = np.full((NH, 128, BIASW), NEG, np.float32)
    j = np.arange(128)[:, None]
    i = np.arange(128)[None, :]
    for pi, d in enumerate((1, 4)):
        step_c = i - j
        val_c = step_c >= 0
        b_c = rel_bucket_np(np.clip(step_c, 0, None) * d)
        step_p = i + 128 - j
        val_p = step_p <= 128
        b_p = rel_bucket_np(np.clip(step_p, 0, None) * d)
        base = 0 if d == 1 else 768
        for h in range(NH):
            cur = np.where(val_c, rel_table[b_c, h], NEG).astype(np.float32)
            prv = np.where(val_p, rel_table[b_p, h], NEG).astype(np.float32)
            if half == 1:
                out[h, :, base:base + 128] = prv
            out[h, :, base + 128:base + 256] = cur
            out[h, :, base + 256:base + 384] = prv
            out[h, :, base + 384:base + 512] = cur
            if d == 1:
                out[h, :, base + 512:base + 640] = prv
                out[h, :, base + 640:base + 768] = cur
    i2 = np.arange(64)[None, :]
    step = 64 + i2 - j
    val = step >= 0
    if half == 0:
        val = val & (j >= 64)
    b16 = rel_bucket_np(np.clip(step, 0, None) * 16)
    for h in range(NH):
        t16 = np.where(val, rel_table[b16, h], NEG).astype(np.float32)
        out[h, :, 1280:1344] = t16
        out[h, :, 1344:1408] = t16
    return out


def plan_inproj(ctx, w_in):
    wv = w_in.rearrange("(c p) f -> p c f", p=128)
    for c2 in range(8):
        ctx.ringA.plan([
            (lambda tl: tl[:, :, 0:256], wv[:, :, 3 * AW + c2 * 256:3 * AW + (c2 + 1) * 256]),
            (lambda tl: tl[:, :, 256:512], wv[:, :, 3 * AW + CW + c2 * 256:3 * AW + CW + (c2 + 1) * 256]),
        ])
    for u in range(12):
        ctx.ringA.plan([(lambda tl: tl[:, :, :], wv[:, :, u * 512:(u + 1) * 512])])


def plan_outproj(ctx, w_out):
    wv = w_out.rearrange("(c p) f -> p c f", p=128)
    for u in range(8):
        ctx.ringA.plan([(lambda tl: tl[:, :, :], wv[:, :, u * 512:(u + 1) * 512])])


class MixDram:
    def __init__(self, nc):
        self.qT = nc.dram_tensor("mx_qT", [AW, TOK], BF16).ap()
        self.xin = nc.dram_tensor("mx_xin", [2 * AW, TOK], BF16).ap()
        self.xout = nc.dram_tensor("mx_xout", [4, 2 * 1024, TOK], BF16).ap()
        self.u = nc.dram_tensor("mx_u", [CW, TOK], F32).ap()
        self.tin = nc.dram_tensor("mx_tin", [CW, 32], F32).ap()
        self.tout = nc.dram_tensor("mx_tout", [2 * CW, 32], F32).ap()
        self.attn = nc.dram_tensor("mx_attn", [AW, TOK], F32).ap()
        self.convy = nc.dram_tensor("mx_convy", [CW, TOK], F32).ap()
        self.y = nc.dram_tensor("mx_y", [D, TOK], BF16).ap()
        mk = lambda k, n: [Buf("mxb_%s%d" % (k, i)) for i in range(n)]
        self.b = {"qT": mk("qT", 16), "kown": mk("kown", 16), "vown": mk("vown", 32), "xout": mk("xout", 1),
                  "u": mk("u", 16), "tin": mk("tin", 16), "tout": mk("tout", 1), "attn": mk("attn", 16), "y": mk("y", 32), "convy": mk("convy", 32)}
        self.kown = self.xin[0:AW, :]
        self.vown = self.xin[AW:2 * AW, :].rearrange("(t two) c -> t (two c)", two=2)
        self.vprev_h = [self.xout[2 + i, 0:1024, :].rearrange("(t two) c -> t (two c)", two=2) for i in range(2)]
        self.tprev = self.tout[0:CW, :]

    def kprev(self, h):
        return self.xout[h // 8, (h % 8) * 128:(h % 8 + 1) * 128, :]


def _ec_bufs(ctx, k):
    s = k % 2
    U = ctx.xacc[:, 2 * s:2 * s + 2, :].rearrange("p a t -> p (a t)")
    return U, ctx.xaccb[2 * s:2 * s + 2], ctx.xacc[:, 4 + s, :], [ctx.xaccb[4 + s]]


def early_conv_load(ctx, md, c, lo, hi):
    P = ctx.P
    nc = P.nc
    U, Ub, C, Cb = _ec_bufs(ctx, c)
    P.op(P.sp, lambda: nc.sync.dma_start(out=U[:, 0:32 + hi - lo], in_=md.u[c * 128:(c + 1) * 128, lo - 32:hi]),
         reads=[md.b["u"][c]], writes=Ub, dma=True)


def early_conv_store(ctx, md, c, lo, hi):
    P = ctx.P
    nc = P.nc
    U, Ub, C, Cb = _ec_bufs(ctx, c)
    P.op(P.sp, lambda: nc.sync.dma_start(out=md.convy[c * 128:(c + 1) * 128, lo:hi], in_=C[:, 0:hi - lo]),
         reads=Cb, writes=[md.b["convy"][2 * c + (1 if lo >= TT else 0)]], dma=True)


def early_conv_chunk(ctx, md, cc_, c, lo, hi):
    P = ctx.P
    nc = P.nc
    n = hi - lo
    U, Ub, C, Cb = _ec_bufs(ctx, c)
    if c == 0:
        early_conv_load(ctx, md, 0, lo, hi)
    if c + 1 < 16:
        early_conv_load(ctx, md, c + 1, lo, hi)
    P.op(P.dve, lambda: nc.vector.tensor_scalar(out=C[:, 0:n], in0=U[:, 2:2 + n], scalar1=cc_.dwk(c, 0), scalar2=cc_.vec(0, c),
                                                op0=ALU.mult, op1=ALU.add),
         reads=Ub + [cc_.b], writes=Cb)
    for j in range(1, CK):
        P.op(P.dve, lambda: nc.vector.scalar_tensor_tensor(out=C[:, 0:n], in0=U[:, 2 + j:2 + j + n], scalar=cc_.dwk(c, j), in1=C[:, 0:n],
                                                           op0=ALU.mult, op1=ALU.add),
             reads=Ub + Cb + [cc_.b], writes=Cb)
    if c >= 1:
        early_conv_store(ctx, md, c - 1, lo, hi)
    if c == 15:
        early_conv_store(ctx, md, 15, lo, hi)


def inproj_tile(ctx, md, tt, gcol, gcolb, after_norm=None, early_cc=None):
    P = ctx.P
    nc = P.nc
    rmsnorm_tile(ctx, gcol, gcolb)
    if after_norm is not None:
        after_norm()
    t0 = tt * TT
    bank_i = 0
    early = early_cc is not None
    elo, ehi = (32, TT) if tt == 0 else (TT, 2 * TT)
    ec = 0
    for ui, u in enumerate(list(range(12, 20)) + list(range(12))):
        if early and ui >= 8:
            while ec < 16 and ec * 12 < (ui - 8 + 1) * 16:
                early_conv_chunk(ctx, md, early_cc, ec, elo, ehi)
                ec += 1
        wa, wab = ctx.ringA.acquire()
        if u < 8:
            for cc in range(4):
                bank = bank_i % 4
                bank_i += 1
                for dc in range(DC):
                    P.op(P.pe, lambda bank=bank, cc=cc, dc=dc: nc.tensor.matmul(
                        ctx.ps[bank][:], lhsT=wa[:, dc, cc * 128:(cc + 1) * 128], rhs=ctx.hT[:, dc, :],
                        start=(dc == 0), stop=(dc == DC - 1)),
                        reads=[wab, ctx.hTb[dc]], writes=[ctx.psb[bank]], signal=(dc == DC - 1))
                st, stb = ctx.gbuf[bank // 2][:, bank % 2, :], ctx.gb[bank // 2][bank % 2]
                if cc % 2 == 0 or early:
                    P.op(P.act, lambda bank=bank, st=st: nc.scalar.copy(out=st, in_=ctx.ps[bank][:]),
                         reads=[ctx.psb[bank]], writes=[stb])
                else:
                    P.op(P.dve, lambda bank=bank, st=st: nc.vector.tensor_copy(out=st, in_=ctx.ps[bank][:]),
                         reads=[ctx.psb[bank]], writes=[stb])
                row = (u % 4) * 512 + cc * 128
                if u < 4:
                    dst, dstb = md.qT[row:row + 128, t0:t0 + TT], md.b["qT"][row // 128]
                else:
                    dst, dstb = md.kown[row:row + 128, t0:t0 + TT], md.b["kown"][row // 128]
                P.op(P.sp, lambda dst=dst, st=st: nc.sync.dma_start(out=dst, in_=st), reads=[stb], writes=[dstb], dma=True)
        elif u < 12:
            for ts in range(4):
                bank = bank_i % 4
                bank_i += 1
                for dc in range(DC):
                    P.op(P.pe, lambda bank=bank, ts=ts, dc=dc: nc.tensor.matmul(
                        ctx.ps[bank][:], lhsT=ctx.hT[:, dc, ts * 128:(ts + 1) * 128], rhs=wa[:, dc, :],
                        start=(dc == 0), stop=(dc == DC - 1)),
                        reads=[wab, ctx.hTb[dc]], writes=[ctx.psb[bank]], signal=(dc == DC - 1))
                st, stb = ctx.gbuf[bank // 2][:, bank % 2, :], ctx.gb[bank // 2][bank % 2]
                if ts % 2 == 0 or early:
                    P.op(P.act, lambda bank=bank, st=st: nc.scalar.copy(out=st, in_=ctx.ps[bank][:]),
                         reads=[ctx.psb[bank]], writes=[stb])
                else:
                    P.op(P.dve, lambda bank=bank, st=st: nc.vector.tensor_copy(out=st, in_=ctx.ps[bank][:]),
                         reads=[ctx.psb[bank]], writes=[stb])
                r0 = t0 + ts * 128
                c0 = (u - 8) * 512
                dst = md.vown[r0:r0 + 128, c0:c0 + 512]
                P.op(P.sp, lambda dst=dst, st=st: nc.sync.dma_start(out=dst, in_=st), reads=[stb], writes=[md.b["vown"][(r0 // 128) * 4 + (u - 8)]], dma=True)
        else:
            c2 = u - 12
            for cc in range(2):
                ba = (bank_i % 2) * 2
                bank_i += 1
                bb = ba + 1
                for (bank, col) in ((ba, cc * 128), (bb, 256 + cc * 128)):
                    for dc in range(DC):
                        P.op(P.pe, lambda bank=bank, col=col, dc=dc: nc.tensor.matmul(
                            ctx.ps[bank][:], lhsT=wa[:, dc, col:col + 128], rhs=ctx.hT[:, dc, :],
                            start=(dc == 0), stop=(dc == DC - 1)),
                            reads=[wab, ctx.hTb[dc]], writes=[ctx.psb[bank]], signal=(dc == DC - 1))
                s = ba // 2
                P.op(P.act, lambda bb=bb, s=s: nc.scalar.activation(out=ctx.sig[s][:], in_=ctx.ps[bb][:], func=AF.Sigmoid),
                     reads=[ctx.psb[bb]], writes=[ctx.sigb[s]])
                P.op(P.dve, lambda ba=ba, s=s: nc.vector.tensor_tensor(out=ctx.sig[s][:], in0=ctx.ps[ba][:], in1=ctx.sig[s][:], op=ALU.mult),
                     reads=[ctx.psb[ba], ctx.sigb[s]], writes=[ctx.sigb[s]])
                row = (c2 * 2 + cc) * 128
                P.op(P.sp, lambda row=row, s=s: nc.sync.dma_start(out=md.u[row:row + 128, t0:t0 + TT], in_=ctx.sig[s][:]),
                     reads=[ctx.sigb[s]], writes=[md.b["u"][row // 128]], dma=True)
                if tt == 1:
                    P.op(P.sp, lambda row=row, s=s: nc.sync.dma_start(out=md.tin[row:row + 128, 2:32], in_=ctx.sig[s][:, TT - 30:TT]),
                         reads=[ctx.sigb[s]], writes=[md.b["tin"][row // 128]], dma=True)
        ctx.ringA.release()


def outproj_tile(ctx, md, tt):
    P = ctx.P
    nc = P.nc
    t0 = tt * TT
    yv = md.y.rearrange("(c p) t -> p c t", p=128)
    for q in range(4):
        P.op(P.sp, lambda q=q: nc.sync.dma_start(out=ctx.hT[:, q * 8:(q + 1) * 8, :], in_=yv[:, q * 8:(q + 1) * 8, t0:t0 + TT]),
             reads=md.b["y"][q * 8:(q + 1) * 8], writes=ctx.hTb[q * 8:(q + 1) * 8], dma=True)
    bank_i = 0
    for u in range(8):
        wa, wab = ctx.ringA.acquire()
        for cc in range(4):
            bank = bank_i % 4
            bank_i += 1
            oc = u * 4 + cc
            for dc in range(DC):
                P.op(P.pe, lambda bank=bank, cc=cc, dc=dc: nc.tensor.matmul(
                    ctx.ps[bank][:], lhsT=wa[:, dc, cc * 128:(cc + 1) * 128], rhs=ctx.hT[:, dc, :],
                    start=(dc == 0), stop=(dc == DC - 1)),
                    reads=[wab, ctx.hTb[dc]], writes=[ctx.psb[bank]], signal=(dc == DC - 1))
            P.op(P.dve, lambda bank=bank, oc=oc: nc.vector.tensor_tensor(
                out=ctx.xacc[:, oc, :], in0=ctx.ps[bank][:], in1=ctx.xacc[:, oc, :], op=ALU.add),
                reads=[ctx.psb[bank], ctx.xaccb[oc]], writes=[ctx.xaccb[oc]])
        ctx.ringA.release()


def exchange(ctx, md, groups):
    P = ctx.P
    nc = P.nc
    P.op(P.pool, lambda: nc.gpsimd.collective_compute(
        "AllGather", ALU.bypass, replica_groups=groups, ins=[md.tin], outs=[md.tout]),
        reads=md.b["tin"], writes=md.b["tout"], cc=True)
    for k in range(4):
        P.op(P.pool, lambda k=k: nc.gpsimd.collective_compute(
            "AllGather", ALU.bypass, replica_groups=groups, ins=[md.xin[k * 1024:(k + 1) * 1024, :]], outs=[md.xout[k]]),
            reads=md.b["kown"] + md.b["vown"], writes=md.b["xout"], cc=True)


class ConvConsts:
    NCOL = 16 * 31 + 16 * 5

    def __init__(self, P, name):
        self.t = P.sbuf(name, [128, self.NCOL], F32)
        self.b = Buf(name)

    def dwk(self, c, j):
        o = c * 31 + j
        return self.t[:, o:o + 1]

    def vec(self, which, c):
        o = 16 * 31 + which * 16 + c
        return self.t[:, o:o + 1]


def host_conv_consts(dw_kernel, dw_bias, ln_g, ln_b, conv_out_g, attn_out_g):
    out = np.zeros((128, ConvConsts.NCOL), np.float32)
    out[:, :16 * 31] = dw_kernel.reshape(31, 16, 128).transpose(2, 1, 0).reshape(128, 16 * 31)
    for w, v in enumerate((dw_bias, ln_g, ln_b, conv_out_g, attn_out_g)):
        out[:, 16 * 31 + w * 16:16 * 31 + (w + 1) * 16] = v.reshape(16, 128).T
    return out


CONV_PE_CH = (2, 5, 8, 10, 12, 14)


def conv_phase(ctx, md, cc_, flag, flagb, ident_d, early=False):
    P = ctx.P
    nc = P.nc
    NCH = 16
    PE_CH = () if early else CONV_PE_CH
    ucur = [P.sbuf("cv_u%d" % i, [128, 32 + TOK], F32) for i in range(2)]
    ucb = [Buf("cv_u%d" % i) for i in range(2)]
    ucurp = P.sbuf("cv_up", [128, 32 + TOK], F32)
    ucpb = Buf("cv_up")
    dg = P.sbuf("cv_dg", [128, CK, 128], F32)
    dgb = Buf("cv_dg")
    idt = P.sbuf("cv_id", [128, 128], F32)
    idtb = Buf("cv_id")
    P.op(P.sp, lambda: nc.sync.dma_start(out=idt[:], in_=ident_d), writes=[idtb], dma=True)
    cy = P.sbuf("cv_y", [128, NCH, TOK], F32)
    cyb = [Buf("cv_y%d" % i) for i in range(NCH)]
    sq0 = P.sbuf("cv_sq", [128, TOK], F32)
    sq = [sq0, sq0]
    sqb0 = Buf("cv_sq")
    sqb = [sqb0, sqb0]
    mu = P.sbuf("cv_mu", [128, TOK], F32)
    mub = Buf("cv_mu")
    rs = P.sbuf("cv_rs", [128, TOK], F32)
    rsb = Buf("cv_rs")
    yb0 = P.sbuf("cv_yb", [128, TOK], BF16)
    yb16 = [yb0, yb0]
    yb0b = Buf("cv_yb")
    yb16b = [yb0b, yb0b]
    ps, psb = ctx.ps, ctx.psb
    order = []
    dl = [c for c in range(NCH) if c not in PE_CH]
    pl = [c for c in range(NCH) if c in PE_CH]
    while dl or pl:
        for _ in range(2):
            if dl:
                order.append(dl.pop(0))
        if pl:
            order.append(pl.pop(0))
    nd_i = 0
    np_i = 0
    NT = 32 if early else TOK
    for c in order:
        if early:
            P.op(P.sp, lambda: nc.sync.dma_start(out=cy[:, c, NT:TOK], in_=md.convy[c * 128:(c + 1) * 128, NT:TOK]),
                 reads=md.b["convy"][2 * c:2 * c + 2], writes=[cyb[c]], dma=True)
        if c in PE_CH:
            P.op(P.sp, lambda: nc.sync.dma_start(out=ucurp[:, 32:32 + NT], in_=md.u[c * 128:(c + 1) * 128, 0:NT]),
                 reads=[md.b["u"][c]], writes=[ucpb], dma=True)
            P.op(P.sp, lambda: nc.sync.dma_start(out=ucurp[:, 0:32], in_=md.tprev[c * 128:(c + 1) * 128, :]),
                 reads=md.b["tout"], writes=[ucpb], dma=True)
            P.op(P.act, lambda: nc.scalar.mul(out=ucurp[:, 2:32], in_=ucurp[:, 2:32], mul=flag[:, 0:1]),
                 reads=[ucpb, flagb], writes=[ucpb])
            for j in range(CK):
                P.op(P.act, lambda: nc.scalar.mul(out=dg[:, j, :], in_=idt[:], mul=cc_.dwk(c, j)),
                     reads=[idtb, cc_.b], writes=[dgb])
            for hh in range(NT // 512):
                bank = 4 + (np_i % 2) * 2 + hh
                for j in range(CK):
                    P.op(P.pe, lambda: nc.tensor.matmul(ps[bank][:], lhsT=dg[:, j, :], rhs=ucurp[:, 2 + j + hh * 512:2 + j + (hh + 1) * 512],
                                                        start=(j == 0), stop=(j == CK - 1)),
                         reads=[dgb, ucpb], writes=[psb[bank]], signal=(j == CK - 1))
                P.op(P.act, lambda: nc.scalar.activation(out=cy[:, c, hh * 512:(hh + 1) * 512], in_=ps[bank][:], func=AF.Identity,
                                                         bias=cc_.vec(0, c), scale=1.0),
                     reads=[psb[bank], cc_.b], writes=[cyb[c]])
            np_i += 1
        else:
            s = nd_i % 2
            nd_i += 1
            P.op(P.sp, lambda: nc.sync.dma_start(out=ucur[s][:, 32:32 + NT], in_=md.u[c * 128:(c + 1) * 128, 0:NT]),
                 reads=[md.b["u"][c]], writes=[ucb[s]], dma=True)
            P.op(P.sp, lambda: nc.sync.dma_start(out=ucur[s][:, 0:32], in_=md.tprev[c * 128:(c + 1) * 128, :]),
                 reads=md.b["tout"], writes=[ucb[s]], dma=True)
            P.op(P.dve, lambda: nc.vector.tensor_scalar(out=ucur[s][:, 2:32], in0=ucur[s][:, 2:32], scalar1=flag[:, 0:1], scalar2=None, op0=ALU.mult),
                 reads=[ucb[s], flagb], writes=[ucb[s]])
            P.op(P.dve, lambda: nc.vector.tensor_scalar(
                out=cy[:, c, 0:NT], in0=ucur[s][:, 2:2 + NT], scalar1=cc_.dwk(c, 0), scalar2=cc_.vec(0, c), op0=ALU.mult, op1=ALU.add),
                reads=[ucb[s], cc_.b], writes=[cyb[c]])
            for j in range(1, CK):
                P.op(P.dve, lambda: nc.vector.scalar_tensor_tensor(
                    out=cy[:, c, 0:NT], in0=ucur[s][:, 2 + j:2 + j + NT], scalar=cc_.dwk(c, j), in1=cy[:, c, 0:NT], op0=ALU.mult, op1=ALU.add),
                    reads=[ucb[s], cyb[c], cc_.b], writes=[cyb[c]])
    for oi, c in enumerate(order):
        s = oi % 2
        P.op(P.act, lambda: nc.scalar.activation(out=sq[s][:], in_=cy[:, c, :], func=AF.Square),
             reads=[cyb[c]], writes=[sqb[s]])
        for hh in range(2):
            P.op(P.pe, lambda: nc.tensor.matmul(ps[hh][:], lhsT=ctx.ones[:], rhs=cy[:, c, hh * 512:(hh + 1) * 512],
                                                start=(oi == 0), stop=(oi == NCH - 1)),
                 reads=[cyb[c], ctx.onesb], writes=[psb[hh]], signal=True)
            P.op(P.pe, lambda: nc.tensor.matmul(ps[2 + hh][:], lhsT=ctx.ones[:], rhs=sq[s][:, hh * 512:(hh + 1) * 512],
                                                start=(oi == 0), stop=(oi == NCH - 1)),
                 reads=[sqb[s], ctx.onesb], writes=[psb[2 + hh]], signal=True)
    for hh in range(2):
        sl = slice(hh * 512, (hh + 1) * 512)
        P.op(P.act, lambda hh=hh, sl=sl: nc.scalar.mul(out=mu[:, sl], in_=ps[hh][:], mul=1.0 / CW),
             reads=[psb[hh]], writes=[mub])
        P.op(P.dve, lambda hh=hh, sl=sl: nc.vector.tensor_tensor(out=rs[:, sl], in0=mu[:, sl], in1=mu[:, sl], op=ALU.mult),
             reads=[mub], writes=[rsb])
        P.op(P.dve, lambda hh=hh, sl=sl: nc.vector.scalar_tensor_tensor(
            out=rs[:, sl], in0=ps[2 + hh][:], scalar=1.0 / CW, in1=rs[:, sl], op0=ALU.mult, op1=ALU.subtract),
            reads=[psb[2 + hh], rsb], writes=[rsb])
    P.op(P.act, lambda: nc.scalar.activation(out=rs[:], in_=rs[:], func=AF.Sqrt, bias=LN_EPS, scale=1.0),
         reads=[rsb], writes=[rsb])
    P.op(P.dve, lambda: nc.vector.reciprocal(out=rs[:], in_=rs[:]), reads=[rsb], writes=[rsb])
    for c in range(NCH):
        s = c % 2
        P.op(P.dve, lambda c=c: nc.vector.tensor_tensor(out=cy[:, c, :], in0=cy[:, c, :], in1=mu[:], op=ALU.subtract),
             reads=[cyb[c], mub], writes=[cyb[c]])
        P.op(P.dve, lambda c=c: nc.vector.tensor_tensor(out=cy[:, c, :], in0=cy[:, c, :], in1=rs[:], op=ALU.mult),
             reads=[cyb[c], rsb], writes=[cyb[c]])
        P.op(P.act, lambda c=c: nc.scalar.activation(out=cy[:, c, :], in_=cy[:, c, :], func=AF.Silu,
                                                     scale=cc_.vec(1, c), bias=cc_.vec(2, c)),
             reads=[cyb[c], cc_.b], writes=[cyb[c]])
        P.op(P.act, lambda c=c, s=s: nc.scalar.activation(out=sq[s][:], in_=cy[:, c, :], func=AF.Square),
             reads=[cyb[c]], writes=[sqb[s]])
        for hh in range(2):
            P.op(P.pe, lambda c=c, hh=hh, s=s: nc.tensor.matmul(ps[4 + hh][:], lhsT=ctx.ones[:], rhs=sq[s][:, hh * 512:(hh + 1) * 512],
                                                               start=(c == 0), stop=(c == NCH - 1)),
                 reads=[sqb[s], ctx.onesb], writes=[psb[4 + hh]], signal=True)
    for hh in range(2):
        sl = slice(hh * 512, (hh + 1) * 512)
        P.op(P.act, lambda hh=hh, sl=sl: nc.scalar.activation(out=rs[:, sl], in_=ps[4 + hh][:], func=AF.Sqrt, scale=1.0 / CW, bias=RMS_EPS),
             reads=[psb[4 + hh]], writes=[rsb])
    P.op(P.dve, lambda: nc.vector.reciprocal(out=rs[:], in_=rs[:]), reads=[rsb], writes=[rsb])
    for c in range(NCH):
        s = c % 2
        P.op(P.dve, lambda c=c, s=s: nc.vector.scalar_tensor_tensor(
            out=yb16[s][:], in0=cy[:, c, :], scalar=cc_.vec(3, c), in1=rs[:], op0=ALU.mult, op1=ALU.mult),
            reads=[cyb[c], rsb, cc_.b], writes=[yb16b[s]])
        P.op(P.sp, lambda c=c, s=s: nc.sync.dma_start(out=md.y[AW + c * 128:AW + (c + 1) * 128, :], in_=yb16[s][:]),
             reads=[yb16b[s]], writes=[md.b["y"][16 + c]], dma=True)


def attn_phase(ctx, md, cc_, bias_d, st):
    P = ctx.P
    nc = P.nc
    ps, psb = ctx.ps, ctx.psb
    kT = [P.sbuf("at_k%d" % i, [128, 2 * TOK], BF16) for i in range(2)]
    qT = [P.sbuf("at_q%d" % i, [128, TOK], BF16) for i in range(2)]
    v1 = [P.sbuf("at_v1%d" % i, [128, 9, HD], BF16) for i in range(2)]
    v4 = [P.sbuf("at_v4%d" % i, [128, 4, 3, HD], BF16) for i in range(2)]
    v16 = [P.sbuf("at_v16%d" % i, [128, 16, HD], BF16) for i in range(2)]
    bia = [P.sbuf("at_b%d" % i, [128, BIASW], F32) for i in range(2)]
    hb = [Buf("at_h%d" % i) for i in range(2)]
    nd = P.sbuf("at_nd", [128, 2, TOK], F32)
    ndb = Buf("at_nd")
    at = [P.sbuf("at_at%d" % i, [128, TOK], F32) for i in range(2)]
    atb = [Buf("at_at%d" % i) for i in range(2)]
    sqh = P.sbuf("at_sq", [128, TOK], F32)
    sqhb = Buf("at_sq")
    sqacc = P.sbuf("at_sqacc", [128, TOK], F32)
    sqaccb = Buf("at_sqacc")
    rsa = P.sbuf("at_rs", [128, TOK], F32)
    rsab = Buf("at_rs")
    yb16 = [P.sbuf("at_yb%d" % i, [128, TOK], BF16) for i in range(2)]
    yb16b = [Buf("at_yb%d" % i) for i in range(2)]
    onesb16 = P.sbuf("at_ones", [128, 128], BF16)
    onesb16b = Buf("at_ones")
    P.op(P.dve, lambda: nc.vector.memset(onesb16[:], 1.0), writes=[onesb16b])

    vown = md.vown

    hbl = [[Buf("at_h%d_%d" % (i, j)) for j in range(16)] for i in range(2)]

    def load_head(h):
        s = h % 2
        hc = slice(h * HD, (h + 1) * HD)
        cnt = [0]

        def dm(out, in_, rd):
            w = [hbl[s][cnt[0]]]
            cnt[0] += 1
            P.op(P.sp, lambda: nc.sync.dma_start(out=out, in_=in_), reads=rd, writes=w, dma=True)
        XO = md.b["xout"]
        VO = [md.b["vown"][tb * 4 + h // 4] for tb in range(8)]
        dm(kT[s][:, 0:TOK], md.kprev(h), XO)
        dm(kT[s][:, TOK:2 * TOK], md.kown[hc, :], [md.b["kown"][h]])
        dm(qT[s][:], md.qT[hc, :], [md.b["qT"][h]])
        dm(bia[s][:], bias_d[h], [])
        dm(v1[s][:, 0:1, :], md.vprev_h[1].rearrange("(b j) c -> j b c", j=128)[:, 3:4, hc], XO)
        dm(v1[s][:, 1:9, :], vown.rearrange("(b j) c -> j b c", j=128)[:, :, hc], VO)
        vp4 = md.vprev_h[1].rearrange("(j r) c -> j r c", r=4)
        vo4 = vown.rearrange("(b j r) c -> j r b c", j=128, r=4)
        dm(v4[s][:, :, 0, :], vp4[:, :, hc], XO)
        for r in range(4):
            dm(v4[s][:, r, 1:3, :], vo4[:, r, :, hc], VO)
        for i in range(2):
            dm(v16[s][32 * i:32 * (i + 1), :, :], md.vprev_h[i].rearrange("(j r) c -> j r c", r=16)[:, :, hc], XO)
        dm(v16[s][64:128, :, :], vown.rearrange("(j r) c -> j r c", r=16)[:, :, hc], VO)
        assert cnt[0] <= 16

    last_of = {}
    NR = 4
    SB = [P.sbuf("at_sbr%d" % i, [128, 512], F32) for i in range(NR)]
    SBb = [Buf("at_sbr%d" % i) for i in range(NR)]
    PT = [P.sbuf("at_pr%d" % i, [128, 512], BF16) for i in range(NR)]
    PTb = [Buf("at_pr%d" % i) for i in range(NR)]
    nd2 = [nd, P.sbuf("at_nd2", [128, 2, TOK], F32)]
    nd2b = [ndb, Buf("at_nd2")]
    SBANK = [0, 1, 2, 3]
    OBANK = [4, 5, 6, 7]

    def front(k, u):
        (s, subs, bias_ap, n, dst_fn, first, h) = u
        i = k % NR
        pss, pssb = ps[SBANK[i]], psb[SBANK[i]]
        nk = len(subs[0][0])
        nmm = len(subs) * nk
        c = 0
        for (kaps, vaps, qap) in subs:
            for kap in kaps:
                P.op(P.pe, lambda: nc.tensor.matmul(pss[:, c * n:(c + 1) * n], lhsT=kap, rhs=qap, start=True, stop=True),
                     reads=hbl[s], writes=[pssb], signal=(c == nmm - 1))
                c += 1
        w = nmm * n
        P.op(P.dve, lambda: nc.vector.scalar_tensor_tensor(out=SB[i][:, 0:w], in0=pss[:, 0:w], scalar=QSCALE, in1=bias_ap,
                                                           op0=ALU.mult, op1=ALU.add),
             reads=[pssb] + hbl[s], writes=[SBb[i]])
        P.op(P.act, lambda: nc.scalar.activation(out=PT[i][:, 0:w], in_=SB[i][:, 0:w], func=AF.Exp),
             reads=[SBb[i]], writes=[PTb[i]])

    def back(k, u):
        (s, subs, bias_ap, n, dst_fn, first, h) = u
        i = k % NR
        pso, psob = ps[OBANK[i]], psb[OBANK[i]]
        nk = len(subs[0][0])
        ns = len(subs)
        for si, (kaps, vaps, qap) in enumerate(subs):
            for a, vap in enumerate(vaps):
                P.op(P.pe, lambda: nc.tensor.matmul(pso[:, si * 2 * n:si * 2 * n + n], lhsT=vap,
                                                    rhs=PT[i][:, (si * nk + a) * n:(si * nk + a + 1) * n],
                                                    start=(a == 0), stop=(a == nk - 1)),
                     reads=hbl[s] + [PTb[i]], writes=[psob], signal=False)
            for a in range(nk):
                P.op(P.pe, lambda: nc.tensor.matmul(pso[:, si * 2 * n + n:(si + 1) * 2 * n], lhsT=onesb16[:],
                                                    rhs=PT[i][:, (si * nk + a) * n:(si * nk + a + 1) * n],
                                                    start=(a == 0), stop=(a == nk - 1)),
                     reads=[onesb16b, PTb[i]], writes=[psob], signal=(a == nk - 1 and si == ns - 1))
        src = pso[:, 0:ns * 2 * n].rearrange("p (s two q) -> p two s q", s=ns, two=2)
        ndt, ndtb = nd2[h % 2], nd2b[h % 2]
        dst = dst_fn(ndt)
        if first:
            P.op(P.act, lambda: nc.scalar.copy(out=dst, in_=src), reads=[psob], writes=[ndtb])
        else:
            P.op(P.dve, lambda: nc.vector.tensor_tensor(out=dst, in0=src, in1=dst, op=ALU.add), reads=[psob, ndtb], writes=[ndtb])

    def finalize(h):
        a = h % 2
        ndt, ndtb = nd2[h % 2], nd2b[h % 2]
        P.op(P.dve, lambda: nc.vector.reciprocal(out=ndt[:, 1, :], in_=ndt[:, 1, :]), reads=[ndtb], writes=[ndtb])
        P.op(P.dve, lambda: nc.vector.tensor_tensor(out=at[a][:], in0=ndt[:, 0, :], in1=ndt[:, 1, :], op=ALU.mult),
             reads=[ndtb], writes=[atb[a]])
        if h == 0:
            P.op(P.act, lambda: nc.scalar.activation(out=sqacc[:], in_=at[a][:], func=AF.Square), reads=[atb[a]], writes=[sqaccb])
        else:
            P.op(P.act, lambda: nc.scalar.activation(out=sqh[:], in_=at[a][:], func=AF.Square), reads=[atb[a]], writes=[sqhb])
            P.op(P.pool, lambda: nc.gpsimd.tensor_tensor(out=sqacc[:], in0=sqacc[:], in1=sqh[:], op=ALU.add),
                 reads=[sqhb, sqaccb], writes=[sqaccb])
        P.op(P.sp, lambda: nc.sync.dma_start(out=md.attn[h * HD:(h + 1) * HD, :], in_=at[a][:]),
             reads=[atb[a]], writes=[md.b["attn"][h]], dma=True)

    def head_units(h):
        s = h % 2
        us = []
        for p in range(4):
            subs = []
            for qb in (2 * p, 2 * p + 1):
                subs.append(([kT[s][:, (7 + qb) * 128:(8 + qb) * 128], kT[s][:, (8 + qb) * 128:(9 + qb) * 128]],
                             [v1[s][:, qb, :], v1[s][:, qb + 1, :]], qT[s][:, qb * 128:(qb + 1) * 128]))
            boff = 0 if p == 0 else 256
            dst_fn = lambda t, p=p: t[:, :, p * 256:(p + 1) * 256].rearrange("p two (s q) -> p two s q", s=2)
            us.append((s, subs, bia[s][:, boff:boff + 512], 128, dst_fn, True, h))
        for r in range(4):
            subs = []
            for b in (2, 3):
                kp = kT[s][:, 512 * (b - 1) + r:512 * b:4]
                kc = kT[s][:, 512 * b + r:512 * (b + 1):4]
                subs.append(([kp, kc], [v4[s][:, r, b - 2, :], v4[s][:, r, b - 1, :]], qT[s][:, 512 * (b - 2) + r:512 * (b - 1):4]))
            dst_fn = lambda t, r=r: t[:, :, r:TOK:4].rearrange("p two (s q) -> p two s q", s=2)
            us.append((s, subs, bia[s][:, 768:768 + 512], 128, dst_fn, False, h))
        for rp in range(8):
            subs = []
            for r in (2 * rp, 2 * rp + 1):
                subs.append(([kT[s][:, r:2 * TOK:16]], [v16[s][:, r, :]], qT[s][:, r:TOK:16]))
            dst_fn = lambda t, rp=rp: t[:, :, :].rearrange("p two (q r) -> p two r q", r=16)[:, :, 2 * rp:2 * rp + 2, :]
            us.append((s, subs, bia[s][:, 1280:1280 + 128], 64, dst_fn, False, h))
        return us

    LAG = 3
    load_head(0)
    pend = []
    k = 0
    for h in range(NH):
        for ui, u in enumerate(head_units(h)):
            front(k, u)
            pend.append((k, u))
            k += 1
            if len(pend) > LAG:
                kk, uu = pend.pop(0)
                back(kk, uu)
                if uu is last_of.get(uu[6]):
                    finalize(uu[6])
            if ui == LAG + 1 and h + 1 < NH:
                load_head(h + 1)
        last_of[h] = u
    while pend:
        kk, uu = pend.pop(0)
        back(kk, uu)
        if uu is last_of.get(uu[6]):
            finalize(uu[6])
    for hh in range(2):
        P.op(P.pe, lambda: nc.tensor.matmul(ps[hh][:], lhsT=ctx.ones[:], rhs=sqacc[:, hh * 512:(hh + 1) * 512], start=True, stop=True),
             reads=[sqaccb, ctx.onesb], writes=[psb[hh]], signal=True)
    for hh in range(2):
        sl = slice(hh * 512, (hh + 1) * 512)
        P.op(P.act, lambda hh=hh, sl=sl: nc.scalar.activation(out=rsa[:, sl], in_=ps[hh][:], func=AF.Sqrt, scale=1.0 / AW, bias=RMS_EPS),
             reads=[psb[hh]], writes=[rsab])
    P.op(P.dve, lambda: nc.vector.reciprocal(out=rsa[:], in_=rsa[:]), reads=[rsab], writes=[rsab])
    for h in range(NH):
        a = h % 2
        P.op(P.sp, lambda a=a, h=h: nc.sync.dma_start(out=at[a][:], in_=md.attn[h * HD:(h + 1) * HD, :]),
             reads=[md.b["attn"][h]], writes=[atb[a]], dma=True)
        P.op(P.dve, lambda a=a, h=h: nc.vector.scalar_tensor_tensor(
            out=yb16[a][:], in0=at[a][:], scalar=cc_.vec(4, h), in1=rsa[:], op0=ALU.mult, op1=ALU.mult),
            reads=[atb[a], rsab, cc_.b], writes=[yb16b[a]])
        P.op(P.sp, lambda a=a, h=h: nc.sync.dma_start(out=md.y[h * HD:(h + 1) * HD, :], in_=yb16[a][:]),
             reads=[yb16b[a]], writes=[md.b["y"][h]], dma=True)

DEPTH = 2
D_FF = 11008
N_CORES = 8


def final_norm_tile(ctx, gcol, gcolb):
    P = ctx.P
    nc = P.nc
    pst, pstb = ctx.ps[4], ctx.psb[4]
    for dc in range(DC):
        s = dc % 2
        P.op(P.act, lambda: nc.scalar.activation(out=ctx.sq[s][:], in_=ctx.xacc[:, dc, :], func=AF.Square),
             reads=[ctx.xaccb[dc]], writes=[ctx.sqb[s]])
        P.op(P.pe, lambda: nc.tensor.matmul(pst[:], lhsT=ctx.ones[:], rhs=ctx.sq[s][:], start=(dc == 0), stop=(dc == DC - 1)),
             reads=[ctx.sqb[s], ctx.onesb], writes=[pstb], signal=True)
    P.op(P.act, lambda: nc.scalar.activation(out=ctx.rstd[:], in_=pst[:], func=AF.Sqrt, scale=1.0 / D, bias=RMS_EPS),
         reads=[pstb], writes=[ctx.rstdb])
    P.op(P.dve, lambda: nc.vector.reciprocal(out=ctx.rstd[:], in_=ctx.rstd[:]), reads=[ctx.rstdb], writes=[ctx.rstdb])
    for dc in range(DC):
        P.op(P.dve, lambda: nc.vector.scalar_tensor_tensor(
            out=ctx.xacc[:, dc, :], in0=ctx.xacc[:, dc, :], scalar=gcol[:, dc:dc + 1], in1=ctx.rstd[:],
            op0=ALU.mult, op1=ALU.mult),
            reads=[ctx.xaccb[dc], ctx.rstdb, gcolb], writes=[ctx.xaccb[dc]])


def build_program(F, n_cores):
    nc = bass.Bass("TRN2", target_bir_lowering=False)
    L = DEPTH
    ein = lambda name, shape: nc.dram_tensor(name, list(shape), F32, kind="ExternalInput").ap()
    xT = ein("xT", [D, TOK])
    w = {}
    for nm in ("ffn1_w1", "ffn1_w3", "ffn2_w1", "ffn2_w3"):
        w[nm] = ein(nm, [L, D, F])
    for nm in ("ffn1_w2", "ffn2_w2"):
        w[nm] = ein(nm, [L, F, D])
    w["w_in"] = ein("w_in", [L, D, 3 * AW + 2 * CW])
    w["w_out"] = ein("w_out", [L, D, D])
    gcd = ein("gcols", [128, 7 * DC])
    ccd = ein("convc", [L, 128, ConvConsts.NCOL])
    flagd = ein("flag", [128, 1])
    biasd = ein("biasd", [NH, 128, BIASW])
    identd = ein("ident", [128, 128])
    out = nc.dram_tensor("outT", [D, TOK], F32, kind="ExternalOutput").ap()
    xres = nc.dram_tensor("xres", [D, TOK], F32).ap()
    md = MixDram(nc)
    groups = [[2 * i, 2 * i + 1] for i in range(n_cores // 2)]
    with ExitStack() as st:
        P = Prog(nc, st)
        ctx = Ctx(P)
        gcol = P.sbuf("gcol", [128, 7 * DC], F32)
        gcolb = Buf("gcol")
        P.op(P.sp, lambda: nc.sync.dma_start(out=gcol[:], in_=gcd), writes=[gcolb], dma=True)
        flag = P.sbuf("flag", [128, 1], F32)
        flagb = Buf("flag")
        P.op(P.sp, lambda: nc.sync.dma_start(out=flag[:], in_=flagd), writes=[flagb], dma=True)
        g_of = lambda l, k: gcol[:, (l * 3 + k) * DC:(l * 3 + k + 1) * DC]
        g_fin = gcol[:, 6 * DC:7 * DC]
        for tt in range(2):
            plan_ffn(ctx, w["ffn1_w1"][0], w["ffn1_w3"][0], w["ffn1_w2"][0], F, ntiles=1)
            plan_inproj(ctx, w["w_in"][0])
        for tt in range(2):
            plan_outproj(ctx, w["w_out"][0])
            plan_ffn(ctx, w["ffn2_w1"][0], w["ffn2_w3"][0], w["ffn2_w2"][0], F, ntiles=1)
            plan_ffn(ctx, w["ffn1_w1"][1], w["ffn1_w3"][1], w["ffn1_w2"][1], F, ntiles=1)
            plan_inproj(ctx, w["w_in"][1])
        for tt in range(2):
            plan_outproj(ctx, w["w_out"][1])
            plan_ffn(ctx, w["ffn2_w1"][1], w["ffn2_w3"][1], w["ffn2_w2"][1], F, ntiles=1)
        ctx.ringA.start()
        ctx.ringB.start()
        xin_v = xT.rearrange("(c p) t -> p c t", p=128)
        xres_v = xres.rearrange("(c p) t -> p c t", p=128)
        out_v = out.rearrange("(c p) t -> p c t", p=128)
        xinb = [Buf("xin0"), Buf("xin1")]
        xresb = [Buf("xres0"), Buf("xres1")]
        outb = [Buf("out0"), Buf("out1")]

        def mixer_core(l):
            exchange(ctx, md, groups)
            P.push()
            cc_ = ConvConsts(P, "ccA%d" % l)
            P.op(P.sp, lambda: nc.sync.dma_start(out=cc_.t[:], in_=ccd[l]), writes=[cc_.b], dma=True)
            conv_phase(ctx, md, cc_, flag, flagb, identd, True)
            P.pop()
            P.push()
            cc2 = ConvConsts(P, "ccB%d" % l)
            P.op(P.sp, lambda: nc.sync.dma_start(out=cc2.t[:], in_=ccd[l]), writes=[cc2.b], dma=True)
            attn_phase(ctx, md, cc2, biasd, None)
            P.pop()

        def load_cce(l):
            c = ConvConsts(P, "ccE%d" % l)
            P.op(P.sp, lambda: nc.sync.dma_start(out=c.t[:], in_=ccd[l]), writes=[c.b], dma=True)
            return c

        P.push()
        ctx.alloc_dense()
        cce = load_cce(0)
        load_x(ctx, xin_v, xinb, 0)
        for tt in range(2):
            ffn_tile(ctx, F, g_of(0, 0), gcolb)

            def _swap(tt=tt):
                store_x(ctx, xres_v, xresb, tt)
                if tt == 0:
                    load_x(ctx, xin_v, xinb, 1, (1, 2, 3))
            inproj_tile(ctx, md, tt, g_of(0, 1), gcolb, _swap, cce)
            if tt == 0:
                load_x(ctx, xin_v, xinb, 1, (0,))
        P.pop()
        mixer_core(0)
        P.push()
        ctx.alloc_dense()
        cce = load_cce(1)
        load_x(ctx, xres_v, xresb, 0)
        for tt in range(2):
            outproj_tile(ctx, md, tt)
            ffn_tile(ctx, F, g_of(0, 2), gcolb)
            ffn_tile(ctx, F, g_of(1, 0), gcolb)

            def _swap(tt=tt):
                store_x(ctx, xres_v, xresb, tt)
                if tt == 0:
                    load_x(ctx, xres_v, xresb, 1, (1, 2, 3))
            inproj_tile(ctx, md, tt, g_of(1, 1), gcolb, _swap, cce)
            if tt == 0:
                load_x(ctx, xres_v, xresb, 1, (0,))
        P.pop()
        mixer_core(1)
        P.push()
        ctx.alloc_dense()
        for tt in range(2):
            load_x(ctx, xres_v, xresb, tt)
            outproj_tile(ctx, md, tt)
            ffn_tile(ctx, F, g_of(1, 2), gcolb)
            final_norm_tile(ctx, g_fin, gcolb)
            store_x(ctx, out_v, outb, tt)
        P.finish(outb)
        P.pop()
        assert ctx.ringA.consumed == len(ctx.ringA.units) and ctx.ringB.consumed == len(ctx.ringB.units)
        build_program.stats = (P.ninstr, P.nwaits, P.nsem)
    return nc


def make_in_maps(inputs, n_cores):
    f = lambda k: np.ascontiguousarray(np.asarray(inputs[k], dtype=np.float32))
    x = f("x")
    col = lambda v: np.ascontiguousarray(np.asarray(v, np.float32).reshape(DC, 128).T)
    gl = []
    for l in range(DEPTH):
        gl += [col(inputs["ffn1_norm_g"][l]), col(inputs["mix_norm_g"][l]), col(inputs["ffn2_norm_g"][l])]
    gl.append(col(inputs["final_norm_g"]))
    gcols = np.ascontiguousarray(np.concatenate(gl, axis=1))
    convc = np.stack([host_conv_consts(np.asarray(inputs["dw_kernel"][l]), np.asarray(inputs["dw_bias"][l]),
                                       np.asarray(inputs["conv_ln_g"][l]), np.asarray(inputs["conv_ln_b"][l]),
                                       np.asarray(inputs["conv_out_g"][l]), np.asarray(inputs["attn_out_g"][l]))
                      for l in range(DEPTH)])
    rel = np.asarray(inputs["rel_bias_table"], np.float32)
    bias_h = [build_bias(rel, 0), build_bias(rel, 1)]
    shared = {k: f(k) for k in ("ffn1_w1", "ffn1_w3", "ffn1_w2", "ffn2_w1", "ffn2_w3", "ffn2_w2", "w_in", "w_out")}
    maps = []
    for c in range(n_cores):
        b, half = c // 2, c % 2
        m = dict(shared)
        m["xT"] = np.ascontiguousarray(x[b, half * TOK:(half + 1) * TOK, :].T)
        m["gcols"] = gcols
        m["convc"] = convc
        m["flag"] = np.full((128, 1), float(half), np.float32)
        m["biasd"] = bias_h[half]
        m["ident"] = np.eye(128, dtype=np.float32)
        maps.append(m)
    return maps


def kernel(**inputs):
    x = np.asarray(inputs["x"])
    B, S, _ = x.shape
    n_cores = B * 2
    F = np.asarray(inputs["ffn1_w1"]).shape[-1]
    nc = build_program(F, n_cores)
    maps = make_in_maps(inputs, n_cores)
    res = run_bass_kernel_spmd(nc, maps, core_ids=list(range(n_cores)))
    out = np.empty((B, S, D), np.float32)
    for c in range(n_cores):
        b, half = c // 2, c % 2
        out[b, half * TOK:(half + 1) * TOK, :] = res.results[c]["outT"].T
    return out
```
